# Optimizing a Trainium2 kernel written in Bass

```python
import math
import jax
import jax.numpy as jnp
from jax import lax
import numpy as np


D_MODEL = 2048
BATCH = 4
SEQ = 2048
DEPTH = 4

N_MIXERS = 2
N_SSD_LAYERS = (DEPTH + 1) // 2
N_POOL_LAYERS = DEPTH // 2
EPS = 1e-6

D_FF = 5632

SSD_EXPAND = 2
D_INNER = SSD_EXPAND * D_MODEL
SSD_HEAD_DIM = 64
SSD_HEADS = D_INNER // SSD_HEAD_DIM
SSD_GROUPS = 8
SSD_HEADS_PER_GROUP = SSD_HEADS // SSD_GROUPS
SSD_STATE = 128
SSD_CONV_W = 5
SSD_CHUNK = 128
SSD_CONV_DIM = D_INNER + 2 * SSD_GROUPS * SSD_STATE
SSD_IN_DIM = D_INNER + SSD_CONV_DIM + 2 * SSD_HEADS

POOL_WINDOWS = (2, 4, 8, 16)
N_POOL_GROUPS = 4
D_POOL = D_MODEL
POOL_GROUP_DIM = D_POOL // N_POOL_GROUPS

kernel_name = "hybrid_ssd_pool_macaron_encoder"


def rmsnorm(x, g):
    xf = x.astype(jnp.float32)
    y = xf * lax.rsqrt(jnp.mean(xf * xf, axis=-1, keepdims=True) + EPS)
    return (y * g.astype(jnp.float32)).astype(x.dtype)


def swiglu(h, w_gate, w_up, w_down):
    return (jax.nn.silu(h @ w_gate) * (h @ w_up)) @ w_down


def centred_dwconv(u, w, bias):
    pad = SSD_CONV_W // 2
    out = lax.conv_general_dilated(
        u, w[:, None, :].astype(u.dtype), window_strides=(1,), padding=[(pad, pad)],
        dimension_numbers=("NWC", "WIO", "NWC"), feature_group_count=u.shape[-1])
    return out + bias.astype(u.dtype)


def ssd_chunked(xh, dt, a, bm, cm):
    b, l, h, p = xh.shape
    q = SSD_CHUNK
    nc = l // q
    g, hg, n = SSD_GROUPS, SSD_HEADS_PER_GROUP, SSD_STATE
    xdt = (xh * dt[..., None].astype(xh.dtype)).reshape(b, nc, q, g, hg, p)
    bm = bm.reshape(b, nc, q, g, n)
    cm = cm.reshape(b, nc, q, g, n)
    a_cs = jnp.cumsum((dt * a).reshape(b, nc, q, g, hg), axis=2)
    tri = jnp.tril(jnp.ones((q, q), dtype=bool))[:, :, None, None]
    seg = a_cs[:, :, :, None] - a_cs[:, :, None, :]
    lmat = jnp.exp(jnp.where(tri, seg, -jnp.inf))
    cb = jnp.einsum("bcign,bcjgn->bcgij", cm, bm)
    y_diag = jnp.einsum("bcgij,bcijgh,bcjghp->bcighp", cb, lmat, xdt)
    decay_states = jnp.exp(a_cs[:, :, -1:] - a_cs)
    states = jnp.einsum("bcjgn,bcjgh,bcjghp->bcghpn", bm, decay_states, xdt)
    chunk_decay = jnp.exp(a_cs[:, :, -1])

    def step(hstate, inp):
        st, dec = inp
        return hstate * dec[..., None, None] + st, hstate

    h0 = jnp.zeros((b, g, hg, p, n), states.dtype)
    _, h_in = lax.scan(step, h0, (jnp.moveaxis(states, 1, 0), jnp.moveaxis(chunk_decay, 1, 0)))
    h_in = jnp.moveaxis(h_in, 0, 1)
    y_off = jnp.einsum("bcign,bcigh,bcghpn->bcighp", cm, jnp.exp(a_cs), h_in)
    return (y_diag + y_off).reshape(b, l, h, p).astype(xh.dtype)


def ssd_mixer(h, w_in, conv_w, conv_b, dt_bias, a_log, d_skip, norm_g, w_out):
    b, l, _ = h.shape
    proj = h @ w_in
    z, xbc, dt_raw = jnp.split(proj, [D_INNER, D_INNER + SSD_CONV_DIM], axis=-1)
    xbc = jax.nn.silu(centred_dwconv(xbc, conv_w, conv_b))
    xs, bm, cm = jnp.split(xbc, [D_INNER, D_INNER + SSD_GROUPS * SSD_STATE], axis=-1)
    xh = xs.reshape(b, l, SSD_HEADS, SSD_HEAD_DIM)
    bm = bm.reshape(b, l, SSD_GROUPS, SSD_STATE)
    cm = cm.reshape(b, l, SSD_GROUPS, SSD_STATE)
    dt = jax.nn.softplus(dt_raw.astype(jnp.float32).reshape(b, l, 2, SSD_HEADS)
                         + dt_bias.astype(jnp.float32))
    a = -jnp.exp(a_log.astype(jnp.float32))
    y_fwd = ssd_chunked(xh, dt[:, :, 0], a[0], bm, cm)
    y_bwd = jnp.flip(ssd_chunked(jnp.flip(xh, 1), jnp.flip(dt[:, :, 1], 1), a[1],
                                 jnp.flip(bm, 1), jnp.flip(cm, 1)), axis=1)
    y = y_fwd + y_bwd + xh * d_skip[:, None].astype(xh.dtype)
    y = y.reshape(b, l, D_INNER)
    y = rmsnorm(y * jax.nn.silu(z), norm_g)
    return y @ w_out


def pool_mixer(h, w_in, w_group, scale, w_out):
    b, l, _ = h.shape
    u = (h @ w_in).reshape(b, l, N_POOL_GROUPS, POOL_GROUP_DIM)
    uf = u.astype(jnp.float32)
    cs = jnp.pad(jnp.cumsum(uf, axis=1), ((0, 0), (1, 0), (0, 0), (0, 0)))
    t = jnp.arange(l)
    pooled = []
    for gi, w in enumerate(POOL_WINDOWS):
        lo = jnp.clip(t - w // 2, 0, l)
        hi = jnp.clip(t + w // 2, 0, l)
        cs_g = cs[:, :, gi]
        cnt = (hi - lo).astype(jnp.float32)[None, :, None]
        pooled.append((cs_g[:, hi] - cs_g[:, lo]) / cnt)
    pooled = jnp.stack(pooled, axis=2)
    mix = (pooled - uf).astype(h.dtype)
    v = jnp.einsum("blgc,gcd->blgd", mix, w_group).reshape(b, l, D_POOL)
    return (v * scale) @ w_out


def setup_inputs(seed: int = 0) -> dict:
    key = jax.random.key(seed)
    ks = jax.random.split(key, 24)
    f32 = jnp.float32

    def nrm(k, shape, fan_in):
        return jax.random.normal(k, shape, f32) * (fan_in ** -0.5)

    def gain(k, shape):
        return 1.0 + 0.02 * jax.random.normal(k, shape, f32)

    x = jax.random.normal(ks[0], (BATCH, SEQ, D_MODEL), f32)
    ffn_norm = gain(ks[1], (DEPTH, 2, D_MODEL))
    ffn_w_gate = nrm(ks[2], (DEPTH, 2, D_MODEL, D_FF), D_MODEL)
    ffn_w_up = nrm(ks[3], (DEPTH, 2, D_MODEL, D_FF), D_MODEL)
    ffn_w_down = nrm(ks[4], (DEPTH, 2, D_FF, D_MODEL), D_FF)
    mix_norm = gain(ks[5], (DEPTH, D_MODEL))

    ssd_w_in = nrm(ks[6], (N_SSD_LAYERS, D_MODEL, SSD_IN_DIM), D_MODEL)
    ssd_conv_w = nrm(ks[7], (N_SSD_LAYERS, SSD_CONV_W, SSD_CONV_DIM), SSD_CONV_W)
    ssd_conv_b = 0.02 * jax.random.normal(ks[8], (N_SSD_LAYERS, SSD_CONV_DIM), f32)
    dt0 = jnp.exp(jax.random.uniform(ks[9], (N_SSD_LAYERS, 2, SSD_HEADS), f32,
                                     minval=math.log(1e-3), maxval=math.log(1e-1)))
    ssd_dt_bias = dt0 + jnp.log(-jnp.expm1(-dt0))
    ssd_a_log = jnp.log(jax.random.uniform(ks[10], (N_SSD_LAYERS, 2, SSD_HEADS), f32,
                                           minval=1.0, maxval=16.0))
    ssd_d = 1.0 + 0.1 * jax.random.normal(ks[11], (N_SSD_LAYERS, SSD_HEADS), f32)
    ssd_norm = gain(ks[12], (N_SSD_LAYERS, D_INNER))
    ssd_w_out = nrm(ks[13], (N_SSD_LAYERS, D_INNER, D_MODEL), D_INNER)

    pool_w_in = nrm(ks[14], (N_POOL_LAYERS, D_MODEL, D_POOL), D_MODEL)
    pool_w_group = nrm(ks[15], (N_POOL_LAYERS, N_POOL_GROUPS, POOL_GROUP_DIM, POOL_GROUP_DIM),
                       POOL_GROUP_DIM)
    pool_scale = 1.0 + 0.1 * jax.random.normal(ks[16], (N_POOL_LAYERS, D_POOL), f32)
    pool_w_out = nrm(ks[17], (N_POOL_LAYERS, D_POOL, D_MODEL), D_POOL)

    final_norm = gain(ks[18], (D_MODEL,))
    return {
        "x": x, "ffn_norm": ffn_norm, "ffn_w_gate": ffn_w_gate, "ffn_w_up": ffn_w_up,
        "ffn_w_down": ffn_w_down, "mix_norm": mix_norm,
        "ssd_w_in": ssd_w_in, "ssd_conv_w": ssd_conv_w, "ssd_conv_b": ssd_conv_b,
        "ssd_dt_bias": ssd_dt_bias, "ssd_a_log": ssd_a_log, "ssd_d": ssd_d,
        "ssd_norm": ssd_norm, "ssd_w_out": ssd_w_out,
        "pool_w_in": pool_w_in, "pool_w_group": pool_w_group, "pool_scale": pool_scale,
        "pool_w_out": pool_w_out, "final_norm": final_norm,
    }


def reference(x, ffn_norm, ffn_w_gate, ffn_w_up, ffn_w_down, mix_norm,
              ssd_w_in, ssd_conv_w, ssd_conv_b, ssd_dt_bias, ssd_a_log, ssd_d,
              ssd_norm, ssd_w_out, pool_w_in, pool_w_group, pool_scale, pool_w_out,
              final_norm):
    for i in range(DEPTH):
        x = x + 0.5 * swiglu(rmsnorm(x, ffn_norm[i, 0]), ffn_w_gate[i, 0], ffn_w_up[i, 0],
                             ffn_w_down[i, 0])
        h = rmsnorm(x, mix_norm[i])
        j = i // N_MIXERS
        if i % N_MIXERS == 0:
            x = x + ssd_mixer(h, ssd_w_in[j], ssd_conv_w[j], ssd_conv_b[j], ssd_dt_bias[j],
                              ssd_a_log[j], ssd_d[j], ssd_norm[j], ssd_w_out[j])
        else:
            x = x + pool_mixer(h, pool_w_in[j], pool_w_group[j], pool_scale[j], pool_w_out[j])
        x = x + 0.5 * swiglu(rmsnorm(x, ffn_norm[i, 1]), ffn_w_gate[i, 1], ffn_w_up[i, 1],
                             ffn_w_down[i, 1])
    return rmsnorm(x, final_norm)
```

```python
import contextlib
import numpy as np
import concourse.bass as bass
import concourse.mybir as mybir
from concourse.bass_utils import run_bass_kernel_spmd

F32 = mybir.dt.float32
BF16 = mybir.dt.bfloat16
ALU = mybir.AluOpType
AF = mybir.ActivationFunctionType

D = 2048
KC = D // 128
DFF = 5632
FC = DFF // 128
NQ = 4
FQ = FC // NQ
EPS = 1e-6
NDMASEM = 24


class Prog:
    ENG = ("pe", "act", "dve", "pool", "sp")

    def __init__(self):
        self.nc = bass.Bass("TRN2", target_bir_lowering=False)
        self.es = contextlib.ExitStack()
        self.ops = {e: [] for e in self.ENG}
        self.cnt = {e: 0 for e in self.ENG}
        self.sem = {e: self.es.enter_context(self.nc.semaphore("s_" + e)) for e in self.ENG}
        self.dsem = [self.es.enter_context(self.nc.semaphore("d%d" % i)) for i in range(NDMASEM)]
        self.ndma = 0
        self.dma_last = [None] * NDMASEM
        self.waited = {e: {} for e in self.ENG}
        self.last_w = {}
        self.readers = {}
        self.all_tokens = []
        self.uid = 0

    def sbuf(self, name, shape, dtype, stack=None):
        self.uid += 1
        t = (stack or self.es).enter_context(self.nc.sbuf_tensor("%s_%d" % (name, self.uid), list(shape), dtype))
        return t

    def psum(self, name, shape, dtype, stack=None):
        self.uid += 1
        return (stack or self.es).enter_context(self.nc.psum_tensor("%s_%d" % (name, self.uid), list(shape), dtype))

    def dram(self, name, shape, dtype, kind="Internal"):
        return self.nc.dram_tensor(name, list(shape), dtype, kind=kind).ap()

    def _deps(self, reads, writes):
        deps = []
        for k in reads:
            if k in self.last_w:
                deps.append(self.last_w[k])
        for k in writes:
            if k in self.last_w:
                deps.append(self.last_w[k])
            deps.extend(self.readers.get(k, ()))
        return deps

    def _commit(self, tok, reads, writes):
        for k in reads:
            self.readers.setdefault(k, []).append(tok)
        for k in writes:
            self.last_w[k] = tok
            self.readers[k] = []

    def _waits(self, eng, deps, pe_skip=False):
        w = self.waited[eng]
        best = {}
        for (s, v, src) in deps:
            if pe_skip and src == "pe":
                continue
            if w.get(id(s), 0) >= v:
                continue
            if best.get(id(s), (None, 0))[1] < v:
                best[id(s)] = (s, v)
        out = []
        for sid, (s, v) in best.items():
            w[sid] = v
            out.append((s, v))
        return out

    def op(self, eng, meth, reads=(), writes=(), **kw):
        fn = (lambda e, meth=meth, kw=kw: getattr(e, meth)(**kw))
        deps = self._deps(reads, writes)
        waits = self._waits(eng, deps, pe_skip=(eng == "pe"))
        self.cnt[eng] += 1
        tok = (self.sem[eng], self.cnt[eng], eng)
        self.ops[eng].append((waits, fn, (self.sem[eng], 1)))
        self._commit(tok, reads, writes)
        return tok

    def dma(self, eng, out, in_, reads=(), writes=(), **kw):
        deps = self._deps(reads, writes)
        i = self.ndma % NDMASEM
        val = 16 * (self.ndma // NDMASEM + 1)
        self.ndma += 1
        if self.dma_last[i] is not None:
            deps.append(self.dma_last[i])
        waits = self._waits(eng, deps)
        tok = (self.dsem[i], val, "dma")
        self.dma_last[i] = tok
        self.ops[eng].append((waits, lambda e, out=out, in_=in_, kw=kw: e.dma_start(out=out, in_=in_, **kw),
                              (self.dsem[i], 16)))
        self._commit(tok, reads, writes)
        self.all_tokens.append(tok)
        return tok

    def barrier(self):
        toks = [(self.sem[e], self.cnt[e], e) for e in self.ENG if self.cnt[e] > 0]
        toks += [t for t in self.dma_last if t is not None]
        for e in self.ENG:
            waits = self._waits(e, toks)
            if waits:
                self.ops[e].append((waits, None, None))
        self.last_w.clear()
        self.readers.clear()
        print("[prog] counts", self.cnt, "ndma", self.ndma, flush=True)

    def finish(self):
        self.barrier()
        with self.nc.Block() as block:
            def runner(name):
                def body(e):
                    for waits, fn, inc in self.ops[name]:
                        for (s, v) in waits:
                            e.wait_ge(s, v)
                        if fn is not None:
                            ins = fn(e)
                            ins.then_inc(inc[0], inc[1])
                return body
            block.tensor(runner("pe"))
            block.scalar(runner("act"))
            block.vector(runner("dve"))
            block.gpsimd(runner("pool"))
            block.sync(runner("sp"))
        self.es.close()
        return self.nc


class Ctx:
    def __init__(self, p, T):
        self.p = p
        self.T = T
        self.NTB = T // 512
        self.xT = p.sbuf("xT", [128, KC, T], F32)
        self.ones_bf = p.sbuf("ones_bf", [128, 128], BF16)
        p.op("dve", "memset", writes=[("ones",)], ap=self.ones_bf[:, :], constant=1.0)
        self.banks = [p.psum("bank%d" % i, [128, 512], F32) for i in range(8)]
        self.NSTG = 2
        self.stg = [p.sbuf("stg", [128, 2048], F32) for _ in range(self.NSTG)]
        self.nstg = 0

    def bank(self, i):
        return self.banks[i][:, :], ("ps", i)

    def load_w(self, dst, dkey, src, ncols):
        p = self.p
        i = self.nstg % self.NSTG
        self.nstg += 1
        p.dma("sp", self.stg[i][:, 0:ncols], src, writes=[("stg", i)])
        p.op("pool", "tensor_copy", reads=[("stg", i)], writes=[dkey], out=dst, in_=self.stg[i][:, 0:ncols])


def emit_load_x(c, x_dram):
    p = c.p
    for kc in range(KC):
        p.dma("sp", c.xT[:, kc, :], x_dram[:, kc, :], writes=[("x", kc, tb) for tb in range(c.NTB)])


def emit_store_x(c, x_dram):
    p = c.p
    for kc in range(KC):
        p.dma("sp", x_dram[:, kc, :], c.xT[:, kc, :], reads=[("x", kc, tb) for tb in range(c.NTB)],
              writes=[("xout", kc)])


def emit_rmsnorm(c, st, hT, gcol, hkey="h"):
    p = c.p
    sq = [p.sbuf("sq", [128, 512], BF16, st) for _ in range(2)]
    rs = [p.sbuf("rs", [128, 512], F32, st) for _ in range(2)]
    for tb in range(c.NTB):
        ts = slice(tb * 512, (tb + 1) * 512)
        ps, pk = c.bank(tb)
        for kc in range(KC):
            s = sq[kc % 2]
            p.op("act", "activation", reads=[("x", kc, tb)], writes=[("sq", kc % 2)],
                 out=s[:, :], in_=c.xT[:, kc, ts], func=AF.Square)
            p.op("pe", "matmul", reads=[("sq", kc % 2), ("ones",)], writes=[pk],
                 out=ps, lhsT=c.ones_bf[:, :], rhs=s[:, :], start=(kc == 0), stop=(kc == KC - 1))
        r = rs[tb % 2]
        p.op("act", "activation", reads=[pk], writes=[("rs", tb % 2)],
             out=r[:, :], in_=ps, func=AF.Sqrt, scale=1.0 / D, bias=EPS)
        p.op("dve", "reciprocal", reads=[("rs", tb % 2)], writes=[("rs", tb % 2)], out=r[:, :], in_=r[:, :])
        for kc in range(KC):
            p.op("dve", "scalar_tensor_tensor", reads=[("x", kc, tb), ("rs", tb % 2), ("par",)],
                 writes=[(hkey, kc, tb)],
                 out=hT[:, kc, ts], in0=c.xT[:, kc, ts], scalar=gcol[:, kc:kc + 1], in1=r[:, :],
                 op0=ALU.mult, op1=ALU.mult)


def emit_ffn(c, wg_d, wu_d, wd_d, gcol_d):
    p = c.p
    T = c.T
    with contextlib.ExitStack() as st:
        hT = p.sbuf("hT", [128, KC, T], BF16, st)
        actq = p.sbuf("actq", [128, FQ, T], BF16, st)
        NS = 3
        wg = [p.sbuf("wg", [128, 2048], BF16, st) for _ in range(NS)]
        wu = [p.sbuf("wu", [128, 2048], BF16, st) for _ in range(NS)]
        NSD = 4
        wd = [p.sbuf("wd", [128, FQ * 128], BF16, st) for _ in range(NSD)]
        sg = [p.sbuf("sg", [128, 512], F32, st) for _ in range(2)]
        gcol = p.sbuf("gcol", [128, KC], F32, st)
        p.dma("sp", gcol[:, :], gcol_d, writes=[("par",)])
        emit_rmsnorm(c, st, hT, gcol)
        nsg = 0
        ndc = 0
        import os
        dbg = int(os.environ.get("DBG", "9"))
        for q in range(NQ if dbg >= 2 else 0):
            for fl in range(FQ):
                fc = q * FQ + fl
                s = fc % NS
                c.load_w(wg[s][:, :], ("wg", s), wg_d[fc], 2048)
                c.load_w(wu[s][:, :], ("wu", s), wu_d[fc], 2048)
                base = 4 * (fc % 2)
                for kc in range(KC):
                    for tb in range(c.NTB):
                        ps, pk = c.bank(base + tb)
                        p.op("pe", "matmul", reads=[("wg", s), ("h", kc, tb)], writes=[pk],
                             out=ps, lhsT=wg[s][:, kc * 128:(kc + 1) * 128],
                             rhs=hT[:, kc, tb * 512:(tb + 1) * 512], start=(kc == 0), stop=(kc == KC - 1))
                    for tb in range(c.NTB):
                        ps, pk = c.bank(base + 2 + tb)
                        p.op("pe", "matmul", reads=[("wu", s), ("h", kc, tb)], writes=[pk],
                             out=ps, lhsT=wu[s][:, kc * 128:(kc + 1) * 128],
                             rhs=hT[:, kc, tb * 512:(tb + 1) * 512], start=(kc == 0), stop=(kc == KC - 1))
                for tb in range(c.NTB):
                    gps, gk = c.bank(base + tb)
                    ups, uk = c.bank(base + 2 + tb)
                    sgt = sg[nsg % 2]
                    sk = ("sg", nsg % 2)
                    nsg += 1
                    p.op("act", "activation", reads=[gk], writes=[sk], out=sgt[:, :], in_=gps, func=AF.Silu)
                    p.op("dve", "tensor_tensor", reads=[sk, uk], writes=[("actq", fl, tb)],
                         out=actq[:, fl, tb * 512:(tb + 1) * 512], in0=sgt[:, :], in1=ups, op=ALU.mult)
            for dc in range(KC if dbg >= 3 else 0):
                s = ndc % NSD
                c.load_w(wd[s][:, :], ("wd", s), wd_d[q, dc], FQ * 128)
                base = 2 * (ndc % 4)
                ndc += 1
                for fl in range(FQ):
                    for tb in range(c.NTB):
                        ps, pk = c.bank(base + tb)
                        p.op("pe", "matmul", reads=[("wd", s), ("actq", fl, tb)], writes=[pk],
                             out=ps, lhsT=wd[s][:, fl * 128:(fl + 1) * 128],
                             rhs=actq[:, fl, tb * 512:(tb + 1) * 512], start=(fl == 0), stop=(fl == FQ - 1))
                for tb in range(c.NTB):
                    ps, pk = c.bank(base + tb)
                    ts = slice(tb * 512, (tb + 1) * 512)
                    p.op("dve", "scalar_tensor_tensor", reads=[pk, ("x", dc, tb)], writes=[("x", dc, tb)],
                         out=c.xT[:, dc, ts], in0=ps, scalar=0.5, in1=c.xT[:, dc, ts],
                         op0=ALU.mult, op1=ALU.add)
        p.barrier()


def lay_w_stat(w):
    K, M = w.shape
    return np.ascontiguousarray(w.reshape(K // 128, 128, M // 128, 128).transpose(2, 1, 0, 3)).reshape(
        M // 128, 128, K)


def lay_wd(w):
    return np.ascontiguousarray(w.reshape(NQ, FQ, 128, KC, 128).transpose(0, 3, 2, 1, 4)).reshape(
        NQ, KC, 128, FQ * 128)


def lay_col(v):
    return np.ascontiguousarray(v.reshape(-1, 128).T)


def lay_xT(xtok):
    T = xtok.shape[0]
    return np.ascontiguousarray(xtok.T.reshape(KC, 128, T).transpose(1, 0, 2))


def unlay_xT(xT):
    T = xT.shape[2]
    return np.ascontiguousarray(xT.transpose(1, 0, 2).reshape(D, T).T)


TS = 2048
NCH = TS // 128
NH = 32
NG = 4
HPG = 8
HD = 64
NST = 128
NXC = NH * HD // 128
NCC = NXC + 2 * NG
MASKV = 30000.0


def emit_ssd_core(p, xbc_d, cw_d, cb_d, dtraw_d, dtpar_d, dbc_d, cstf_d, cstb_d, ymain_d, yboff_d):
    es = contextlib.ExitStack()
    with es:
        cstb = p.sbuf("cstb", [128, 384], BF16, es)
        SEL = p.sbuf("SEL", [128, 64 * 128], BF16, es)
        onesb = p.sbuf("onesb", [128, 128], BF16, es)
        dtpar = p.sbuf("dtpar", [128, 8], F32, es)
        dbc = p.sbuf("dbc", [128, NH], F32, es)
        cw = p.sbuf("cw", [128, NCC * 5], F32, es)
        cbias = p.sbuf("cbias", [128, NCC], F32, es)
        Rhl = p.sbuf("Rhl", [128, 2, TS], BF16, es)
        tokA = p.sbuf("tokA", [128, NCH, 128], F32, es)
        tokB = p.sbuf("tokB", [128, NCH, 128], F32, es)
        decbc = p.sbuf("decbc", [128, NCH, 64], F32, es)
        p.dma("sp", cstb[:, :], cstb_d[:, 0:384], writes=[("cstb",)])
        p.dma("sp", SEL[:, :], cstb_d[:, 384:384 + 64 * 128], writes=[("SEL",)])
        p.dma("sp", dtpar[:, :], dtpar_d, writes=[("dtpar",)])
        p.dma("sp", dbc[:, :], dbc_d, writes=[("dbc",)])
        p.dma("sp", cw[:, :], cw_d, writes=[("cw",)])
        p.dma("sp", cbias[:, :], cb_d, writes=[("cbias",)])
        p.op("dve", "memset", writes=[("onesb",)], ap=onesb[:, :], constant=1.0)
        identb = cstb[:, 0:128]
        maskF = cstb[:, 128:256]
        maskB = cstb[:, 256:384]

        with contextlib.ExitStack() as sa:
            reset = p.sbuf("reset", [128, TS], F32, sa)
            Rt = p.sbuf("Rt", [128, TS], F32, sa)
            hl = p.sbuf("hl", [128, 3, 2, TS], BF16, sa)
            dtr = p.sbuf("dtr", [128, TS], F32, sa)
            dt = p.sbuf("dt", [128, TS], F32, sa)
            dta = p.sbuf("dta", [128, TS], F32, sa)
            cs = p.sbuf("cs", [128, TS], F32, sa)
            Qt = p.sbuf("Qt", [128, TS], F32, sa)
            eR = p.sbuf("eR", [128, TS], F32, sa)
            eQ = p.sbuf("eQ", [128, TS], F32, sa)
            QA = p.sbuf("QA", [128, TS], F32, sa)
            QB = p.sbuf("QB", [128, TS], F32, sa)
            acol = p.sbuf("acol", [128, 2], F32, sa)
            psA = [p.psum("psA", [128, 512], F32, sa) for _ in range(2)]
            psB = [p.psum("psB", [128, 512], F32, sa) for _ in range(2)]
            p.dma("sp", reset[:, :], cstf_d[:, 128:128 + TS], writes=[("reset",)])
            p.dma("sp", dtr[:, :], dtraw_d, writes=[("dtr",)])
            p.op("act", "activation", reads=[("dtpar",)], writes=[("acol",)], out=acol[:, 0:1], in_=dtpar[:, 1:2],
                 func=AF.Exp)
            p.op("dve", "tensor_scalar", reads=[("acol",)], writes=[("acol2",)], out=acol[:, 1:2], in0=acol[:, 0:1],
                 scalar1=-1.0, scalar2=None, op0=ALU.mult)
            p.op("act", "activation", reads=[("dtr",), ("dtpar",)], writes=[("dt",)], out=dt[:, :], in_=dtr[:, :],
                 func=AF.Exp, bias=dtpar[:, 0:1], scale=1.0)
            p.op("act", "activation", reads=[("dt",)], writes=[("dt",)], out=dt[:, :], in_=dt[:, :],
                 func=AF.Ln, bias=1.0, scale=1.0)
            import os
            sa_lvl = int(os.environ.get("SSD_A", "9"))
            if sa_lvl < 1:
                p.barrier(); return
            p.op("dve", "tensor_scalar", reads=[("dt",), ("acol2",)], writes=[("dta",)], out=dta[:, :], in0=dt[:, :],
                 scalar1=acol[:, 1:2], scalar2=None, op0=ALU.mult)
            p.op("dve", "tensor_tensor_scan", reads=[("reset",), ("dta",)], writes=[("cs",)], out=cs[:, :],
                 data0=reset[:, :], data1=dta[:, :], initial=0.0, op0=ALU.mult, op1=ALU.add)
            if sa_lvl < 2:
                p.barrier(); return
            p.op("dve", "tensor_scalar", reads=[("dta",), ("dtpar",)], writes=[("Qt",)], out=Qt[:, :], in0=dta[:, :],
                 scalar1=dtpar[:, 3:4], scalar2=None, op0=ALU.mult)
            p.op("dve", "tensor_tensor", reads=[("cs",), ("Qt",)], writes=[("R",)], out=Rt[:, :], in0=cs[:, :],
                 in1=Qt[:, :], op=ALU.subtract)
            tot_bc = cs[:, :].rearrange("p (c q) -> p c q", q=128)[:, :, 127:128].to_broadcast([128, NCH, 128])
            p.op("dve", "tensor_tensor", reads=[("cs",), ("R",)], writes=[("Qt",)],
                 out=Qt[:, :].rearrange("p (c q) -> p c q", q=128), in0=tot_bc,
                 in1=Rt[:, :].rearrange("p (c q) -> p c q", q=128), op=ALU.subtract)
            p.op("act", "activation", reads=[("R",)], writes=[("eR",)], out=eR[:, :], in_=Rt[:, :], func=AF.Exp)
            p.op("act", "activation", reads=[("Qt",)], writes=[("eQ",)], out=eQ[:, :], in_=Qt[:, :], func=AF.Exp)
            p.op("dve", "tensor_copy", reads=[("dt",)], writes=[("QA0",)], out=QA[0:64, :], in_=dt[0:64, :])
            p.op("dve", "tensor_scalar", reads=[("R",), ("dtpar",)], writes=[("QA1",)], out=QA[64:128, :],
                 in0=Rt[64:128, :], scalar1=dtpar[64:128, 2:3], scalar2=None, op0=ALU.mult)
            p.op("dve", "tensor_scalar", reads=[("eQ",), ("dtpar",)], writes=[("QB0",)], out=QB[0:64, :],
                 in0=eQ[0:64, :], scalar1=dtpar[0:64, 4:5], scalar2=None, op0=ALU.mult)
            p.op("dve", "scalar_tensor_tensor", reads=[("eR",), ("QB0",), ("dtpar",)], writes=[("QB0",)],
                 out=QB[0:64, :], in0=eR[0:64, :], scalar=dtpar[0:64, 3:4], in1=QB[0:64, :],
                 op0=ALU.mult, op1=ALU.add)
            p.op("dve", "tensor_tensor", reads=[("QB0",), ("dt",)], writes=[("QB0",)], out=QB[0:64, :],
                 in0=QB[0:64, :], in1=dt[0:64, :], op=ALU.mult)
            p.op("dve", "tensor_scalar", reads=[("eR",), ("dtpar",)], writes=[("QB1",)], out=QB[64:128, :],
                 in0=eR[64:128, :], scalar1=dtpar[64:128, 4:5], scalar2=None, op0=ALU.mult)
            p.op("dve", "scalar_tensor_tensor", reads=[("eQ",), ("QB1",), ("dtpar",)], writes=[("QB1",)],
                 out=QB[64:128, :], in0=eQ[64:128, :], scalar=dtpar[64:128, 3:4], in1=QB[64:128, :],
                 op0=ALU.mult, op1=ALU.add)
            if sa_lvl < 3:
                p.barrier(); return
            def split(src, skeys, dst_hi, dst_lo, dkey, scratch, sckey):
                p.op("act", "activation", reads=skeys, writes=[(dkey, 0)], out=dst_hi, in_=src, func=AF.Identity)
                p.op("dve", "tensor_tensor", reads=skeys + [(dkey, 0)], writes=[sckey], out=scratch, in0=src,
                     in1=dst_hi, op=ALU.subtract)
                p.op("act", "activation", reads=[sckey], writes=[(dkey, 1)], out=dst_lo, in_=scratch, func=AF.Identity)
            split(QA[:, :], [("QA0",), ("QA1",)], hl[:, 0, 0, :], hl[:, 0, 1, :], "hlA", eR[:, :], ("eR",))
            split(QB[:, :], [("QB0",), ("QB1",), ("eR",)], hl[:, 1, 0, :], hl[:, 1, 1, :], "hlB", eQ[:, :], ("eQ",))
            split(Rt[:, :], [("R",), ("eQ",)], Rhl[:, 0, :], Rhl[:, 1, :], "Rhl", Qt[:, :], ("Qt",))
            identb_ = cstb[:, 0:128]
            if sa_lvl < 4:
                p.barrier(); return
            Dm = dt[:, 0:NCH * 64]
            Dm3 = Dm.rearrange("p (c r) -> p c r", r=64)
            tot3 = cs[:, :].rearrange("p (c q) -> p c q", q=128)[:, :, 127:128].to_broadcast([128, NCH, 64])
            id3 = cstb[:, 0:64].rearrange("p (o r) -> p o r", o=1).to_broadcast([128, NCH, 64])
            p.op("dve", "tensor_tensor", reads=[("cs",), ("cstb",), ("hlA", 0), ("hlA", 1), ("hlB", 0), ("hlB", 1)],
                 writes=[("Dm",)], out=Dm3, in0=tot3, in1=id3, op=ALU.mult)
            if sa_lvl < 5:
                p.barrier(); return
            Dhl = hl[:, 2, :, 0:NCH * 64]
            split(Dm, [("Dm",), ("Qt",), ("Rhl", 1)], Dhl[:, 0, :], Dhl[:, 1, :], "Dhl", Qt[:, 0:NCH * 64], ("Qt",))
            if sa_lvl < 6:
                p.barrier(); return
            for half in range(2):
                pb = psB[half]
                for w in range(2):
                    p.op("pe", "matmul", reads=[("Dhl", w), ("onesb",)], writes=[("psB", half)], out=pb[:, :],
                         lhsT=onesb[:, :], rhs=Dhl[:, w, half * 512:(half + 1) * 512], start=(w == 0), stop=(w == 1))
                p.op("act", "activation", reads=[("psB", half)], writes=[("decbc", half)],
                     out=decbc[:, half * 8:(half + 1) * 8, :],
                     in_=pb[:, :].rearrange("p (c r) -> p c r", r=64), func=AF.Exp)
            if sa_lvl < 7:
                p.barrier(); return
            for c in range(NCH):
                cs_ = slice(c * 128, (c + 1) * 128)
                pa = psA[c % 2]
                pak = ("psA", c % 2)
                for q in range(2):
                    for w in range(2):
                        p.op("pe", "matmul", reads=[("hlA", w), ("hlB", w), ("cstb",)], writes=[pak],
                             out=pa[:, q * 128:(q + 1) * 128], lhsT=hl[:, q, w, cs_], rhs=identb_,
                             start=(w == 0), stop=(w == 1))
                p.op("act", "activation", reads=[pak], writes=[("tokA", c)], out=tokA[:, c, :], in_=pa[:, 0:128],
                     func=AF.Identity)
                p.op("act", "activation", reads=[pak], writes=[("tokB", c)], out=tokB[:, c, :], in_=pa[:, 128:256],
                     func=AF.Identity)
            p.barrier()
        import os
        if int(os.environ.get("SSD_DBG", "9")) < 2:
            return
        _emit_ssd_main(p, es, xbc_d, ymain_d, yboff_d, SEL, identb, maskF, maskB, dbc, cw, cbias, Rhl, tokA, tokB,
                       decbc)
        p.barrier()


def _emit_ssd_main(p, es, xbc_d, ymain_d, yboff_d, SEL, identb, maskF, maskB, dbc, cw, cbias, Rhl, tokA, tokB,
                   decbc):
    x_tok = p.sbuf("x_tok", [128, NCH, NH * HD], BF16, es)
    B_tok = p.sbuf("B_tok", [128, NCH, NG * NST], BF16, es)
    BT = p.sbuf("BT", [128, NG, TS], BF16, es)
    CT = p.sbuf("CT", [128, NG, TS], BF16, es)
    with contextlib.ExitStack() as sb:
        u = [p.sbuf("u", [128, TS + 4], F32, sb) for _ in range(2)]
        acc = [p.sbuf("acc", [128, TS], F32, sb) for _ in range(2)]
        fm = [p.sbuf("fm", [128, TS], BF16, sb) for _ in range(2)]
        pst = [p.psum("pst", [128, 1024], F32, sb) for _ in range(2)]
        npst = 0
        for cc in range(NCC):
            b = cc % 2
            p.dma("sp", u[b][:, :], xbc_d[cc], writes=[("u", b)])
            p.op("dve", "tensor_scalar", reads=[("u", b), ("cw",), ("cbias",)], writes=[("acc", b)],
                 out=acc[b][:, :], in0=u[b][:, 0:TS], scalar1=cw[:, cc * 5:cc * 5 + 1],
                 scalar2=cbias[:, cc:cc + 1], op0=ALU.mult, op1=ALU.add)
            for j in range(1, 5):
                p.op("dve", "scalar_tensor_tensor", reads=[("u", b), ("cw",), ("acc", b)], writes=[("acc", b)],
                     out=acc[b][:, :], in0=u[b][:, j:j + TS], scalar=cw[:, cc * 5 + j:cc * 5 + j + 1],
                     in1=acc[b][:, :], op0=ALU.mult, op1=ALU.add)
            if cc < NXC:
                dst, dk = fm[b][:, :], ("fm", b)
            elif cc < NXC + NG:
                dst, dk = BT[:, cc - NXC, :], ("BT", cc - NXC)
            else:
                dst, dk = CT[:, cc - NXC - NG, :], ("CT", cc - NXC - NG)
            p.op("act", "activation", reads=[("acc", b)], writes=[dk], out=dst, in_=acc[b][:, :], func=AF.Silu)
            if cc < NXC + NG:
                for c4 in range(NCH // 8):
                    pt = pst[npst % 2]
                    pk = ("pst", npst % 2)
                    npst += 1
                    for k in range(8):
                        c = c4 * 8 + k
                        p.op("pe", "matmul", reads=[dk, ("cstb",)], writes=[pk],
                             out=pt[:, k * 128:(k + 1) * 128], lhsT=dst[:, c * 128:(c + 1) * 128], rhs=identb,
                             start=True, stop=True)
                    if cc < NXC:
                        o = x_tok[:, c4 * 8:(c4 + 1) * 8, cc * 128:(cc + 1) * 128]
                        ok = [("x_tok", c) for c in range(c4 * 8, c4 * 8 + 8)]
                    else:
                        g = cc - NXC
                        o = B_tok[:, c4 * 8:(c4 + 1) * 8, g * 128:(g + 1) * 128]
                        ok = [("B_tok", c) for c in range(c4 * 8, c4 * 8 + 8)]
                    p.op("pool" if False else "act", "activation", reads=[pk], writes=ok, out=o,
                         in_=pt[:, :].rearrange("p (k q) -> p k q", q=128), func=AF.Identity)
        p.barrier()

    import os
    sdbg = int(os.environ.get("SSD_DBG", "9"))
    if sdbg < 3:
        return
    with contextlib.ExitStack() as sc:
        H = p.sbuf("H", [128, NG, 512], F32, sc)
        Hb = p.sbuf("Hb", [128, NG, 512], BF16, sc)
        xw = [p.sbuf("xw", [128, NH * HD], BF16, sc) for _ in range(1)] * 2
        yt = [p.sbuf("yt", [128, NH * HD], F32, sc) for _ in range(1)] * 2
        tmp = [p.sbuf("tmp", [128, 512], F32, sc) for _ in range(2)]
        tmp2 = [p.sbuf("tmp2", [128, 512], F32, sc) for _ in range(2)]
        Lt = [p.sbuf("Lt", [128, 128], BF16, sc) for _ in range(4)]
        Mt = [p.sbuf("Mt", [128, 128], BF16, sc) for _ in range(4)]
        cbt = [p.sbuf("cbt", [128, NG * 128], BF16, sc) for _ in range(2)]
        ps_seg = [p.psum("ps_seg", [128, 512], F32, sc) for _ in range(2)]
        ps_cb = p.psum("ps_cb", [128, 512], F32, sc)
        ps_y = [p.psum("ps_y", [128, 512], F32, sc) for _ in range(2)]
        ps_g = [p.psum("ps_g", [128, 512], F32, sc) for _ in range(2)]
        ps_s = p.psum("ps_s", [128, 512], F32, sc)
        cnt = {"g": 0, "t": 0, "l": 0}

        def hview(g):
            return H[:, g, :].rearrange("p (h d) -> p h d", d=HD)

        def state_update(c, d, nxw):
            xwt = xw[nxw % 2]
            xk = ("xw", 0)
            wst = tokB[:, c, d * NH:(d + 1) * NH].rearrange("p (h o) -> p h o", o=1).to_broadcast([128, NH, HD])
            p.op("pool", "tensor_tensor", reads=[("x_tok", c), ("tokB", c)], writes=[xk],
                 out=xwt[:, :].rearrange("p (h d) -> p h d", d=HD),
                 in0=x_tok[:, c, :].rearrange("p (h d) -> p h d", d=HD), in1=wst, op=ALU.mult)
            for g in range(NG):
                p.op("pe", "matmul", reads=[("B_tok", c), xk], writes=[("ps_s",)], out=ps_s[:, :],
                     lhsT=B_tok[:, c, g * 128:(g + 1) * 128], rhs=xwt[:, g * 512:(g + 1) * 512],
                     start=True, stop=True)
                dec = decbc[:, c, d * NH + g * HPG:d * NH + (g + 1) * HPG].rearrange(
                    "p (h o) -> p h o", o=1).to_broadcast([128, HPG, HD])
                p.op("pool", "tensor_tensor", reads=[("H", g), ("decbc", c // 8)], writes=[("H", g)], out=hview(g),
                     in0=hview(g), in1=dec, op=ALU.mult)
                p.op("dve", "tensor_tensor", reads=[("H", g), ("ps_s",)], writes=[("H", g)], out=H[:, g, :],
                     in0=H[:, g, :], in1=ps_s[:, :], op=ALU.add)
                p.op("act", "activation", reads=[("H", g)], writes=[("Hb", g)], out=Hb[:, g, :], in_=H[:, g, :],
                     func=AF.Identity)

        def g_term(c, d, g):
            pg = ps_g[cnt["g"] % 2]
            gk = ("ps_g", cnt["g"] % 2)
            cnt["g"] += 1
            p.op("pe", "matmul", reads=[("CT", g), ("Hb", g)], writes=[gk], out=pg[:, :],
                 lhsT=CT[:, g, c * 128:(c + 1) * 128], rhs=Hb[:, g, :], start=True, stop=True)
            t = tmp[cnt["t"] % 2]
            tk = ("tmp", cnt["t"] % 2)
            cnt["t"] += 1
            eo = tokB[:, c, 64 + d * NH + g * HPG:64 + d * NH + (g + 1) * HPG].rearrange(
                "p (h o) -> p h o", o=1).to_broadcast([128, HPG, HD])
            p.op("dve", "tensor_tensor", reads=[gk, ("tokB", c)], writes=[tk],
                 out=t[:, :].rearrange("p (h d) -> p h d", d=HD),
                 in0=pg[:, :].rearrange("p (h d) -> p h d", d=HD), in1=eo, op=ALU.mult)
            return t, tk

        for g in range(NG):
            p.op("pool", "memset", writes=[("H", g)], ap=H[:, g, :], constant=0.0)
            p.op("pool", "memset", writes=[("Hb", g)], ap=Hb[:, g, :], constant=0.0)
        nxw = 0
        for c in range(NCH - 1, -1, -1):
            ytile = yt[c % 2]
            for g in range(NG):
                t, tk = g_term(c, 1, g)
                p.op("pool", "tensor_copy", reads=[tk], writes=[("yt", 0, g)],
                     out=ytile[:, g * 512:(g + 1) * 512], in_=t[:, :])
            p.dma("sp", yboff_d[c * 128:(c + 1) * 128, :], ytile[:, :], reads=[("yt", 0, g) for g in range(NG)],
                  writes=[("yboff", c)])
            if c > 0:
                state_update(c, 1, nxw)
                nxw += 1

        if sdbg < 4:
            p.barrier()
            return
        for g in range(NG):
            p.op("pool", "memset", writes=[("H", g)], ap=H[:, g, :], constant=0.0)
            p.op("pool", "memset", writes=[("Hb", g)], ap=Hb[:, g, :], constant=0.0)
        nseg = 0
        for c in range(NCH):
            cs_ = slice(c * 128, (c + 1) * 128)
            ytile = yt[c % 2]
            cb_t = cbt[c % 2]
            for g in range(NG):
                p.op("pe", "matmul", reads=[("BT", g), ("CT", g)], writes=[("ps_cb",)],
                     out=ps_cb[:, g * 128:(g + 1) * 128], lhsT=BT[:, g, cs_], rhs=CT[:, g, cs_],
                     start=True, stop=True)
            p.op("act", "activation", reads=[("ps_cb",)], writes=[("cbt", c % 2)], out=cb_t[:, :], in_=ps_cb[:, :],
                 func=AF.Identity)
            for g in range(NG):
                py = ps_y[g % 2]
                yk = ("ps_y", g % 2)
                for hp in range(HPG // 2):
                    bank = nseg % 2
                    nseg += 1
                    sk = ("ps_seg", bank)
                    units = [(hp * 2 + u // 2, u % 2) for u in range(4)]
                    for u, (hl, d) in enumerate(units):
                        r = d * NH + g * HPG + hl
                        pseg = ps_seg[bank][:, u * 128:(u + 1) * 128]
                        for w in range(2):
                            p.op("pe", "matmul", reads=[("Rhl", w), ("SEL",)], writes=[sk], out=pseg,
                                 lhsT=SEL[:, r * 128:(r + 1) * 128], rhs=Rhl[:, w, cs_],
                                 start=(w == 0), stop=False)
                        p.op("pe", "matmul", reads=[("cstb",)], writes=[sk], out=pseg, lhsT=identb,
                             rhs=(maskF if d == 0 else maskB), start=False, stop=True)
                    for u, (hl, d) in enumerate(units):
                        h = g * HPG + hl
                        r = d * NH + h
                        pseg = ps_seg[bank][:, u * 128:(u + 1) * 128]
                        li = cnt["l"] % 4
                        cnt["l"] += 1
                        p.op("act", "activation", reads=[sk, ("tokA", c)], writes=[("Lt", li)], out=Lt[li][:, :],
                             in_=pseg, func=AF.Exp, bias=tokA[:, c, 64 + r:64 + r + 1],
                             scale=(1.0 if d == 0 else -1.0))
                        p.op("dve", "scalar_tensor_tensor", reads=[("Lt", li), ("tokA", c), ("cbt", c % 2)],
                             writes=[("Mt", li)], out=Mt[li][:, :], in0=Lt[li][:, :], scalar=tokA[:, c, r:r + 1],
                             in1=cb_t[:, g * 128:(g + 1) * 128], op0=ALU.mult, op1=ALU.mult)
                        p.op("pe", "matmul", reads=[("Mt", li), ("x_tok", c)], writes=[yk],
                             out=py[:, hl * HD:(hl + 1) * HD], lhsT=Mt[li][:, :],
                             rhs=x_tok[:, c, h * HD:(h + 1) * HD], start=(d == 0), stop=(d == 1))
                t, tk = g_term(c, 0, g)
                t2 = tmp2[g % 2]
                dv = dbc[:, g * HPG:(g + 1) * HPG].rearrange("p (h o) -> p h o", o=1).to_broadcast([128, HPG, HD])
                p.op("pool", "tensor_tensor", reads=[("x_tok", c), ("dbc",)], writes=[("tmp2", g % 2)],
                     out=t2[:, :].rearrange("p (h d) -> p h d", d=HD),
                     in0=x_tok[:, c, g * 512:(g + 1) * 512].rearrange("p (h d) -> p h d", d=HD), in1=dv, op=ALU.mult)
                p.op("pool", "tensor_tensor", reads=[("tmp2", g % 2), tk], writes=[("tmp2", g % 2)], out=t2[:, :],
                     in0=t2[:, :], in1=t[:, :], op=ALU.add)
                p.op("dve", "tensor_tensor", reads=[("tmp2", g % 2), yk], writes=[("yt", 0, g)],
                     out=ytile[:, g * 512:(g + 1) * 512], in0=t2[:, :], in1=py[:, :], op=ALU.add)
            p.dma("sp", ymain_d[c * 128:(c + 1) * 128, :], ytile[:, :], reads=[("yt", 0, g) for g in range(NG)],
                  writes=[("ymain", c)])
            if c < NCH - 1:
                state_update(c, 0, nxw)
                nxw += 1
        p.barrier()


def emit_norm_generic(c, st, src, skey, nk, gcol, dst, dkey, tbs, dim):
    p = c.p
    sq = [p.sbuf("sq", [128, 512], BF16, st) for _ in range(2)]
    rs = [p.sbuf("rs", [128, 512], F32, st) for _ in range(2)]
    for i, tb in enumerate(tbs):
        ts = slice(i * 512, (i + 1) * 512)
        ps, pk = c.bank(i % 2)
        for k in range(nk):
            s = sq[k % 2]
            p.op("act", "activation", reads=[(skey, k, tb)], writes=[("sq", k % 2)],
                 out=s[:, :], in_=src[:, k, ts], func=AF.Square)
            p.op("pe", "matmul", reads=[("sq", k % 2), ("ones",)], writes=[pk],
                 out=ps, lhsT=c.ones_bf[:, :], rhs=s[:, :], start=(k == 0), stop=(k == nk - 1))
        r = rs[i % 2]
        p.op("act", "activation", reads=[pk], writes=[("rs", i % 2)],
             out=r[:, :], in_=ps, func=AF.Sqrt, scale=1.0 / dim, bias=EPS)
        p.op("dve", "reciprocal", reads=[("rs", i % 2)], writes=[("rs", i % 2)], out=r[:, :], in_=r[:, :])
        for k in range(nk):
            p.op("dve", "scalar_tensor_tensor", reads=[(skey, k, tb), ("rs", i % 2), ("par",)],
                 writes=[(dkey, k, tb)],
                 out=dst[:, k, ts], in0=src[:, k, ts], scalar=gcol[:, k:k + 1], in1=r[:, :],
                 op0=ALU.mult, op1=ALU.mult)


def emit_proj_out(c, st, hT, w_d, out_d, ncc, nsilu=0):
    p = c.p
    NS = 3
    wt = [p.sbuf("wt", [128, 2048], BF16, st) for _ in range(NS)]
    ot = [p.sbuf("ot", [128, 512], F32, st) for _ in range(4)]
    no = 0
    for cc in range(ncc):
        s = cc % NS
        c.load_w(wt[s][:, :], ("wt", s), w_d[cc], 2048)
        for kc in range(KC):
            for tb in range(c.NTB):
                ps, pk = c.bank(2 * (cc % 4) + tb)
                p.op("pe", "matmul", reads=[("wt", s), ("h", kc, tb)], writes=[pk],
                     out=ps, lhsT=wt[s][:, kc * 128:(kc + 1) * 128], rhs=hT[:, kc, tb * 512:(tb + 1) * 512],
                     start=(kc == 0), stop=(kc == KC - 1))
        for tb in range(c.NTB):
            ps, pk = c.bank(2 * (cc % 4) + tb)
            o = ot[no % 4]
            ok = ("ot", no % 4)
            no += 1
            p.op("act", "activation", reads=[pk], writes=[ok], out=o[:, :], in_=ps,
                 func=(AF.Silu if cc < nsilu else AF.Identity))
            p.dma("sp", out_d[cc, :, tb * 512:(tb + 1) * 512], o[:, :], reads=[ok], writes=[("projout", cc, tb)])


def emit_proj_resid(c, st, srcT, skey, nk, w_d, tbs, scale, wt, wkey_base):
    p = c.p
    nw = len(wt)
    for dc in range(KC):
        s = c.nwt % nw
        c.nwt += 1
        for a in range(0, nk * 128, 2048):
            b = min(a + 2048, nk * 128)
            c.load_w(wt[s][:, a:b], (wkey_base, s, a), w_d[dc, :, a:b], b - a)
        wkeys = [(wkey_base, s, a) for a in range(0, nk * 128, 2048)]
        for k in range(nk):
            for i, tb in enumerate(tbs):
                ps, pk = c.bank(2 * (dc % 4) + (i % 2))
                p.op("pe", "matmul", reads=wkeys + [(skey, k, tb)], writes=[pk],
                     out=ps, lhsT=wt[s][:, k * 128:(k + 1) * 128], rhs=srcT[:, k, i * 512:(i + 1) * 512],
                     start=(k == 0), stop=(k == nk - 1))
        for i, tb in enumerate(tbs):
            ps, pk = c.bank(2 * (dc % 4) + (i % 2))
            ts = slice(tb * 512, (tb + 1) * 512)
            p.op("dve", "scalar_tensor_tensor", reads=[pk, ("x", dc, tb)], writes=[("x", dc, tb)],
                 out=c.xT[:, dc, ts], in0=ps, scalar=scale, in1=c.xT[:, dc, ts], op0=ALU.mult, op1=ALU.add)


def emit_mix_in(c, gcol_d, w_d, out_d, ncc, nsilu):
    p = c.p
    with contextlib.ExitStack() as st:
        hT = p.sbuf("hT", [128, KC, c.T], BF16, st)
        gcol = p.sbuf("gcol", [128, KC], F32, st)
        p.dma("sp", gcol[:, :], gcol_d, writes=[("par",)])
        emit_rmsnorm(c, st, hT, gcol)
        emit_proj_out(c, st, hT, w_d, out_d, ncc, nsilu)
        p.barrier()


NIC = 32


def emit_ssd_out(c, ym_d, yb_d, sz_d, gn_d, wout_d):
    p = c.p
    with contextlib.ExitStack() as st:
        gT = p.sbuf("gT", [128, NIC, 512], BF16, st)
        ynT = p.sbuf("ynT", [128, NIC, 512], BF16, st)
        ld = [p.sbuf("ld", [128, 3, 512], F32, st) for _ in range(2)]
        gcol = p.sbuf("gcol", [128, NIC], F32, st)
        wt = [p.sbuf("wo", [128, NIC * 128], BF16, st) for _ in range(2)]
        p.dma("sp", gcol[:, :], gn_d, writes=[("par",)])
        c.nwt = 0
        for tb in range(c.NTB):
            ts = slice(tb * 512, (tb + 1) * 512)
            for k in range(NIC):
                l = ld[k % 2]
                p.dma("sp", l[:, 0, :], ym_d[k, :, ts], writes=[("ld", k % 2, 0)])
                p.dma("sp", l[:, 1, :], yb_d[k, :, ts], writes=[("ld", k % 2, 1)])
                p.dma("sp", l[:, 2, :], sz_d[k, :, ts], writes=[("ld", k % 2, 2)])
                p.op("pool", "tensor_tensor", reads=[("ld", k % 2, 0), ("ld", k % 2, 1)], writes=[("ld", k % 2, 0)],
                     out=l[:, 0, :], in0=l[:, 0, :], in1=l[:, 1, :], op=ALU.add)
                p.op("pool", "tensor_tensor", reads=[("ld", k % 2, 0), ("ld", k % 2, 2)], writes=[("g", k, tb)],
                     out=gT[:, k, :], in0=l[:, 0, :], in1=l[:, 2, :], op=ALU.mult)
            emit_norm_generic(c, st, gT, "g", NIC, gcol, ynT, "yn", [tb], NIC * 128)
            emit_proj_resid(c, st, ynT, "yn", NIC, wout_d, [tb], 1.0, wt, "wo")
        p.barrier()


POOL_W = (2, 4, 8, 16)


def emit_pool_out(c, up_d, icnt_d, wg_d, sc_d, wout_d):
    p = c.p
    T = c.T
    with contextlib.ExitStack() as st:
        mixT = p.sbuf("mixT", [128, KC, T], BF16, st)
        vT = p.sbuf("vT", [128, KC, T], BF16, st)
        icnt = p.sbuf("icnt", [128, 4 * T], F32, st)
        sc = p.sbuf("sc", [128, KC], F32, st)
        A = [[p.sbuf("A", [128, T + 16], F32, st) for _ in range(3)] for _ in range(2)]
        wgt = [p.sbuf("wgt", [128, 512], BF16, st) for _ in range(2)]
        wt = [p.sbuf("wpo", [128, 2048], BF16, st) for _ in range(3)]
        p.dma("sp", icnt[:, :], icnt_d, writes=[("icnt",)])
        p.dma("sp", sc[:, :], sc_d, writes=[("par",)])
        for cc in range(KC):
            gi = cc // 4
            b = cc % 2
            eng = "pool" if b == 0 else "dve"
            a0, a1, a2 = A[b]
            p.dma("sp", a0[:, :], up_d[cc], writes=[("A", b, 0)])
            p.op(eng, "tensor_tensor", reads=[("A", b, 0)], writes=[("A", b, 1)], out=a1[:, 1:T + 16],
                 in0=a0[:, 0:T + 15], in1=a0[:, 1:T + 16], op=ALU.add)
            cur, curk, oth, othk = a1, ("A", b, 1), a2, ("A", b, 2)
            lo, hi = 1, T + 16
            sh = 1
            for lvl in range(gi):
                nlo, nhi = lo + sh, hi - sh
                p.op(eng, "tensor_tensor", reads=[curk], writes=[othk], out=oth[:, nlo:nhi],
                     in0=cur[:, nlo - sh:nhi - sh], in1=cur[:, nlo + sh:nhi + sh], op=ALU.add)
                cur, curk, oth, othk = oth, othk, cur, curk
                lo, hi = nlo, nhi
                sh *= 2
            assert lo <= 8 and hi >= T + 8
            p.op(eng, "tensor_tensor", reads=[curk, ("icnt",)], writes=[othk], out=oth[:, 8:8 + T],
                 in0=cur[:, 8:8 + T], in1=icnt[:, gi * T:(gi + 1) * T], op=ALU.mult)
            for tb in range(c.NTB):
                p.op(eng, "tensor_tensor", reads=[othk, ("A", b, 0)], writes=[("mix", cc, tb)],
                     out=mixT[:, cc, tb * 512:(tb + 1) * 512], in0=oth[:, 8 + tb * 512:8 + (tb + 1) * 512],
                     in1=a0[:, 8 + tb * 512:8 + (tb + 1) * 512], op=ALU.subtract)
        for dc in range(KC):
            gi = dc // 4
            s = dc % 2
            c.load_w(wgt[s][:, :], ("wgt", s), wg_d[dc], 512)
            for kl in range(4):
                for tb in range(c.NTB):
                    ps, pk = c.bank(2 * (dc % 4) + tb)
                    p.op("pe", "matmul", reads=[("wgt", s), ("mix", gi * 4 + kl, tb)], writes=[pk],
                         out=ps, lhsT=wgt[s][:, kl * 128:(kl + 1) * 128],
                         rhs=mixT[:, gi * 4 + kl, tb * 512:(tb + 1) * 512], start=(kl == 0), stop=(kl == 3))
            for tb in range(c.NTB):
                ps, pk = c.bank(2 * (dc % 4) + tb)
                p.op("act", "activation", reads=[pk, ("par",)], writes=[("v", dc, tb)],
                     out=vT[:, dc, tb * 512:(tb + 1) * 512], in_=ps, func=AF.Identity, scale=sc[:, dc:dc + 1])
        c.nwt = 0
        emit_proj_resid(c, st, vT, "v", KC, wout_d, list(range(c.NTB)), 1.0, wt, "wpo")
        p.barrier()


def emit_final(c, gcol_d, out_d):
    p = c.p
    with contextlib.ExitStack() as st:
        oT = p.sbuf("oT", [128, KC, c.T], F32, st)
        gcol = p.sbuf("gcol", [128, KC], F32, st)
        p.dma("sp", gcol[:, :], gcol_d, writes=[("par",)])
        emit_rmsnorm(c, st, oT, gcol, hkey="o")
        for kc in range(KC):
            p.dma("sp", out_d[:, kc, :], oT[:, kc, :], reads=[("o", kc, tb) for tb in range(c.NTB)],
                  writes=[("final", kc)])
        p.barrier()


TT = 1024
_PROG_CACHE = {}


def build_tok_launch(stages):
    key = tuple(stages)
    if key in _PROG_CACHE:
        return _PROG_CACHE[key]
    p = Prog()
    c = Ctx(p, TT)
    ein = lambda n, sh, dt=F32: p.dram(n, sh, dt, kind="ExternalInput")
    eout = lambda n, sh, dt=F32: p.dram(n, sh, dt, kind="ExternalOutput")
    emit_load_x(c, ein("xin", [128, KC, TT]))
    has_final = False
    for i, stg in enumerate(stages):
        t = "_%d" % i
        if stg == "ffn":
            emit_ffn(c, ein("wg" + t, [FC, 128, 2048]), ein("wu" + t, [FC, 128, 2048]),
                     ein("wd" + t, [NQ, KC, 128, FQ * 128]), ein("gcol" + t, [128, KC]))
        elif stg == "ssd_in":
            emit_mix_in(c, ein("gcol" + t, [128, KC]), ein("w" + t, [81, 128, 2048]), eout("proj", [81, 128, TT]), 81, 32)
        elif stg == "pool_in":
            emit_mix_in(c, ein("gcol" + t, [128, KC]), ein("w" + t, [KC, 128, 2048]), eout("proj", [KC, 128, TT]), KC, 0)
        elif stg == "ssd_out":
            emit_ssd_out(c, ein("ym" + t, [NIC, 128, TT]), ein("yb" + t, [NIC, 128, TT]), ein("sz" + t, [NIC, 128, TT]),
                         ein("gn" + t, [128, NIC]), ein("wout" + t, [KC, 128, NIC * 128]))
        elif stg == "pool_out":
            emit_pool_out(c, ein("up" + t, [KC, 128, TT + 16]), ein("icnt" + t, [128, 4 * TT]),
                          ein("wgrp" + t, [KC, 128, 512]), ein("sc" + t, [128, KC]), ein("wout" + t, [KC, 128, 2048]))
        elif stg == "final":
            emit_final(c, ein("gcol" + t, [128, KC]), eout("final", [128, KC, TT]))
            has_final = True
    if not has_final:
        emit_store_x(c, eout("xout", [128, KC, TT]))
    nc = p.finish()
    _PROG_CACHE[key] = nc
    return nc


def build_ssd_core_launch():
    if "ssd_core" in _PROG_CACHE:
        return _PROG_CACHE["ssd_core"]
    p = Prog()
    ein = lambda n, sh, dt=F32: p.dram(n, sh, dt, kind="ExternalInput")
    eout = lambda n, sh, dt=F32: p.dram(n, sh, dt, kind="ExternalOutput")
    emit_ssd_core(p, ein("xbc", [NCC, 128, TS + 4]), ein("cw", [128, NCC * 5]), ein("cb", [128, NCC]),
                  ein("dtraw", [128, TS]), ein("dtpar", [128, 8]), ein("dbc", [128, NH]),
                  ein("cstf", [128, 128 + TS]), ein("cstb", [128, 384 + 64 * 128], BF16),
                  eout("ymain", [TS, NH * HD]), eout("yboff", [TS, NH * HD]))
    nc = p.finish()
    _PROG_CACHE["ssd_core"] = nc
    return nc


def _ssd_consts():
    import ml_dtypes
    identf = np.eye(128, dtype=np.float32)
    reset = np.ones((128, TS), np.float32)
    reset[:, ::128] = 0.0
    cstf = np.concatenate([identf, reset], axis=1)
    j = np.arange(128)[:, None]
    i = np.arange(128)[None, :]
    maskF = np.where(j > i, -MASKV, 0.0).astype(np.float32)
    maskB = np.where(j < i, MASKV, 0.0).astype(np.float32)
    sel = np.zeros((128, 64, 128), np.float32)
    for r in range(64):
        sel[r, r, :] = 1.0
    cstb = np.concatenate([identf, maskF, maskB, sel.reshape(128, 64 * 128)], axis=1).astype(ml_dtypes.bfloat16)
    return cstf, cstb


def _run(nc, in_maps):
    res = run_bass_kernel_spmd(nc, in_maps, core_ids=list(range(8)))
    return res.results


def _ffn_inputs(t, wg, wu, wd, g):
    return {"wg" + t: lay_w_stat(wg), "wu" + t: lay_w_stat(wu), "wd" + t: lay_wd(wd), "gcol" + t: lay_col(g)}


def kernel(x, ffn_norm, ffn_w_gate, ffn_w_up, ffn_w_down, mix_norm, ssd_w_in, ssd_conv_w, ssd_conv_b, ssd_dt_bias,
           ssd_a_log, ssd_d, ssd_norm, ssd_w_out, pool_w_in, pool_w_group, pool_scale, pool_w_out, final_norm):
    f = lambda a: np.ascontiguousarray(np.asarray(a, dtype=np.float32))
    x = f(x)
    B, L, _ = x.shape
    xs = [lay_xT(x[c // 2, (c % 2) * TT:(c % 2 + 1) * TT]) for c in range(8)]
    cstf, cstb = _ssd_consts()

    def ffn_in(t, l, i):
        return _ffn_inputs(t, f(ffn_w_gate[l, i]), f(ffn_w_up[l, i]), f(ffn_w_down[l, i]), f(ffn_norm[l, i]))

    def tok_launch(stages, shared, percore):
        nc = build_tok_launch(stages)
        in_maps = []
        for c in range(8):
            m = {"xin": xs[c]}
            m.update(shared)
            m.update(percore[c])
            in_maps.append(m)
        return _run(nc, in_maps)

    def ssd_core_stage(j, proj):
        nc = build_ssd_core_launch()
        cwf, cbf = f(ssd_conv_w[j]), f(ssd_conv_b[j])
        in_maps = []
        for c in range(8):
            b, hh = c // 2, c % 2
            P = np.concatenate([proj[2 * b].reshape(81 * 128, TT), proj[2 * b + 1].reshape(81 * 128, TT)], axis=1)
            rows = np.concatenate([np.arange(4096 + hh * 2048, 4096 + (hh + 1) * 2048),
                                   np.arange(8192 + hh * 512, 8192 + (hh + 1) * 512),
                                   np.arange(9216 + hh * 512, 9216 + (hh + 1) * 512)])
            xbc = np.pad(P[rows], ((0, 0), (2, 2))).reshape(NCC, 128, TS + 4)
            cch = rows - 4096
            cw = np.ascontiguousarray(cwf[:, cch].T.reshape(NCC, 128, 5).transpose(1, 0, 2)).reshape(128, NCC * 5)
            cb = np.ascontiguousarray(cbf[cch].reshape(NCC, 128).T)
            drow = np.concatenate([10240 + d * 64 + hh * 32 + np.arange(32) for d in range(2)])
            dtr = P[drow]
            par = np.zeros((64, 8), np.float32)
            par[:, 0] = f(ssd_dt_bias[j])[:, hh * 32:(hh + 1) * 32].reshape(64)
            par[:, 1] = f(ssd_a_log[j])[:, hh * 32:(hh + 1) * 32].reshape(64)
            par[:32, 2] = -1.0
            par[32:, 2] = 1.0
            par[32:, 3] = 1.0
            par[:32, 4] = 1.0
            dbc = np.ascontiguousarray(np.broadcast_to(f(ssd_d[j])[None, hh * 32:(hh + 1) * 32], (128, NH)))
            in_maps.append({"xbc": np.ascontiguousarray(xbc), "cw": cw, "cb": cb,
                            "dtraw": np.ascontiguousarray(np.concatenate([dtr, dtr], axis=0)),
                            "dtpar": np.concatenate([par, par], axis=0), "dbc": dbc, "cstf": cstf, "cstb": cstb})
        res = _run(nc, in_maps)
        ym, yb = [], []
        for c in range(8):
            b, half = c // 2, c % 2
            for name, lst in (("ymain", ym), ("yboff", yb)):
                Y = np.concatenate([res[2 * b][name], res[2 * b + 1][name]], axis=1)
                lst.append(np.ascontiguousarray(Y[half * TT:(half + 1) * TT].T.reshape(NIC, 128, TT)))
        return ym, yb

    def pool_inputs(t, j, proj):
        per = []
        for c in range(8):
            b, half = c // 2, c % 2
            U = np.concatenate([proj[2 * b].reshape(D, TT), proj[2 * b + 1].reshape(D, TT)], axis=1)
            Up = np.pad(U, ((0, 0), (8, 8)))
            up = np.ascontiguousarray(Up[:, half * TT:half * TT + TT + 16]).reshape(KC, 128, TT + 16)
            tg = half * TT + np.arange(TT)
            ic = np.stack([1.0 / (np.clip(tg + w // 2, 0, L) - np.clip(tg - w // 2, 0, L)) for w in POOL_W])
            icnt = np.ascontiguousarray(np.broadcast_to(ic.reshape(1, 4 * TT), (128, 4 * TT))).astype(np.float32)
            per.append({"up" + t: up, "icnt" + t: icnt})
        shared = {"wgrp" + t: np.concatenate([lay_w_stat(f(pool_w_group[j, gi])) for gi in range(4)], axis=0),
                  "sc" + t: lay_col(f(pool_scale[j])), "wout" + t: lay_w_stat(f(pool_w_out[j]))}
        return shared, per

    empty = [dict() for _ in range(8)]
    sh = ffn_in("_0", 0, 0)
    sh.update({"gcol_1": lay_col(f(mix_norm[0])), "w_1": lay_w_stat(f(ssd_w_in[0]))})
    r = tok_launch(("ffn", "ssd_in"), sh, empty)
    xs = [r[c]["xout"] for c in range(8)]
    proj = [r[c]["proj"] for c in range(8)]
    for j in range(2):
        l = 2 * j
        ym, yb = ssd_core_stage(j, proj)
        sh = {"gn_0": lay_col(f(ssd_norm[j])), "wout_0": lay_w_stat(f(ssd_w_out[j]))}
        sh.update(ffn_in("_1", l, 1))
        sh.update(ffn_in("_2", l + 1, 0))
        sh.update({"gcol_3": lay_col(f(mix_norm[l + 1])), "w_3": lay_w_stat(f(pool_w_in[j]))})
        per = [{"ym_0": ym[c], "yb_0": yb[c], "sz_0": np.ascontiguousarray(proj[c][0:NIC])} for c in range(8)]
        r = tok_launch(("ssd_out", "ffn", "ffn", "pool_in"), sh, per)
        xs = [r[c]["xout"] for c in range(8)]
        proj = [r[c]["proj"] for c in range(8)]
        sh, per = pool_inputs("_0", j, proj)
        sh.update(ffn_in("_1", l + 1, 1))
        if j == 0:
            sh.update(ffn_in("_2", l + 2, 0))
            sh.update({"gcol_3": lay_col(f(mix_norm[l + 2])), "w_3": lay_w_stat(f(ssd_w_in[1]))})
            r = tok_launch(("pool_out", "ffn", "ffn", "ssd_in"), sh, per)
            xs = [r[c]["xout"] for c in range(8)]
            proj = [r[c]["proj"] for c in range(8)]
        else:
            sh.update({"gcol_2": lay_col(f(final_norm))})
            r = tok_launch(("pool_out", "ffn", "final"), sh, per)
    out = np.empty((B, L, D), np.float32)
    for c in range(8):
        out[c // 2, (c % 2) * TT:(c % 2 + 1) * TT] = unlay_xT(r[c]["final"])
    return out
```

```python
import contextlib
import numpy as np
import concourse.bass as bass
import concourse.mybir as mybir
from concourse.bass_utils import run_bass_kernel_spmd

F32 = mybir.dt.float32
BF16 = mybir.dt.bfloat16
ALU = mybir.AluOpType
AF = mybir.ActivationFunctionType

D = 2048
KC = D // 128
DFF = 5632
FC = DFF // 128
NQ = 4
FQ = FC // NQ
EPS = 1e-6
NDMASEM = 24


class Prog:
    ENG = ("pe", "act", "dve", "pool", "sp")

    def __init__(self):
        self.nc = bass.Bass("TRN2", target_bir_lowering=False)
        self.es = contextlib.ExitStack()
        self.ops = {e: [] for e in self.ENG}
        self.cnt = {e: 0 for e in self.ENG}
        self.sem = {e: self.es.enter_context(self.nc.semaphore("s_" + e)) for e in self.ENG}
        self.dsem = [self.es.enter_context(self.nc.semaphore("d%d" % i)) for i in range(NDMASEM)]
        self.ndma = 0
        self.dma_last = [None] * NDMASEM
        self.waited = {e: {} for e in self.ENG}
        self.last_w = {}
        self.readers = {}
        self.all_tokens = []
        self.uid = 0

    def sbuf(self, name, shape, dtype, stack=None):
        self.uid += 1
        t = (stack or self.es).enter_context(self.nc.sbuf_tensor("%s_%d" % (name, self.uid), list(shape), dtype))
        return t

    def psum(self, name, shape, dtype, stack=None):
        self.uid += 1
        return (stack or self.es).enter_context(self.nc.psum_tensor("%s_%d" % (name, self.uid), list(shape), dtype))

    def dram(self, name, shape, dtype, kind="Internal"):
        return self.nc.dram_tensor(name, list(shape), dtype, kind=kind).ap()

    def _deps(self, reads, writes):
        deps = []
        for k in reads:
            if k in self.last_w:
                deps.append(self.last_w[k])
        for k in writes:
            if k in self.last_w:
                deps.append(self.last_w[k])
            deps.extend(self.readers.get(k, ()))
        return deps

    def _commit(self, tok, reads, writes):
        for k in reads:
            self.readers.setdefault(k, []).append(tok)
        for k in writes:
            self.last_w[k] = tok
            self.readers[k] = []

    def _waits(self, eng, deps, pe_skip=False):
        w = self.waited[eng]
        best = {}
        for (s, v, src) in deps:
            if pe_skip and src == "pe":
                continue
            if w.get(id(s), 0) >= v:
                continue
            if best.get(id(s), (None, 0))[1] < v:
                best[id(s)] = (s, v)
        out = []
        for sid, (s, v) in best.items():
            w[sid] = v
            out.append((s, v))
        return out

    def op(self, eng, meth, reads=(), writes=(), **kw):
        fn = (lambda e, meth=meth, kw=kw: getattr(e, meth)(**kw))
        deps = self._deps(reads, writes)
        waits = self._waits(eng, deps, pe_skip=(eng == "pe"))
        self.cnt[eng] += 1
        tok = (self.sem[eng], self.cnt[eng], eng)
        self.ops[eng].append((waits, fn, (self.sem[eng], 1)))
        self._commit(tok, reads, writes)
        return tok

    def dma(self, eng, out, in_, reads=(), writes=(), **kw):
        deps = self._deps(reads, writes)
        i = self.ndma % NDMASEM
        val = 16 * (self.ndma // NDMASEM + 1)
        self.ndma += 1
        if self.dma_last[i] is not None:
            deps.append(self.dma_last[i])
        waits = self._waits(eng, deps)
        tok = (self.dsem[i], val, "dma")
        self.dma_last[i] = tok
        self.ops[eng].append((waits, lambda e, out=out, in_=in_, kw=kw: e.dma_start(out=out, in_=in_, **kw),
                              (self.dsem[i], 16)))
        self._commit(tok, reads, writes)
        self.all_tokens.append(tok)
        return tok

    def barrier(self):
        toks = [(self.sem[e], self.cnt[e], e) for e in self.ENG if self.cnt[e] > 0]
        toks += [t for t in self.dma_last if t is not None]
        for e in self.ENG:
            waits = self._waits(e, toks)
            if waits:
                self.ops[e].append((waits, None, None))
        self.last_w.clear()
        self.readers.clear()
        print("[prog] counts", self.cnt, "ndma", self.ndma, flush=True)

    def finish(self):
        self.barrier()
        with self.nc.Block() as block:
            def runner(name):
                def body(e):
                    for waits, fn, inc in self.ops[name]:
                        for (s, v) in waits:
                            e.wait_ge(s, v)
                        if fn is not None:
                            ins = fn(e)
                            ins.then_inc(inc[0], inc[1])
                return body
            block.tensor(runner("pe"))
            block.scalar(runner("act"))
            block.vector(runner("dve"))
            block.gpsimd(runner("pool"))
            block.sync(runner("sp"))
        self.es.close()
        return self.nc


class Ctx:
    def __init__(self, p, T):
        self.p = p
        self.T = T
        self.NTB = T // 512
        self.xT = p.sbuf("xT", [128, KC, T], F32)
        self.ones_bf = p.sbuf("ones_bf", [128, 128], BF16)
        p.op("dve", "memset", writes=[("ones",)], ap=self.ones_bf[:, :], constant=1.0)
        self.banks = [p.psum("bank%d" % i, [128, 512], F32) for i in range(8)]
        self.NSTG = 3
        import os
        self.two_q = os.environ.get("TWOQ", "0") == "1"
        self.stg = [p.sbuf("stg", [128, 2048], F32) for _ in range(self.NSTG)]
        self.nstg = 0

    def bank(self, i):
        return self.banks[i][:, :], ("ps", i)

    def load_w(self, dst, dkey, src, ncols):
        p = self.p
        i = self.nstg % self.NSTG
        self.nstg += 1
        q = "sp" if (self.nstg % 2 == 0 or not self.two_q) else "act"
        p.dma(q, self.stg[i][:, 0:ncols], src, writes=[("stg", i)])
        p.op("pool", "tensor_copy", reads=[("stg", i)], writes=[dkey], out=dst, in_=self.stg[i][:, 0:ncols])


def emit_load_x(c, x_dram):
    p = c.p
    for kc in range(KC):
        p.dma("sp", c.xT[:, kc, :], x_dram[:, kc, :], writes=[("x", kc, tb) for tb in range(c.NTB)])


def emit_store_x(c, x_dram):
    p = c.p
    for kc in range(KC):
        p.dma("sp", x_dram[:, kc, :], c.xT[:, kc, :], reads=[("x", kc, tb) for tb in range(c.NTB)],
              writes=[("xout", kc)])


def emit_rmsnorm(c, st, hT, gcol, hkey="h"):
    p = c.p
    sq = [p.sbuf("sq", [128, 512], BF16, st) for _ in range(2)]
    rs = [p.sbuf("rs", [128, 512], F32, st) for _ in range(2)]
    for tb in range(c.NTB):
        ts = slice(tb * 512, (tb + 1) * 512)
        ps, pk = c.bank(tb)
        for kc in range(KC):
            s = sq[kc % 2]
            p.op("act", "activation", reads=[("x", kc, tb)], writes=[("sq", kc % 2)],
                 out=s[:, :], in_=c.xT[:, kc, ts], func=AF.Square)
            p.op("pe", "matmul", reads=[("sq", kc % 2), ("ones",)], writes=[pk],
                 out=ps, lhsT=c.ones_bf[:, :], rhs=s[:, :], start=(kc == 0), stop=(kc == KC - 1))
        r = rs[tb % 2]
        p.op("act", "activation", reads=[pk], writes=[("rs", tb % 2)],
             out=r[:, :], in_=ps, func=AF.Sqrt, scale=1.0 / D, bias=EPS)
        p.op("dve", "reciprocal", reads=[("rs", tb % 2)], writes=[("rs", tb % 2)], out=r[:, :], in_=r[:, :])
        for kc in range(KC):
            p.op("dve", "scalar_tensor_tensor", reads=[("x", kc, tb), ("rs", tb % 2), ("par",)],
                 writes=[(hkey, kc, tb)],
                 out=hT[:, kc, ts], in0=c.xT[:, kc, ts], scalar=gcol[:, kc:kc + 1], in1=r[:, :],
                 op0=ALU.mult, op1=ALU.mult)


def emit_ffn(c, wg_d, wu_d, wd_d, gcol_d):
    p = c.p
    T = c.T
    with contextlib.ExitStack() as st:
        hT = p.sbuf("hT", [128, KC, T], BF16, st)
        actq = p.sbuf("actq", [128, FQ, T], BF16, st)
        NS = 3
        wg = [p.sbuf("wg", [128, 2048], BF16, st) for _ in range(NS)]
        wu = [p.sbuf("wu", [128, 2048], BF16, st) for _ in range(NS)]
        NSD = 4
        wd = [p.sbuf("wd", [128, FQ * 128], BF16, st) for _ in range(NSD)]
        sg = [p.sbuf("sg", [128, 512], F32, st) for _ in range(2)]
        gcol = p.sbuf("gcol", [128, KC], F32, st)
        p.dma("sp", gcol[:, :], gcol_d, writes=[("par",)])
        emit_rmsnorm(c, st, hT, gcol)
        nsg = 0
        ndc = 0
        import os
        dbg = int(os.environ.get("DBG", "9"))
        for q in range(NQ if dbg >= 2 else 0):
            for fl in range(FQ):
                fc = q * FQ + fl
                s = fc % NS
                c.load_w(wg[s][:, :], ("wg", s), wg_d[fc], 2048)
                c.load_w(wu[s][:, :], ("wu", s), wu_d[fc], 2048)
                base = 4 * (fc % 2)
                for kc in range(KC):
                    for tb in range(c.NTB):
                        ps, pk = c.bank(base + tb)
                        p.op("pe", "matmul", reads=[("wg", s), ("h", kc, tb)], writes=[pk],
                             out=ps, lhsT=wg[s][:, kc * 128:(kc + 1) * 128],
                             rhs=hT[:, kc, tb * 512:(tb + 1) * 512], start=(kc == 0), stop=(kc == KC - 1))
                    for tb in range(c.NTB):
                        ps, pk = c.bank(base + 2 + tb)
                        p.op("pe", "matmul", reads=[("wu", s), ("h", kc, tb)], writes=[pk],
                             out=ps, lhsT=wu[s][:, kc * 128:(kc + 1) * 128],
                             rhs=hT[:, kc, tb * 512:(tb + 1) * 512], start=(kc == 0), stop=(kc == KC - 1))
                for tb in range(c.NTB):
                    gps, gk = c.bank(base + tb)
                    ups, uk = c.bank(base + 2 + tb)
                    sgt = sg[nsg % 2]
                    sk = ("sg", nsg % 2)
                    nsg += 1
                    p.op("act", "activation", reads=[gk], writes=[sk], out=sgt[:, :], in_=gps, func=AF.Silu)
                    p.op("dve", "tensor_tensor", reads=[sk, uk], writes=[("actq", fl, tb)],
                         out=actq[:, fl, tb * 512:(tb + 1) * 512], in0=sgt[:, :], in1=ups, op=ALU.mult)
            for dc in range(KC if dbg >= 3 else 0):
                s = ndc % NSD
                c.load_w(wd[s][:, :], ("wd", s), wd_d[q, dc], FQ * 128)
                base = 2 * (ndc % 4)
                ndc += 1
                for fl in range(FQ):
                    for tb in range(c.NTB):
                        ps, pk = c.bank(base + tb)
                        p.op("pe", "matmul", reads=[("wd", s), ("actq", fl, tb)], writes=[pk],
                             out=ps, lhsT=wd[s][:, fl * 128:(fl + 1) * 128],
                             rhs=actq[:, fl, tb * 512:(tb + 1) * 512], start=(fl == 0), stop=(fl == FQ - 1))
                for tb in range(c.NTB):
                    ps, pk = c.bank(base + tb)
                    ts = slice(tb * 512, (tb + 1) * 512)
                    p.op("dve", "scalar_tensor_tensor", reads=[pk, ("x", dc, tb)], writes=[("x", dc, tb)],
                         out=c.xT[:, dc, ts], in0=ps, scalar=0.5, in1=c.xT[:, dc, ts],
                         op0=ALU.mult, op1=ALU.add)
        p.barrier()


def lay_w_stat(w):
    K, M = w.shape
    return np.ascontiguousarray(w.reshape(K // 128, 128, M // 128, 128).transpose(2, 1, 0, 3)).reshape(
        M // 128, 128, K)


def lay_wd(w):
    return np.ascontiguousarray(w.reshape(NQ, FQ, 128, KC, 128).transpose(0, 3, 2, 1, 4)).reshape(
        NQ, KC, 128, FQ * 128)


def lay_col(v):
    return np.ascontiguousarray(v.reshape(-1, 128).T)


def lay_xT(xtok):
    T = xtok.shape[0]
    return np.ascontiguousarray(xtok.T.reshape(KC, 128, T).transpose(1, 0, 2))


def unlay_xT(xT):
    T = xT.shape[2]
    return np.ascontiguousarray(xT.transpose(1, 0, 2).reshape(D, T).T)


TS = 2048
NCH = TS // 128
NH = 32
NG = 4
HPG = 8
HD = 64
NST = 128
NXC = NH * HD // 128
NCC = NXC + 2 * NG
MASKV = 30000.0


def emit_ssd_core(p, xbc_d, cw_d, cb_d, dtraw_d, dtpar_d, dbc_d, cstf_d, cstb_d, ymain_d, yboff_d):
    es = contextlib.ExitStack()
    with es:
        cstb = p.sbuf("cstb", [128, 384], BF16, es)
        SEL = p.sbuf("SEL", [128, 64 * 128], BF16, es)
        onesb = p.sbuf("onesb", [128, 128], BF16, es)
        dtpar = p.sbuf("dtpar", [128, 8], F32, es)
        dbc = p.sbuf("dbc", [128, NH], F32, es)
        cw = p.sbuf("cw", [128, NCC * 5], F32, es)
        cbias = p.sbuf("cbias", [128, NCC], F32, es)
        Rhl = p.sbuf("Rhl", [128, 2, TS], BF16, es)
        tokA = p.sbuf("tokA", [128, NCH, 128], F32, es)
        tokB = p.sbuf("tokB", [128, NCH, 128], F32, es)
        decbc = p.sbuf("decbc", [128, NCH, 64], F32, es)
        p.dma("sp", cstb[:, :], cstb_d[:, 0:384], writes=[("cstb",)])
        p.dma("sp", SEL[:, :], cstb_d[:, 384:384 + 64 * 128], writes=[("SEL",)])
        p.dma("sp", dtpar[:, :], dtpar_d, writes=[("dtpar",)])
        p.dma("sp", dbc[:, :], dbc_d, writes=[("dbc",)])
        p.dma("sp", cw[:, :], cw_d, writes=[("cw",)])
        p.dma("sp", cbias[:, :], cb_d, writes=[("cbias",)])
        p.op("dve", "memset", writes=[("onesb",)], ap=onesb[:, :], constant=1.0)
        identb = cstb[:, 0:128]
        maskF = cstb[:, 128:256]
        maskB = cstb[:, 256:384]

        with contextlib.ExitStack() as sa:
            reset = p.sbuf("reset", [128, TS], F32, sa)
            Rt = p.sbuf("Rt", [128, TS], F32, sa)
            hl = p.sbuf("hl", [128, 3, 2, TS], BF16, sa)
            dtr = p.sbuf("dtr", [128, TS], F32, sa)
            dt = p.sbuf("dt", [128, TS], F32, sa)
            dta = p.sbuf("dta", [128, TS], F32, sa)
            cs = p.sbuf("cs", [128, TS], F32, sa)
            Qt = p.sbuf("Qt", [128, TS], F32, sa)
            eR = p.sbuf("eR", [128, TS], F32, sa)
            eQ = p.sbuf("eQ", [128, TS], F32, sa)
            QA = p.sbuf("QA", [128, TS], F32, sa)
            QB = p.sbuf("QB", [128, TS], F32, sa)
            acol = p.sbuf("acol", [128, 2], F32, sa)
            psA = [p.psum("psA", [128, 512], F32, sa) for _ in range(2)]
            psB = [p.psum("psB", [128, 512], F32, sa) for _ in range(2)]
            p.dma("sp", reset[:, :], cstf_d[:, 128:128 + TS], writes=[("reset",)])
            p.dma("sp", dtr[:, :], dtraw_d, writes=[("dtr",)])
            p.op("act", "activation", reads=[("dtpar",)], writes=[("acol",)], out=acol[:, 0:1], in_=dtpar[:, 1:2],
                 func=AF.Exp)
            p.op("dve", "tensor_scalar", reads=[("acol",)], writes=[("acol2",)], out=acol[:, 1:2], in0=acol[:, 0:1],
                 scalar1=-1.0, scalar2=None, op0=ALU.mult)
            p.op("act", "activation", reads=[("dtr",), ("dtpar",)], writes=[("dt",)], out=dt[:, :], in_=dtr[:, :],
                 func=AF.Exp, bias=dtpar[:, 0:1], scale=1.0)
            p.op("act", "activation", reads=[("dt",)], writes=[("dt",)], out=dt[:, :], in_=dt[:, :],
                 func=AF.Ln, bias=1.0, scale=1.0)
            import os
            sa_lvl = int(os.environ.get("SSD_A", "9"))
            if sa_lvl < 1:
                p.barrier(); return
            p.op("dve", "tensor_scalar", reads=[("dt",), ("acol2",)], writes=[("dta",)], out=dta[:, :], in0=dt[:, :],
                 scalar1=acol[:, 1:2], scalar2=None, op0=ALU.mult)
            p.op("dve", "tensor_tensor_scan", reads=[("reset",), ("dta",)], writes=[("cs",)], out=cs[:, :],
                 data0=reset[:, :], data1=dta[:, :], initial=0.0, op0=ALU.mult, op1=ALU.add)
            if sa_lvl < 2:
                p.barrier(); return
            p.op("dve", "tensor_scalar", reads=[("dta",), ("dtpar",)], writes=[("Qt",)], out=Qt[:, :], in0=dta[:, :],
                 scalar1=dtpar[:, 3:4], scalar2=None, op0=ALU.mult)
            p.op("dve", "tensor_tensor", reads=[("cs",), ("Qt",)], writes=[("R",)], out=Rt[:, :], in0=cs[:, :],
                 in1=Qt[:, :], op=ALU.subtract)
            tot_bc = cs[:, :].rearrange("p (c q) -> p c q", q=128)[:, :, 127:128].to_broadcast([128, NCH, 128])
            p.op("dve", "tensor_tensor", reads=[("cs",), ("R",)], writes=[("Qt",)],
                 out=Qt[:, :].rearrange("p (c q) -> p c q", q=128), in0=tot_bc,
                 in1=Rt[:, :].rearrange("p (c q) -> p c q", q=128), op=ALU.subtract)
            p.op("act", "activation", reads=[("R",)], writes=[("eR",)], out=eR[:, :], in_=Rt[:, :], func=AF.Exp)
            p.op("act", "activation", reads=[("Qt",)], writes=[("eQ",)], out=eQ[:, :], in_=Qt[:, :], func=AF.Exp)
            p.op("dve", "tensor_copy", reads=[("dt",)], writes=[("QA0",)], out=QA[0:64, :], in_=dt[0:64, :])
            p.op("dve", "tensor_scalar", reads=[("R",), ("dtpar",)], writes=[("QA1",)], out=QA[64:128, :],
                 in0=Rt[64:128, :], scalar1=dtpar[64:128, 2:3], scalar2=None, op0=ALU.mult)
            p.op("dve", "tensor_scalar", reads=[("eQ",), ("dtpar",)], writes=[("QB0",)], out=QB[0:64, :],
                 in0=eQ[0:64, :], scalar1=dtpar[0:64, 4:5], scalar2=None, op0=ALU.mult)
            p.op("dve", "scalar_tensor_tensor", reads=[("eR",), ("QB0",), ("dtpar",)], writes=[("QB0",)],
                 out=QB[0:64, :], in0=eR[0:64, :], scalar=dtpar[0:64, 3:4], in1=QB[0:64, :],
                 op0=ALU.mult, op1=ALU.add)
            p.op("dve", "tensor_tensor", reads=[("QB0",), ("dt",)], writes=[("QB0",)], out=QB[0:64, :],
                 in0=QB[0:64, :], in1=dt[0:64, :], op=ALU.mult)
            p.op("dve", "tensor_scalar", reads=[("eR",), ("dtpar",)], writes=[("QB1",)], out=QB[64:128, :],
                 in0=eR[64:128, :], scalar1=dtpar[64:128, 4:5], scalar2=None, op0=ALU.mult)
            p.op("dve", "scalar_tensor_tensor", reads=[("eQ",), ("QB1",), ("dtpar",)], writes=[("QB1",)],
                 out=QB[64:128, :], in0=eQ[64:128, :], scalar=dtpar[64:128, 3:4], in1=QB[64:128, :],
                 op0=ALU.mult, op1=ALU.add)
            if sa_lvl < 3:
                p.barrier(); return
            def split(src, skeys, dst_hi, dst_lo, dkey, scratch, sckey):
                p.op("act", "activation", reads=skeys, writes=[(dkey, 0)], out=dst_hi, in_=src, func=AF.Identity)
                p.op("dve", "tensor_tensor", reads=skeys + [(dkey, 0)], writes=[sckey], out=scratch, in0=src,
                     in1=dst_hi, op=ALU.subtract)
                p.op("act", "activation", reads=[sckey], writes=[(dkey, 1)], out=dst_lo, in_=scratch, func=AF.Identity)
            split(QA[:, :], [("QA0",), ("QA1",)], hl[:, 0, 0, :], hl[:, 0, 1, :], "hlA", eR[:, :], ("eR",))
            split(QB[:, :], [("QB0",), ("QB1",), ("eR",)], hl[:, 1, 0, :], hl[:, 1, 1, :], "hlB", eQ[:, :], ("eQ",))
            split(Rt[:, :], [("R",), ("eQ",)], Rhl[:, 0, :], Rhl[:, 1, :], "Rhl", Qt[:, :], ("Qt",))
            identb_ = cstb[:, 0:128]
            if sa_lvl < 4:
                p.barrier(); return
            Dm = dt[:, 0:NCH * 64]
            Dm3 = Dm.rearrange("p (c r) -> p c r", r=64)
            tot3 = cs[:, :].rearrange("p (c q) -> p c q", q=128)[:, :, 127:128].to_broadcast([128, NCH, 64])
            id3 = cstb[:, 0:64].rearrange("p (o r) -> p o r", o=1).to_broadcast([128, NCH, 64])
            p.op("dve", "tensor_tensor", reads=[("cs",), ("cstb",), ("hlA", 0), ("hlA", 1), ("hlB", 0), ("hlB", 1)],
                 writes=[("Dm",)], out=Dm3, in0=tot3, in1=id3, op=ALU.mult)
            if sa_lvl < 5:
                p.barrier(); return
            Dhl = hl[:, 2, :, 0:NCH * 64]
            split(Dm, [("Dm",), ("Qt",), ("Rhl", 1)], Dhl[:, 0, :], Dhl[:, 1, :], "Dhl", Qt[:, 0:NCH * 64], ("Qt",))
            if sa_lvl < 6:
                p.barrier(); return
            for half in range(2):
                pb = psB[half]
                for w in range(2):
                    p.op("pe", "matmul", reads=[("Dhl", w), ("onesb",)], writes=[("psB", half)], out=pb[:, :],
                         lhsT=onesb[:, :], rhs=Dhl[:, w, half * 512:(half + 1) * 512], start=(w == 0), stop=(w == 1))
                p.op("act", "activation", reads=[("psB", half)], writes=[("decbc", half)],
                     out=decbc[:, half * 8:(half + 1) * 8, :],
                     in_=pb[:, :].rearrange("p (c r) -> p c r", r=64), func=AF.Exp)
            if sa_lvl < 7:
                p.barrier(); return
            for c in range(NCH):
                cs_ = slice(c * 128, (c + 1) * 128)
                pa = psA[c % 2]
                pak = ("psA", c % 2)
                for q in range(2):
                    for w in range(2):
                        p.op("pe", "matmul", reads=[("hlA", w), ("hlB", w), ("cstb",)], writes=[pak],
                             out=pa[:, q * 128:(q + 1) * 128], lhsT=hl[:, q, w, cs_], rhs=identb_,
                             start=(w == 0), stop=(w == 1))
                p.op("act", "activation", reads=[pak], writes=[("tokA", c)], out=tokA[:, c, :], in_=pa[:, 0:128],
                     func=AF.Identity)
                p.op("act", "activation", reads=[pak], writes=[("tokB", c)], out=tokB[:, c, :], in_=pa[:, 128:256],
                     func=AF.Identity)
            p.barrier()
        import os
        if int(os.environ.get("SSD_DBG", "9")) < 2:
            return
        _emit_ssd_main(p, es, xbc_d, ymain_d, yboff_d, SEL, identb, maskF, maskB, dbc, cw, cbias, Rhl, tokA, tokB,
                       decbc)
        p.barrier()


def _emit_ssd_main(p, es, xbc_d, ymain_d, yboff_d, SEL, identb, maskF, maskB, dbc, cw, cbias, Rhl, tokA, tokB,
                   decbc):
    x_tok = p.sbuf("x_tok", [128, NCH, NH * HD], BF16, es)
    B_tok = p.sbuf("B_tok", [128, NCH, NG * NST], BF16, es)
    BT = p.sbuf("BT", [128, NG, TS], BF16, es)
    CT = p.sbuf("CT", [128, NG, TS], BF16, es)
    with contextlib.ExitStack() as sb:
        u = [p.sbuf("u", [128, TS + 4], F32, sb) for _ in range(2)]
        acc = [p.sbuf("acc", [128, TS], F32, sb) for _ in range(2)]
        fm = [p.sbuf("fm", [128, TS], BF16, sb) for _ in range(2)]
        pst = [p.psum("pst", [128, 1024], F32, sb) for _ in range(2)]
        npst = 0
        for cc in range(NCC):
            b = cc % 2
            p.dma("sp", u[b][:, :], xbc_d[cc], writes=[("u", b)])
            p.op("dve", "tensor_scalar", reads=[("u", b), ("cw",), ("cbias",)], writes=[("acc", b)],
                 out=acc[b][:, :], in0=u[b][:, 0:TS], scalar1=cw[:, cc * 5:cc * 5 + 1],
                 scalar2=cbias[:, cc:cc + 1], op0=ALU.mult, op1=ALU.add)
            for j in range(1, 5):
                p.op("dve", "scalar_tensor_tensor", reads=[("u", b), ("cw",), ("acc", b)], writes=[("acc", b)],
                     out=acc[b][:, :], in0=u[b][:, j:j + TS], scalar=cw[:, cc * 5 + j:cc * 5 + j + 1],
                     in1=acc[b][:, :], op0=ALU.mult, op1=ALU.add)
            if cc < NXC:
                dst, dk = fm[b][:, :], ("fm", b)
            elif cc < NXC + NG:
                dst, dk = BT[:, cc - NXC, :], ("BT", cc - NXC)
            else:
                dst, dk = CT[:, cc - NXC - NG, :], ("CT", cc - NXC - NG)
            p.op("act", "activation", reads=[("acc", b)], writes=[dk], out=dst, in_=acc[b][:, :], func=AF.Silu)
            if cc < NXC + NG:
                for c4 in range(NCH // 8):
                    pt = pst[npst % 2]
                    pk = ("pst", npst % 2)
                    npst += 1
                    for k in range(8):
                        c = c4 * 8 + k
                        p.op("pe", "matmul", reads=[dk, ("cstb",)], writes=[pk],
                             out=pt[:, k * 128:(k + 1) * 128], lhsT=dst[:, c * 128:(c + 1) * 128], rhs=identb,
                             start=True, stop=True)
                    if cc < NXC:
                        o = x_tok[:, c4 * 8:(c4 + 1) * 8, cc * 128:(cc + 1) * 128]
                        ok = [("x_tok", c) for c in range(c4 * 8, c4 * 8 + 8)]
                    else:
                        g = cc - NXC
                        o = B_tok[:, c4 * 8:(c4 + 1) * 8, g * 128:(g + 1) * 128]
                        ok = [("B_tok", c) for c in range(c4 * 8, c4 * 8 + 8)]
                    p.op("pool" if False else "act", "activation", reads=[pk], writes=ok, out=o,
                         in_=pt[:, :].rearrange("p (k q) -> p k q", q=128), func=AF.Identity)
        p.barrier()

    import os
    sdbg = int(os.environ.get("SSD_DBG", "9"))
    if sdbg < 3:
        return
    with contextlib.ExitStack() as sc:
        H = p.sbuf("H", [128, NG, 512], F32, sc)
        Hb = p.sbuf("Hb", [128, NG, 512], BF16, sc)
        xw = [p.sbuf("xw", [128, NH * HD], BF16, sc) for _ in range(1)] * 2
        yt = [p.sbuf("yt", [128, NH * HD], F32, sc) for _ in range(1)] * 2
        tmp = [p.sbuf("tmp", [128, 512], F32, sc) for _ in range(2)]
        tmp2 = [p.sbuf("tmp2", [128, 512], F32, sc) for _ in range(2)]
        Lt = [p.sbuf("Lt", [128, 128], BF16, sc) for _ in range(4)]
        Mt = [p.sbuf("Mt", [128, 128], BF16, sc) for _ in range(4)]
        cbt = [p.sbuf("cbt", [128, NG * 128], BF16, sc) for _ in range(2)]
        ps_seg = [p.psum("ps_seg", [128, 512], F32, sc) for _ in range(2)]
        ps_cb = p.psum("ps_cb", [128, 512], F32, sc)
        ps_y = [p.psum("ps_y", [128, 512], F32, sc) for _ in range(2)]
        ps_g = [p.psum("ps_g", [128, 512], F32, sc) for _ in range(2)]
        ps_s = p.psum("ps_s", [128, 512], F32, sc)
        cnt = {"g": 0, "t": 0, "l": 0}

        def hview(g):
            return H[:, g, :].rearrange("p (h d) -> p h d", d=HD)

        def state_update(c, d, nxw):
            xwt = xw[nxw % 2]
            xk = ("xw", 0)
            wst = tokB[:, c, d * NH:(d + 1) * NH].rearrange("p (h o) -> p h o", o=1).to_broadcast([128, NH, HD])
            p.op("pool", "tensor_tensor", reads=[("x_tok", c), ("tokB", c)], writes=[xk],
                 out=xwt[:, :].rearrange("p (h d) -> p h d", d=HD),
                 in0=x_tok[:, c, :].rearrange("p (h d) -> p h d", d=HD), in1=wst, op=ALU.mult)
            for g in range(NG):
                p.op("pe", "matmul", reads=[("B_tok", c), xk], writes=[("ps_s",)], out=ps_s[:, :],
                     lhsT=B_tok[:, c, g * 128:(g + 1) * 128], rhs=xwt[:, g * 512:(g + 1) * 512],
                     start=True, stop=True)
                dec = decbc[:, c, d * NH + g * HPG:d * NH + (g + 1) * HPG].rearrange(
                    "p (h o) -> p h o", o=1).to_broadcast([128, HPG, HD])
                p.op("pool", "tensor_tensor", reads=[("H", g), ("decbc", c // 8)], writes=[("H", g)], out=hview(g),
                     in0=hview(g), in1=dec, op=ALU.mult)
                p.op("dve", "tensor_tensor", reads=[("H", g), ("ps_s",)], writes=[("H", g)], out=H[:, g, :],
                     in0=H[:, g, :], in1=ps_s[:, :], op=ALU.add)
                p.op("act", "activation", reads=[("H", g)], writes=[("Hb", g)], out=Hb[:, g, :], in_=H[:, g, :],
                     func=AF.Identity)

        def g_term(c, d, g):
            pg = ps_g[cnt["g"] % 2]
            gk = ("ps_g", cnt["g"] % 2)
            cnt["g"] += 1
            p.op("pe", "matmul", reads=[("CT", g), ("Hb", g)], writes=[gk], out=pg[:, :],
                 lhsT=CT[:, g, c * 128:(c + 1) * 128], rhs=Hb[:, g, :], start=True, stop=True)
            t = tmp[cnt["t"] % 2]
            tk = ("tmp", cnt["t"] % 2)
            cnt["t"] += 1
            eo = tokB[:, c, 64 + d * NH + g * HPG:64 + d * NH + (g + 1) * HPG].rearrange(
                "p (h o) -> p h o", o=1).to_broadcast([128, HPG, HD])
            p.op("dve", "tensor_tensor", reads=[gk, ("tokB", c)], writes=[tk],
                 out=t[:, :].rearrange("p (h d) -> p h d", d=HD),
                 in0=pg[:, :].rearrange("p (h d) -> p h d", d=HD), in1=eo, op=ALU.mult)
            return t, tk

        for g in range(NG):
            p.op("pool", "memset", writes=[("H", g)], ap=H[:, g, :], constant=0.0)
            p.op("pool", "memset", writes=[("Hb", g)], ap=Hb[:, g, :], constant=0.0)
        nxw = 0
        for c in range(NCH - 1, -1, -1):
            ytile = yt[c % 2]
            for g in range(NG):
                t, tk = g_term(c, 1, g)
                p.op("pool", "tensor_copy", reads=[tk], writes=[("yt", 0, g)],
                     out=ytile[:, g * 512:(g + 1) * 512], in_=t[:, :])
            p.dma("sp", yboff_d[c * 128:(c + 1) * 128, :], ytile[:, :], reads=[("yt", 0, g) for g in range(NG)],
                  writes=[("yboff", c)])
            if c > 0:
                state_update(c, 1, nxw)
                nxw += 1

        if sdbg < 4:
            p.barrier()
            return
        for g in range(NG):
            p.op("pool", "memset", writes=[("H", g)], ap=H[:, g, :], constant=0.0)
            p.op("pool", "memset", writes=[("Hb", g)], ap=Hb[:, g, :], constant=0.0)
        nseg = 0
        for c in range(NCH):
            cs_ = slice(c * 128, (c + 1) * 128)
            ytile = yt[c % 2]
            cb_t = cbt[c % 2]
            for g in range(NG):
                p.op("pe", "matmul", reads=[("BT", g), ("CT", g)], writes=[("ps_cb",)],
                     out=ps_cb[:, g * 128:(g + 1) * 128], lhsT=BT[:, g, cs_], rhs=CT[:, g, cs_],
                     start=True, stop=True)
            p.op("act", "activation", reads=[("ps_cb",)], writes=[("cbt", c % 2)], out=cb_t[:, :], in_=ps_cb[:, :],
                 func=AF.Identity)
            for g in range(NG):
                py = ps_y[g % 2]
                yk = ("ps_y", g % 2)
                for hp in range(HPG // 2):
                    bank = nseg % 2
                    nseg += 1
                    sk = ("ps_seg", bank)
                    units = [(hp * 2 + u // 2, u % 2) for u in range(4)]
                    for u, (hl, d) in enumerate(units):
                        r = d * NH + g * HPG + hl
                        pseg = ps_seg[bank][:, u * 128:(u + 1) * 128]
                        for w in range(2):
                            p.op("pe", "matmul", reads=[("Rhl", w), ("SEL",)], writes=[sk], out=pseg,
                                 lhsT=SEL[:, r * 128:(r + 1) * 128], rhs=Rhl[:, w, cs_],
                                 start=(w == 0), stop=False)
                        p.op("pe", "matmul", reads=[("cstb",)], writes=[sk], out=pseg, lhsT=identb,
                             rhs=(maskF if d == 0 else maskB), start=False, stop=True)
                    for u, (hl, d) in enumerate(units):
                        h = g * HPG + hl
                        r = d * NH + h
                        pseg = ps_seg[bank][:, u * 128:(u + 1) * 128]
                        li = cnt["l"] % 4
                        cnt["l"] += 1
                        p.op("act", "activation", reads=[sk, ("tokA", c)], writes=[("Lt", li)], out=Lt[li][:, :],
                             in_=pseg, func=AF.Exp, bias=tokA[:, c, 64 + r:64 + r + 1],
                             scale=(1.0 if d == 0 else -1.0))
                        p.op("dve", "scalar_tensor_tensor", reads=[("Lt", li), ("tokA", c), ("cbt", c % 2)],
                             writes=[("Mt", li)], out=Mt[li][:, :], in0=Lt[li][:, :], scalar=tokA[:, c, r:r + 1],
                             in1=cb_t[:, g * 128:(g + 1) * 128], op0=ALU.mult, op1=ALU.mult)
                        p.op("pe", "matmul", reads=[("Mt", li), ("x_tok", c)], writes=[yk],
                             out=py[:, hl * HD:(hl + 1) * HD], lhsT=Mt[li][:, :],
                             rhs=x_tok[:, c, h * HD:(h + 1) * HD], start=(d == 0), stop=(d == 1))
                t, tk = g_term(c, 0, g)
                t2 = tmp2[g % 2]
                dv = dbc[:, g * HPG:(g + 1) * HPG].rearrange("p (h o) -> p h o", o=1).to_broadcast([128, HPG, HD])
                p.op("pool", "tensor_tensor", reads=[("x_tok", c), ("dbc",)], writes=[("tmp2", g % 2)],
                     out=t2[:, :].rearrange("p (h d) -> p h d", d=HD),
                     in0=x_tok[:, c, g * 512:(g + 1) * 512].rearrange("p (h d) -> p h d", d=HD), in1=dv, op=ALU.mult)
                p.op("pool", "tensor_tensor", reads=[("tmp2", g % 2), tk], writes=[("tmp2", g % 2)], out=t2[:, :],
                     in0=t2[:, :], in1=t[:, :], op=ALU.add)
                p.op("dve", "tensor_tensor", reads=[("tmp2", g % 2), yk], writes=[("yt", 0, g)],
                     out=ytile[:, g * 512:(g + 1) * 512], in0=t2[:, :], in1=py[:, :], op=ALU.add)
            p.dma("sp", ymain_d[c * 128:(c + 1) * 128, :], ytile[:, :], reads=[("yt", 0, g) for g in range(NG)],
                  writes=[("ymain", c)])
            if c < NCH - 1:
                state_update(c, 0, nxw)
                nxw += 1
        p.barrier()


def emit_norm_generic(c, st, src, skey, nk, gcol, dst, dkey, tbs, dim):
    p = c.p
    sq = [p.sbuf("sq", [128, 512], BF16, st) for _ in range(2)]
    rs = [p.sbuf("rs", [128, 512], F32, st) for _ in range(2)]
    for i, tb in enumerate(tbs):
        ts = slice(i * 512, (i + 1) * 512)
        ps, pk = c.bank(i % 2)
        for k in range(nk):
            s = sq[k % 2]
            p.op("act", "activation", reads=[(skey, k, tb)], writes=[("sq", k % 2)],
                 out=s[:, :], in_=src[:, k, ts], func=AF.Square)
            p.op("pe", "matmul", reads=[("sq", k % 2), ("ones",)], writes=[pk],
                 out=ps, lhsT=c.ones_bf[:, :], rhs=s[:, :], start=(k == 0), stop=(k == nk - 1))
        r = rs[i % 2]
        p.op("act", "activation", reads=[pk], writes=[("rs", i % 2)],
             out=r[:, :], in_=ps, func=AF.Sqrt, scale=1.0 / dim, bias=EPS)
        p.op("dve", "reciprocal", reads=[("rs", i % 2)], writes=[("rs", i % 2)], out=r[:, :], in_=r[:, :])
        for k in range(nk):
            p.op("dve", "scalar_tensor_tensor", reads=[(skey, k, tb), ("rs", i % 2), ("par",)],
                 writes=[(dkey, k, tb)],
                 out=dst[:, k, ts], in0=src[:, k, ts], scalar=gcol[:, k:k + 1], in1=r[:, :],
                 op0=ALU.mult, op1=ALU.mult)


def emit_proj_out(c, st, hT, w_d, out_d, ncc, nsilu=0):
    p = c.p
    NS = 3
    wt = [p.sbuf("wt", [128, 2048], BF16, st) for _ in range(NS)]
    ot = [p.sbuf("ot", [128, 512], F32, st) for _ in range(4)]
    no = 0
    for cc in range(ncc):
        s = cc % NS
        c.load_w(wt[s][:, :], ("wt", s), w_d[cc], 2048)
        for kc in range(KC):
            for tb in range(c.NTB):
                ps, pk = c.bank(2 * (cc % 4) + tb)
                p.op("pe", "matmul", reads=[("wt", s), ("h", kc, tb)], writes=[pk],
                     out=ps, lhsT=wt[s][:, kc * 128:(kc + 1) * 128], rhs=hT[:, kc, tb * 512:(tb + 1) * 512],
                     start=(kc == 0), stop=(kc == KC - 1))
        for tb in range(c.NTB):
            ps, pk = c.bank(2 * (cc % 4) + tb)
            o = ot[no % 4]
            ok = ("ot", no % 4)
            no += 1
            p.op("act", "activation", reads=[pk], writes=[ok], out=o[:, :], in_=ps,
                 func=(AF.Silu if cc < nsilu else AF.Identity))
            p.dma("act", out_d[cc, :, tb * 512:(tb + 1) * 512], o[:, :], reads=[ok], writes=[("projout", cc, tb)])


def emit_proj_resid(c, st, srcT, skey, nk, w_d, tbs, scale, wt, wkey_base):
    p = c.p
    nw = len(wt)
    for dc in range(KC):
        s = c.nwt % nw
        c.nwt += 1
        for a in range(0, nk * 128, 2048):
            b = min(a + 2048, nk * 128)
            c.load_w(wt[s][:, a:b], (wkey_base, s, a), w_d[dc, :, a:b], b - a)
        wkeys = [(wkey_base, s, a) for a in range(0, nk * 128, 2048)]
        for k in range(nk):
            for i, tb in enumerate(tbs):
                ps, pk = c.bank(2 * (dc % 4) + (i % 2))
                p.op("pe", "matmul", reads=wkeys + [(skey, k, tb)], writes=[pk],
                     out=ps, lhsT=wt[s][:, k * 128:(k + 1) * 128], rhs=srcT[:, k, i * 512:(i + 1) * 512],
                     start=(k == 0), stop=(k == nk - 1))
        for i, tb in enumerate(tbs):
            ps, pk = c.bank(2 * (dc % 4) + (i % 2))
            ts = slice(tb * 512, (tb + 1) * 512)
            p.op("dve", "scalar_tensor_tensor", reads=[pk, ("x", dc, tb)], writes=[("x", dc, tb)],
                 out=c.xT[:, dc, ts], in0=ps, scalar=scale, in1=c.xT[:, dc, ts], op0=ALU.mult, op1=ALU.add)


def emit_mix_in(c, gcol_d, w_d, out_d, ncc, nsilu):
    p = c.p
    with contextlib.ExitStack() as st:
        hT = p.sbuf("hT", [128, KC, c.T], BF16, st)
        gcol = p.sbuf("gcol", [128, KC], F32, st)
        p.dma("sp", gcol[:, :], gcol_d, writes=[("par",)])
        emit_rmsnorm(c, st, hT, gcol)
        emit_proj_out(c, st, hT, w_d, out_d, ncc, nsilu)
        p.barrier()


NIC = 32


def emit_ssd_out(c, ym_d, yb_d, sz_d, gn_d, wout_d):
    p = c.p
    T = c.T
    with contextlib.ExitStack() as st:
        gs = p.sbuf("gs", [128, NIC, T], BF16, st)
        KB = 2
        ld = [p.sbuf("ld", [128, 3, KB, 512], F32, st) for _ in range(2)]
        sq = [p.sbuf("sq", [128, KB, 512], BF16, st) for _ in range(2)]
        rs = [p.sbuf("rs", [128, 512], F32, st) for _ in range(c.NTB)]
        tmpo = [p.sbuf("tmpo", [128, 512], F32, st) for _ in range(2)]
        gcol = p.sbuf("gcol", [128, NIC], F32, st)
        wt = [p.sbuf("wo", [128, NIC * 128], BF16, st) for _ in range(2)]
        p.dma("sp", gcol[:, :], gn_d, writes=[("par",)])
        n = 0
        for tb in range(c.NTB):
            ts = slice(tb * 512, (tb + 1) * 512)
            ps, pk = c.bank(tb)
            for k0 in range(0, NIC, KB):
                i = n % 2
                n += 1
                l = ld[i]
                src = lambda d: d[k0:k0 + KB, :, ts].rearrange("k p t -> p k t")
                p.dma("sp", l[:, 0, :, :], src(ym_d), writes=[("ld", i, 0)])
                p.dma("act", l[:, 1, :, :], src(yb_d), writes=[("ld", i, 1)])
                p.dma("sp", l[:, 2, :, :], src(sz_d), writes=[("ld", i, 2)])
                p.op("pool", "tensor_tensor", reads=[("ld", i, 0), ("ld", i, 1)], writes=[("ld", i, 0)],
                     out=l[:, 0, :, :], in0=l[:, 0, :, :], in1=l[:, 1, :, :], op=ALU.add)
                p.op("pool", "tensor_tensor", reads=[("ld", i, 0), ("ld", i, 2)], writes=[("ld", i, 0)],
                     out=l[:, 0, :, :], in0=l[:, 0, :, :], in1=l[:, 2, :, :], op=ALU.mult)
                s_ = sq[i]
                p.op("act", "activation", reads=[("ld", i, 0)], writes=[("sq", i)], out=s_[:, :, :],
                     in_=l[:, 0, :, :], func=AF.Square)
                for kk in range(KB):
                    k = k0 + kk
                    p.op("pe", "matmul", reads=[("sq", i), ("ones",)], writes=[pk], out=ps, lhsT=c.ones_bf[:, :],
                         rhs=s_[:, kk, :], start=(k == 0), stop=(k == NIC - 1))
                    p.op("dve", "tensor_scalar", reads=[("ld", i, 0), ("par",)], writes=[("gs", k, tb)],
                         out=gs[:, k, ts], in0=l[:, 0, kk, :], scalar1=gcol[:, k:k + 1], scalar2=None, op0=ALU.mult)
            r = rs[tb]
            p.op("act", "activation", reads=[pk], writes=[("rs", tb)], out=r[:, :], in_=ps, func=AF.Sqrt,
                 scale=1.0 / (NIC * 128), bias=EPS)
            p.op("dve", "reciprocal", reads=[("rs", tb)], writes=[("rs", tb)], out=r[:, :], in_=r[:, :])
        nt = 0
        for dc in range(KC):
            s = dc % 2
            for a in range(0, NIC * 128, 2048):
                c.load_w(wt[s][:, a:a + 2048], ("wo", s, a), wout_d[dc, :, a:a + 2048], 2048)
            wkeys = [("wo", s, a) for a in range(0, NIC * 128, 2048)]
            for k in range(NIC):
                for tb in range(c.NTB):
                    ps, pk = c.bank(2 * (dc % 4) + tb)
                    p.op("pe", "matmul", reads=wkeys + [("gs", k, tb)], writes=[pk], out=ps,
                         lhsT=wt[s][:, k * 128:(k + 1) * 128], rhs=gs[:, k, tb * 512:(tb + 1) * 512],
                         start=(k == 0), stop=(k == NIC - 1))
            for tb in range(c.NTB):
                ps, pk = c.bank(2 * (dc % 4) + tb)
                ts = slice(tb * 512, (tb + 1) * 512)
                t_ = tmpo[nt % 2]
                tk = ("tmpo", nt % 2)
                nt += 1
                p.op("dve", "tensor_tensor", reads=[pk, ("rs", tb)], writes=[tk], out=t_[:, :], in0=ps,
                     in1=rs[tb][:, :], op=ALU.mult)
                p.op("pool", "tensor_tensor", reads=[tk, ("x", dc, tb)], writes=[("x", dc, tb)], out=c.xT[:, dc, ts],
                     in0=c.xT[:, dc, ts], in1=t_[:, :], op=ALU.add)
        p.barrier()


POOL_W = (2, 4, 8, 16)


def emit_pool_out(c, up_d, icnt_d, wg_d, sc_d, wout_d):
    p = c.p
    T = c.T
    with contextlib.ExitStack() as st:
        mixT = p.sbuf("mixT", [128, KC, T], BF16, st)
        vT = p.sbuf("vT", [128, KC, T], BF16, st)
        icnt = p.sbuf("icnt", [128, 4 * T], F32, st)
        sc = p.sbuf("sc", [128, KC], F32, st)
        A = [[p.sbuf("A", [128, T + 16], F32, st) for _ in range(3)] for _ in range(2)]
        wgt = [p.sbuf("wgt", [128, 512], BF16, st) for _ in range(2)]
        wt = [p.sbuf("wpo", [128, 2048], BF16, st) for _ in range(3)]
        p.dma("sp", icnt[:, :], icnt_d, writes=[("icnt",)])
        p.dma("sp", sc[:, :], sc_d, writes=[("par",)])
        for cc in range(KC):
            gi = cc // 4
            b = cc % 2
            eng = "pool" if b == 0 else "dve"
            a0, a1, a2 = A[b]
            p.dma("sp", a0[:, :], up_d[cc], writes=[("A", b, 0)])
            p.op(eng, "tensor_tensor", reads=[("A", b, 0)], writes=[("A", b, 1)], out=a1[:, 1:T + 16],
                 in0=a0[:, 0:T + 15], in1=a0[:, 1:T + 16], op=ALU.add)
            cur, curk, oth, othk = a1, ("A", b, 1), a2, ("A", b, 2)
            lo, hi = 1, T + 16
            sh = 1
            for lvl in range(gi):
                nlo, nhi = lo + sh, hi - sh
                p.op(eng, "tensor_tensor", reads=[curk], writes=[othk], out=oth[:, nlo:nhi],
                     in0=cur[:, nlo - sh:nhi - sh], in1=cur[:, nlo + sh:nhi + sh], op=ALU.add)
                cur, curk, oth, othk = oth, othk, cur, curk
                lo, hi = nlo, nhi
                sh *= 2
            assert lo <= 8 and hi >= T + 8
            p.op(eng, "tensor_tensor", reads=[curk, ("icnt",)], writes=[othk], out=oth[:, 8:8 + T],
                 in0=cur[:, 8:8 + T], in1=icnt[:, gi * T:(gi + 1) * T], op=ALU.mult)
            for tb in range(c.NTB):
                p.op(eng, "tensor_tensor", reads=[othk, ("A", b, 0)], writes=[("mix", cc, tb)],
                     out=mixT[:, cc, tb * 512:(tb + 1) * 512], in0=oth[:, 8 + tb * 512:8 + (tb + 1) * 512],
                     in1=a0[:, 8 + tb * 512:8 + (tb + 1) * 512], op=ALU.subtract)
        for dc in range(KC):
            gi = dc // 4
            s = dc % 2
            c.load_w(wgt[s][:, :], ("wgt", s), wg_d[dc], 512)
            for kl in range(4):
                for tb in range(c.NTB):
                    ps, pk = c.bank(2 * (dc % 4) + tb)
                    p.op("pe", "matmul", reads=[("wgt", s), ("mix", gi * 4 + kl, tb)], writes=[pk],
                         out=ps, lhsT=wgt[s][:, kl * 128:(kl + 1) * 128],
                         rhs=mixT[:, gi * 4 + kl, tb * 512:(tb + 1) * 512], start=(kl == 0), stop=(kl == 3))
            for tb in range(c.NTB):
                ps, pk = c.bank(2 * (dc % 4) + tb)
                p.op("act", "activation", reads=[pk, ("par",)], writes=[("v", dc, tb)],
                     out=vT[:, dc, tb * 512:(tb + 1) * 512], in_=ps, func=AF.Identity, scale=sc[:, dc:dc + 1])
        c.nwt = 0
        emit_proj_resid(c, st, vT, "v", KC, wout_d, list(range(c.NTB)), 1.0, wt, "wpo")
        p.barrier()


def emit_final(c, gcol_d, out_d):
    p = c.p
    with contextlib.ExitStack() as st:
        oT = p.sbuf("oT", [128, KC, c.T], F32, st)
        gcol = p.sbuf("gcol", [128, KC], F32, st)
        p.dma("sp", gcol[:, :], gcol_d, writes=[("par",)])
        emit_rmsnorm(c, st, oT, gcol, hkey="o")
        for kc in range(KC):
            p.dma("sp", out_d[:, kc, :], oT[:, kc, :], reads=[("o", kc, tb) for tb in range(c.NTB)],
                  writes=[("final", kc)])
        p.barrier()


TT = 1024
_PROG_CACHE = {}


def build_tok_launch(stages):
    key = tuple(stages)
    if key in _PROG_CACHE:
        return _PROG_CACHE[key]
    p = Prog()
    c = Ctx(p, TT)
    ein = lambda n, sh, dt=F32: p.dram(n, sh, dt, kind="ExternalInput")
    eout = lambda n, sh, dt=F32: p.dram(n, sh, dt, kind="ExternalOutput")
    emit_load_x(c, ein("xin", [128, KC, TT]))
    has_final = False
    for i, stg in enumerate(stages):
        t = "_%d" % i
        if stg == "ffn":
            emit_ffn(c, ein("wg" + t, [FC, 128, 2048]), ein("wu" + t, [FC, 128, 2048]),
                     ein("wd" + t, [NQ, KC, 128, FQ * 128]), ein("gcol" + t, [128, KC]))
        elif stg == "ssd_in":
            emit_mix_in(c, ein("gcol" + t, [128, KC]), ein("w" + t, [81, 128, 2048]), eout("proj", [81, 128, TT]), 81, 32)
        elif stg == "pool_in":
            emit_mix_in(c, ein("gcol" + t, [128, KC]), ein("w" + t, [KC, 128, 2048]), eout("proj", [KC, 128, TT]), KC, 0)
        elif stg == "ssd_out":
            emit_ssd_out(c, ein("ym" + t, [NIC, 128, TT]), ein("yb" + t, [NIC, 128, TT]), ein("sz" + t, [NIC, 128, TT]),
                         ein("gn" + t, [128, NIC]), ein("wout" + t, [KC, 128, NIC * 128]))
        elif stg == "pool_out":
            emit_pool_out(c, ein("up" + t, [KC, 128, TT + 16]), ein("icnt" + t, [128, 4 * TT]),
                          ein("wgrp" + t, [KC, 128, 512]), ein("sc" + t, [128, KC]), ein("wout" + t, [KC, 128, 2048]))
        elif stg == "final":
            emit_final(c, ein("gcol" + t, [128, KC]), eout("final", [128, KC, TT]))
            has_final = True
    if not has_final:
        emit_store_x(c, eout("xout", [128, KC, TT]))
    nc = p.finish()
    _PROG_CACHE[key] = nc
    return nc


def build_ssd_core_launch():
    if "ssd_core" in _PROG_CACHE:
        return _PROG_CACHE["ssd_core"]
    p = Prog()
    ein = lambda n, sh, dt=F32: p.dram(n, sh, dt, kind="ExternalInput")
    eout = lambda n, sh, dt=F32: p.dram(n, sh, dt, kind="ExternalOutput")
    emit_ssd_core(p, ein("xbc", [NCC, 128, TS + 4]), ein("cw", [128, NCC * 5]), ein("cb", [128, NCC]),
                  ein("dtraw", [128, TS]), ein("dtpar", [128, 8]), ein("dbc", [128, NH]),
                  ein("cstf", [128, 128 + TS]), ein("cstb", [128, 384 + 64 * 128], BF16),
                  eout("ymain", [TS, NH * HD]), eout("yboff", [TS, NH * HD]))
    nc = p.finish()
    _PROG_CACHE["ssd_core"] = nc
    return nc


def _ssd_consts():
    import ml_dtypes
    identf = np.eye(128, dtype=np.float32)
    reset = np.ones((128, TS), np.float32)
    reset[:, ::128] = 0.0
    cstf = np.concatenate([identf, reset], axis=1)
    j = np.arange(128)[:, None]
    i = np.arange(128)[None, :]
    maskF = np.where(j > i, -MASKV, 0.0).astype(np.float32)
    maskB = np.where(j < i, MASKV, 0.0).astype(np.float32)
    sel = np.zeros((128, 64, 128), np.float32)
    for r in range(64):
        sel[r, r, :] = 1.0
    cstb = np.concatenate([identf, maskF, maskB, sel.reshape(128, 64 * 128)], axis=1).astype(ml_dtypes.bfloat16)
    return cstf, cstb


def _run(nc, in_maps):
    res = run_bass_kernel_spmd(nc, in_maps, core_ids=list(range(8)))
    return res.results


def _ffn_inputs(t, wg, wu, wd, g):
    return {"wg" + t: lay_w_stat(wg), "wu" + t: lay_w_stat(wu), "wd" + t: lay_wd(wd), "gcol" + t: lay_col(g)}


def kernel(x, ffn_norm, ffn_w_gate, ffn_w_up, ffn_w_down, mix_norm, ssd_w_in, ssd_conv_w, ssd_conv_b, ssd_dt_bias,
           ssd_a_log, ssd_d, ssd_norm, ssd_w_out, pool_w_in, pool_w_group, pool_scale, pool_w_out, final_norm):
    f = lambda a: np.ascontiguousarray(np.asarray(a, dtype=np.float32))
    x = f(x)
    B, L, _ = x.shape
    xs = [lay_xT(x[c // 2, (c % 2) * TT:(c % 2 + 1) * TT]) for c in range(8)]
    cstf, cstb = _ssd_consts()

    def ffn_in(t, l, i):
        return _ffn_inputs(t, f(ffn_w_gate[l, i]), f(ffn_w_up[l, i]), f(ffn_w_down[l, i]), f(ffn_norm[l, i]))

    def tok_launch(stages, shared, percore):
        nc = build_tok_launch(stages)
        in_maps = []
        for c in range(8):
            m = {"xin": xs[c]}
            m.update(shared)
            m.update(percore[c])
            in_maps.append(m)
        return _run(nc, in_maps)

    def ssd_core_stage(j, proj):
        nc = build_ssd_core_launch()
        cwf, cbf = f(ssd_conv_w[j]), f(ssd_conv_b[j])
        in_maps = []
        for c in range(8):
            b, hh = c // 2, c % 2
            P = np.concatenate([proj[2 * b].reshape(81 * 128, TT), proj[2 * b + 1].reshape(81 * 128, TT)], axis=1)
            rows = np.concatenate([np.arange(4096 + hh * 2048, 4096 + (hh + 1) * 2048),
                                   np.arange(8192 + hh * 512, 8192 + (hh + 1) * 512),
                                   np.arange(9216 + hh * 512, 9216 + (hh + 1) * 512)])
            xbc = np.pad(P[rows], ((0, 0), (2, 2))).reshape(NCC, 128, TS + 4)
            cch = rows - 4096
            cw = np.ascontiguousarray(cwf[:, cch].T.reshape(NCC, 128, 5).transpose(1, 0, 2)).reshape(128, NCC * 5)
            cb = np.ascontiguousarray(cbf[cch].reshape(NCC, 128).T)
            drow = np.concatenate([10240 + d * 64 + hh * 32 + np.arange(32) for d in range(2)])
            dtr = P[drow]
            par = np.zeros((64, 8), np.float32)
            par[:, 0] = f(ssd_dt_bias[j])[:, hh * 32:(hh + 1) * 32].reshape(64)
            par[:, 1] = f(ssd_a_log[j])[:, hh * 32:(hh + 1) * 32].reshape(64)
            par[:32, 2] = -1.0
            par[32:, 2] = 1.0
            par[32:, 3] = 1.0
            par[:32, 4] = 1.0
            dbc = np.ascontiguousarray(np.broadcast_to(f(ssd_d[j])[None, hh * 32:(hh + 1) * 32], (128, NH)))
            in_maps.append({"xbc": np.ascontiguousarray(xbc), "cw": cw, "cb": cb,
                            "dtraw": np.ascontiguousarray(np.concatenate([dtr, dtr], axis=0)),
                            "dtpar": np.concatenate([par, par], axis=0), "dbc": dbc, "cstf": cstf, "cstb": cstb})
        res = _run(nc, in_maps)
        ym, yb = [], []
        for c in range(8):
            b, half = c // 2, c % 2
            for name, lst in (("ymain", ym), ("yboff", yb)):
                Y = np.concatenate([res[2 * b][name], res[2 * b + 1][name]], axis=1)
                lst.append(np.ascontiguousarray(Y[half * TT:(half + 1) * TT].T.reshape(NIC, 128, TT)))
        return ym, yb

    def pool_inputs(t, j, proj):
        per = []
        for c in range(8):
            b, half = c // 2, c % 2
            U = np.concatenate([proj[2 * b].reshape(D, TT), proj[2 * b + 1].reshape(D, TT)], axis=1)
            Up = np.pad(U, ((0, 0), (8, 8)))
            up = np.ascontiguousarray(Up[:, half * TT:half * TT + TT + 16]).reshape(KC, 128, TT + 16)
            tg = half * TT + np.arange(TT)
            ic = np.stack([1.0 / (np.clip(tg + w // 2, 0, L) - np.clip(tg - w // 2, 0, L)) for w in POOL_W])
            icnt = np.ascontiguousarray(np.broadcast_to(ic.reshape(1, 4 * TT), (128, 4 * TT))).astype(np.float32)
            per.append({"up" + t: up, "icnt" + t: icnt})
        shared = {"wgrp" + t: np.concatenate([lay_w_stat(f(pool_w_group[j, gi])) for gi in range(4)], axis=0),
                  "sc" + t: lay_col(f(pool_scale[j])), "wout" + t: lay_w_stat(f(pool_w_out[j]))}
        return shared, per

    empty = [dict() for _ in range(8)]
    sh = ffn_in("_0", 0, 0)
    sh.update({"gcol_1": lay_col(f(mix_norm[0])), "w_1": lay_w_stat(f(ssd_w_in[0]))})
    r = tok_launch(("ffn", "ssd_in"), sh, empty)
    xs = [r[c]["xout"] for c in range(8)]
    proj = [r[c]["proj"] for c in range(8)]
    for j in range(2):
        l = 2 * j
        ym, yb = ssd_core_stage(j, proj)
        sh = {"gn_0": lay_col(f(ssd_norm[j])), "wout_0": lay_w_stat(f(ssd_w_out[j]))}
        sh.update(ffn_in("_1", l, 1))
        sh.update(ffn_in("_2", l + 1, 0))
        sh.update({"gcol_3": lay_col(f(mix_norm[l + 1])), "w_3": lay_w_stat(f(pool_w_in[j]))})
        per = [{"ym_0": ym[c], "yb_0": yb[c], "sz_0": np.ascontiguousarray(proj[c][0:NIC])} for c in range(8)]
        r = tok_launch(("ssd_out", "ffn", "ffn", "pool_in"), sh, per)
        xs = [r[c]["xout"] for c in range(8)]
        proj = [r[c]["proj"] for c in range(8)]
        sh, per = pool_inputs("_0", j, proj)
        sh.update(ffn_in("_1", l + 1, 1))
        if j == 0:
            sh.update(ffn_in("_2", l + 2, 0))
            sh.update({"gcol_3": lay_col(f(mix_norm[l + 2])), "w_3": lay_w_stat(f(ssd_w_in[1]))})
            r = tok_launch(("pool_out", "ffn", "ffn", "ssd_in"), sh, per)
            xs = [r[c]["xout"] for c in range(8)]
            proj = [r[c]["proj"] for c in range(8)]
        else:
            sh.update({"gcol_2": lay_col(f(final_norm))})
            r = tok_launch(("pool_out", "ffn", "final"), sh, per)
    out = np.empty((B, L, D), np.float32)
    for c in range(8):
        out[c // 2, (c % 2) * TT:(c % 2 + 1) * TT] = unlay_xT(r[c]["final"])
    return out
```

```python
import contextlib
import numpy as np
import concourse.bass as bass
import concourse.mybir as mybir
from concourse.bass_utils import run_bass_kernel_spmd

F32 = mybir.dt.float32
BF16 = mybir.dt.bfloat16
ALU = mybir.AluOpType
AF = mybir.ActivationFunctionType

D = 2048
KC = D // 128
DFF = 5632
FC = DFF // 128
NQ = 4
FQ = FC // NQ
EPS = 1e-6
NDMASEM = 24


class Prog:
    ENG = ("pe", "act", "dve", "pool", "sp")

    def __init__(self):
        self.nc = bass.Bass("TRN2", target_bir_lowering=False)
        self.es = contextlib.ExitStack()
        self.ops = {e: [] for e in self.ENG}
        self.cnt = {e: 0 for e in self.ENG}
        self.sem = {e: self.es.enter_context(self.nc.semaphore("s_" + e)) for e in self.ENG}
        self.dsem = [self.es.enter_context(self.nc.semaphore("d%d" % i)) for i in range(NDMASEM)]
        self.ndma = 0
        self.dma_last = [None] * NDMASEM
        self.waited = {e: {} for e in self.ENG}
        self.last_w = {}
        self.readers = {}
        self.all_tokens = []
        self.uid = 0

    def sbuf(self, name, shape, dtype, stack=None):
        self.uid += 1
        t = (stack or self.es).enter_context(self.nc.sbuf_tensor("%s_%d" % (name, self.uid), list(shape), dtype))
        return t

    def psum(self, name, shape, dtype, stack=None):
        self.uid += 1
        return (stack or self.es).enter_context(self.nc.psum_tensor("%s_%d" % (name, self.uid), list(shape), dtype))

    def dram(self, name, shape, dtype, kind="Internal"):
        return self.nc.dram_tensor(name, list(shape), dtype, kind=kind).ap()

    def _deps(self, reads, writes):
        deps = []
        for k in reads:
            if k in self.last_w:
                deps.append(self.last_w[k])
        for k in writes:
            if k in self.last_w:
                deps.append(self.last_w[k])
            deps.extend(self.readers.get(k, ()))
        return deps

    def _commit(self, tok, reads, writes):
        for k in reads:
            self.readers.setdefault(k, []).append(tok)
        for k in writes:
            self.last_w[k] = tok
            self.readers[k] = []

    def _waits(self, eng, deps, pe_skip=False):
        w = self.waited[eng]
        best = {}
        for (s, v, src) in deps:
            if pe_skip and src == "pe":
                continue
            if w.get(id(s), 0) >= v:
                continue
            if best.get(id(s), (None, 0))[1] < v:
                best[id(s)] = (s, v)
        out = []
        for sid, (s, v) in best.items():
            w[sid] = v
            out.append((s, v))
        return out

    def op(self, eng, meth, reads=(), writes=(), **kw):
        fn = (lambda e, meth=meth, kw=kw: getattr(e, meth)(**kw))
        deps = self._deps(reads, writes)
        waits = self._waits(eng, deps, pe_skip=(eng == "pe"))
        self.cnt[eng] += 1
        tok = (self.sem[eng], self.cnt[eng], eng)
        self.ops[eng].append((waits, fn, (self.sem[eng], 1)))
        self._commit(tok, reads, writes)
        return tok

    def dma(self, eng, out, in_, reads=(), writes=(), **kw):
        deps = self._deps(reads, writes)
        i = self.ndma % NDMASEM
        val = 16 * (self.ndma // NDMASEM + 1)
        self.ndma += 1
        if self.dma_last[i] is not None:
            deps.append(self.dma_last[i])
        waits = self._waits(eng, deps)
        tok = (self.dsem[i], val, "dma")
        self.dma_last[i] = tok
        self.ops[eng].append((waits, lambda e, out=out, in_=in_, kw=kw: e.dma_start(out=out, in_=in_, **kw),
                              (self.dsem[i], 16)))
        self._commit(tok, reads, writes)
        self.all_tokens.append(tok)
        return tok

    def barrier(self):
        toks = [(self.sem[e], self.cnt[e], e) for e in self.ENG if self.cnt[e] > 0]
        toks += [t for t in self.dma_last if t is not None]
        for e in self.ENG:
            waits = self._waits(e, toks)
            if waits:
                self.ops[e].append((waits, None, None))
        self.last_w.clear()
        self.readers.clear()
        print("[prog] counts", self.cnt, "ndma", self.ndma, flush=True)

    def finish(self):
        self.barrier()
        with self.nc.Block() as block:
            def runner(name):
                def body(e):
                    for waits, fn, inc in self.ops[name]:
                        for (s, v) in waits:
                            e.wait_ge(s, v)
                        if fn is not None:
                            ins = fn(e)
                            ins.then_inc(inc[0], inc[1])
                return body
            block.tensor(runner("pe"))
            block.scalar(runner("act"))
            block.vector(runner("dve"))
            block.gpsimd(runner("pool"))
            block.sync(runner("sp"))
        self.es.close()
        return self.nc


class Ctx:
    def __init__(self, p, T):
        self.p = p
        self.T = T
        self.NTB = T // 512
        self.xT = p.sbuf("xT", [128, KC, T], F32)
        self.ones_bf = p.sbuf("ones_bf", [128, 128], BF16)
        p.op("dve", "memset", writes=[("ones",)], ap=self.ones_bf[:, :], constant=1.0)
        self.banks = [p.psum("bank%d" % i, [128, 512], F32) for i in range(8)]
        self.NSTG = 3
        import os
        self.two_q = os.environ.get("TWOQ", "0") == "1"
        self.stg = [p.sbuf("stg", [128, 2048], F32) for _ in range(self.NSTG)]
        self.nstg = 0

    def bank(self, i):
        return self.banks[i][:, :], ("ps", i)

    def load_w(self, dst, dkey, src, ncols):
        p = self.p
        i = self.nstg % self.NSTG
        self.nstg += 1
        q = "sp" if (self.nstg % 2 == 0 or not self.two_q) else "act"
        p.dma(q, self.stg[i][:, 0:ncols], src, writes=[("stg", i)])
        p.op("pool", "tensor_copy", reads=[("stg", i)], writes=[dkey], out=dst, in_=self.stg[i][:, 0:ncols])


def emit_load_x(c, x_dram):
    p = c.p
    for kc in range(KC):
        p.dma("sp", c.xT[:, kc, :], x_dram[:, kc, :], writes=[("x", kc, tb) for tb in range(c.NTB)])


def emit_store_x(c, x_dram):
    p = c.p
    for kc in range(KC):
        p.dma("sp", x_dram[:, kc, :], c.xT[:, kc, :], reads=[("x", kc, tb) for tb in range(c.NTB)],
              writes=[("xout", kc)])


def emit_rmsnorm(c, st, hT, gcol, hkey="h"):
    p = c.p
    sq = [p.sbuf("sq", [128, 512], BF16, st) for _ in range(2)]
    rs = [p.sbuf("rs", [128, 512], F32, st) for _ in range(2)]
    for tb in range(c.NTB):
        ts = slice(tb * 512, (tb + 1) * 512)
        ps, pk = c.bank(tb)
        for kc in range(KC):
            s = sq[kc % 2]
            p.op("act", "activation", reads=[("x", kc, tb)], writes=[("sq", kc % 2)],
                 out=s[:, :], in_=c.xT[:, kc, ts], func=AF.Square)
            p.op("pe", "matmul", reads=[("sq", kc % 2), ("ones",)], writes=[pk],
                 out=ps, lhsT=c.ones_bf[:, :], rhs=s[:, :], start=(kc == 0), stop=(kc == KC - 1))
        r = rs[tb % 2]
        p.op("act", "activation", reads=[pk], writes=[("rs", tb % 2)],
             out=r[:, :], in_=ps, func=AF.Sqrt, scale=1.0 / D, bias=EPS)
        p.op("dve", "reciprocal", reads=[("rs", tb % 2)], writes=[("rs", tb % 2)], out=r[:, :], in_=r[:, :])
        for kc in range(KC):
            p.op("dve", "scalar_tensor_tensor", reads=[("x", kc, tb), ("rs", tb % 2), ("par",)],
                 writes=[(hkey, kc, tb)],
                 out=hT[:, kc, ts], in0=c.xT[:, kc, ts], scalar=gcol[:, kc:kc + 1], in1=r[:, :],
                 op0=ALU.mult, op1=ALU.mult)


def emit_ffn(c, wg_d, wu_d, wd_d, gcol_d):
    p = c.p
    T = c.T
    with contextlib.ExitStack() as st:
        hT = p.sbuf("hT", [128, KC, T], BF16, st)
        actq = p.sbuf("actq", [128, FQ, T], BF16, st)
        NS = 3
        wg = [p.sbuf("wg", [128, 2048], BF16, st) for _ in range(NS)]
        wu = [p.sbuf("wu", [128, 2048], BF16, st) for _ in range(NS)]
        NSD = 4
        wd = [p.sbuf("wd", [128, FQ * 128], BF16, st) for _ in range(NSD)]
        sg = [p.sbuf("sg", [128, 512], F32, st) for _ in range(2)]
        gcol = p.sbuf("gcol", [128, KC], F32, st)
        p.dma("sp", gcol[:, :], gcol_d, writes=[("par",)])
        emit_rmsnorm(c, st, hT, gcol)
        nsg = 0
        ndc = 0
        import os
        dbg = int(os.environ.get("DBG", "9"))
        for q in range(NQ if dbg >= 2 else 0):
            for fl in range(FQ):
                fc = q * FQ + fl
                s = fc % NS
                c.load_w(wg[s][:, :], ("wg", s), wg_d[fc], 2048)
                c.load_w(wu[s][:, :], ("wu", s), wu_d[fc], 2048)
                base = 4 * (fc % 2)
                for kc in range(KC):
                    for tb in range(c.NTB):
                        ps, pk = c.bank(base + tb)
                        p.op("pe", "matmul", reads=[("wg", s), ("h", kc, tb)], writes=[pk],
                             out=ps, lhsT=wg[s][:, kc * 128:(kc + 1) * 128],
                             rhs=hT[:, kc, tb * 512:(tb + 1) * 512], start=(kc == 0), stop=(kc == KC - 1))
                    for tb in range(c.NTB):
                        ps, pk = c.bank(base + 2 + tb)
                        p.op("pe", "matmul", reads=[("wu", s), ("h", kc, tb)], writes=[pk],
                             out=ps, lhsT=wu[s][:, kc * 128:(kc + 1) * 128],
                             rhs=hT[:, kc, tb * 512:(tb + 1) * 512], start=(kc == 0), stop=(kc == KC - 1))
                for tb in range(c.NTB):
                    gps, gk = c.bank(base + tb)
                    ups, uk = c.bank(base + 2 + tb)
                    sgt = sg[nsg % 2]
                    sk = ("sg", nsg % 2)
                    nsg += 1
                    p.op("act", "activation", reads=[gk], writes=[sk], out=sgt[:, :], in_=gps, func=AF.Silu)
                    p.op("dve", "tensor_tensor", reads=[sk, uk], writes=[("actq", fl, tb)],
                         out=actq[:, fl, tb * 512:(tb + 1) * 512], in0=sgt[:, :], in1=ups, op=ALU.mult)
            for dc in range(KC if dbg >= 3 else 0):
                s = ndc % NSD
                c.load_w(wd[s][:, :], ("wd", s), wd_d[q, dc], FQ * 128)
                base = 2 * (ndc % 4)
                ndc += 1
                for fl in range(FQ):
                    for tb in range(c.NTB):
                        ps, pk = c.bank(base + tb)
                        p.op("pe", "matmul", reads=[("wd", s), ("actq", fl, tb)], writes=[pk],
                             out=ps, lhsT=wd[s][:, fl * 128:(fl + 1) * 128],
                             rhs=actq[:, fl, tb * 512:(tb + 1) * 512], start=(fl == 0), stop=(fl == FQ - 1))
                for tb in range(c.NTB):
                    ps, pk = c.bank(base + tb)
                    ts = slice(tb * 512, (tb + 1) * 512)
                    p.op("dve", "scalar_tensor_tensor", reads=[pk, ("x", dc, tb)], writes=[("x", dc, tb)],
                         out=c.xT[:, dc, ts], in0=ps, scalar=0.5, in1=c.xT[:, dc, ts],
                         op0=ALU.mult, op1=ALU.add)
        p.barrier()


def lay_w_stat(w):
    K, M = w.shape
    return np.ascontiguousarray(w.reshape(K // 128, 128, M // 128, 128).transpose(2, 1, 0, 3)).reshape(
        M // 128, 128, K)


def lay_wd(w):
    return np.ascontiguousarray(w.reshape(NQ, FQ, 128, KC, 128).transpose(0, 3, 2, 1, 4)).reshape(
        NQ, KC, 128, FQ * 128)


def lay_col(v):
    return np.ascontiguousarray(v.reshape(-1, 128).T)


def lay_xT(xtok):
    T = xtok.shape[0]
    return np.ascontiguousarray(xtok.T.reshape(KC, 128, T).transpose(1, 0, 2))


def unlay_xT(xT):
    T = xT.shape[2]
    return np.ascontiguousarray(xT.transpose(1, 0, 2).reshape(D, T).T)


TS = 2048
NCH = TS // 128
NH = 32
NG = 4
HPG = 8
HD = 64
NST = 128
NXC = NH * HD // 128
NCC = NXC + 2 * NG
MASKV = 30000.0


def emit_ssd_core(p, xbc_d, cw_d, cb_d, dtraw_d, dtpar_d, dbc_d, cstf_d, cstb_d, ymain_d, yboff_d):
    es = contextlib.ExitStack()
    with es:
        cstb = p.sbuf("cstb", [128, 384], BF16, es)
        SEL = p.sbuf("SEL", [128, 64 * 128], BF16, es)
        onesb = p.sbuf("onesb", [128, 128], BF16, es)
        dtpar = p.sbuf("dtpar", [128, 8], F32, es)
        dbc = p.sbuf("dbc", [128, NH], F32, es)
        cw = p.sbuf("cw", [128, NCC * 5], F32, es)
        cbias = p.sbuf("cbias", [128, NCC], F32, es)
        Rhl = p.sbuf("Rhl", [128, 2, TS], BF16, es)
        tokA = p.sbuf("tokA", [128, NCH, 128], F32, es)
        tokB = p.sbuf("tokB", [128, NCH, 128], F32, es)
        decbc = p.sbuf("decbc", [128, NCH, 64], F32, es)
        p.dma("sp", cstb[:, :], cstb_d[:, 0:384], writes=[("cstb",)])
        p.dma("sp", SEL[:, :], cstb_d[:, 384:384 + 64 * 128], writes=[("SEL",)])
        p.dma("sp", dtpar[:, :], dtpar_d, writes=[("dtpar",)])
        p.dma("sp", dbc[:, :], dbc_d, writes=[("dbc",)])
        p.dma("sp", cw[:, :], cw_d, writes=[("cw",)])
        p.dma("sp", cbias[:, :], cb_d, writes=[("cbias",)])
        p.op("dve", "memset", writes=[("onesb",)], ap=onesb[:, :], constant=1.0)
        identb = cstb[:, 0:128]
        maskF = cstb[:, 128:256]
        maskB = cstb[:, 256:384]

        with contextlib.ExitStack() as sa:
            reset = p.sbuf("reset", [128, TS], F32, sa)
            Rt = p.sbuf("Rt", [128, TS], F32, sa)
            hl = p.sbuf("hl", [128, 3, 2, TS], BF16, sa)
            dtr = p.sbuf("dtr", [128, TS], F32, sa)
            dt = p.sbuf("dt", [128, TS], F32, sa)
            dta = p.sbuf("dta", [128, TS], F32, sa)
            cs = p.sbuf("cs", [128, TS], F32, sa)
            Qt = p.sbuf("Qt", [128, TS], F32, sa)
            eR = p.sbuf("eR", [128, TS], F32, sa)
            eQ = p.sbuf("eQ", [128, TS], F32, sa)
            QA = p.sbuf("QA", [128, TS], F32, sa)
            QB = p.sbuf("QB", [128, TS], F32, sa)
            acol = p.sbuf("acol", [128, 2], F32, sa)
            psA = [p.psum("psA", [128, 512], F32, sa) for _ in range(2)]
            psB = [p.psum("psB", [128, 512], F32, sa) for _ in range(2)]
            p.dma("sp", reset[:, :], cstf_d[:, 128:128 + TS], writes=[("reset",)])
            p.dma("sp", dtr[:, :], dtraw_d, writes=[("dtr",)])
            p.op("act", "activation", reads=[("dtpar",)], writes=[("acol",)], out=acol[:, 0:1], in_=dtpar[:, 1:2],
                 func=AF.Exp)
            p.op("dve", "tensor_scalar", reads=[("acol",)], writes=[("acol2",)], out=acol[:, 1:2], in0=acol[:, 0:1],
                 scalar1=-1.0, scalar2=None, op0=ALU.mult)
            p.op("act", "activation", reads=[("dtr",), ("dtpar",)], writes=[("dt",)], out=dt[:, :], in_=dtr[:, :],
                 func=AF.Exp, bias=dtpar[:, 0:1], scale=1.0)
            p.op("act", "activation", reads=[("dt",)], writes=[("dt",)], out=dt[:, :], in_=dt[:, :],
                 func=AF.Ln, bias=1.0, scale=1.0)
            import os
            sa_lvl = int(os.environ.get("SSD_A", "9"))
            if sa_lvl < 1:
                p.barrier(); return
            p.op("dve", "tensor_scalar", reads=[("dt",), ("acol2",)], writes=[("dta",)], out=dta[:, :], in0=dt[:, :],
                 scalar1=acol[:, 1:2], scalar2=None, op0=ALU.mult)
            p.op("dve", "tensor_tensor_scan", reads=[("reset",), ("dta",)], writes=[("cs",)], out=cs[:, :],
                 data0=reset[:, :], data1=dta[:, :], initial=0.0, op0=ALU.mult, op1=ALU.add)
            if sa_lvl < 2:
                p.barrier(); return
            p.op("dve", "tensor_scalar", reads=[("dta",), ("dtpar",)], writes=[("Qt",)], out=Qt[:, :], in0=dta[:, :],
                 scalar1=dtpar[:, 3:4], scalar2=None, op0=ALU.mult)
            p.op("dve", "tensor_tensor", reads=[("cs",), ("Qt",)], writes=[("R",)], out=Rt[:, :], in0=cs[:, :],
                 in1=Qt[:, :], op=ALU.subtract)
            tot_bc = cs[:, :].rearrange("p (c q) -> p c q", q=128)[:, :, 127:128].to_broadcast([128, NCH, 128])
            p.op("dve", "tensor_tensor", reads=[("cs",), ("R",)], writes=[("Qt",)],
                 out=Qt[:, :].rearrange("p (c q) -> p c q", q=128), in0=tot_bc,
                 in1=Rt[:, :].rearrange("p (c q) -> p c q", q=128), op=ALU.subtract)
            p.op("act", "activation", reads=[("R",)], writes=[("eR",)], out=eR[:, :], in_=Rt[:, :], func=AF.Exp)
            p.op("act", "activation", reads=[("Qt",)], writes=[("eQ",)], out=eQ[:, :], in_=Qt[:, :], func=AF.Exp)
            p.op("dve", "tensor_copy", reads=[("dt",)], writes=[("QA0",)], out=QA[0:64, :], in_=dt[0:64, :])
            p.op("dve", "tensor_scalar", reads=[("R",), ("dtpar",)], writes=[("QA1",)], out=QA[64:128, :],
                 in0=Rt[64:128, :], scalar1=dtpar[64:128, 2:3], scalar2=None, op0=ALU.mult)
            p.op("dve", "tensor_scalar", reads=[("eQ",), ("dtpar",)], writes=[("QB0",)], out=QB[0:64, :],
                 in0=eQ[0:64, :], scalar1=dtpar[0:64, 4:5], scalar2=None, op0=ALU.mult)
            p.op("dve", "scalar_tensor_tensor", reads=[("eR",), ("QB0",), ("dtpar",)], writes=[("QB0",)],
                 out=QB[0:64, :], in0=eR[0:64, :], scalar=dtpar[0:64, 3:4], in1=QB[0:64, :],
                 op0=ALU.mult, op1=ALU.add)
            p.op("dve", "tensor_tensor", reads=[("QB0",), ("dt",)], writes=[("QB0",)], out=QB[0:64, :],
                 in0=QB[0:64, :], in1=dt[0:64, :], op=ALU.mult)
            p.op("dve", "tensor_scalar", reads=[("eR",), ("dtpar",)], writes=[("QB1",)], out=QB[64:128, :],
                 in0=eR[64:128, :], scalar1=dtpar[64:128, 4:5], scalar2=None, op0=ALU.mult)
            p.op("dve", "scalar_tensor_tensor", reads=[("eQ",), ("QB1",), ("dtpar",)], writes=[("QB1",)],
                 out=QB[64:128, :], in0=eQ[64:128, :], scalar=dtpar[64:128, 3:4], in1=QB[64:128, :],
                 op0=ALU.mult, op1=ALU.add)
            if sa_lvl < 3:
                p.barrier(); return
            def split(src, skeys, dst_hi, dst_lo, dkey, scratch, sckey):
                p.op("act", "activation", reads=skeys, writes=[(dkey, 0)], out=dst_hi, in_=src, func=AF.Identity)
                p.op("dve", "tensor_tensor", reads=skeys + [(dkey, 0)], writes=[sckey], out=scratch, in0=src,
                     in1=dst_hi, op=ALU.subtract)
                p.op("act", "activation", reads=[sckey], writes=[(dkey, 1)], out=dst_lo, in_=scratch, func=AF.Identity)
            split(QA[:, :], [("QA0",), ("QA1",)], hl[:, 0, 0, :], hl[:, 0, 1, :], "hlA", eR[:, :], ("eR",))
            split(QB[:, :], [("QB0",), ("QB1",), ("eR",)], hl[:, 1, 0, :], hl[:, 1, 1, :], "hlB", eQ[:, :], ("eQ",))
            split(Rt[:, :], [("R",), ("eQ",)], Rhl[:, 0, :], Rhl[:, 1, :], "Rhl", Qt[:, :], ("Qt",))
            identb_ = cstb[:, 0:128]
            if sa_lvl < 4:
                p.barrier(); return
            Dm = dt[:, 0:NCH * 64]
            Dm3 = Dm.rearrange("p (c r) -> p c r", r=64)
            tot3 = cs[:, :].rearrange("p (c q) -> p c q", q=128)[:, :, 127:128].to_broadcast([128, NCH, 64])
            id3 = cstb[:, 0:64].rearrange("p (o r) -> p o r", o=1).to_broadcast([128, NCH, 64])
            p.op("dve", "tensor_tensor", reads=[("cs",), ("cstb",), ("hlA", 0), ("hlA", 1), ("hlB", 0), ("hlB", 1)],
                 writes=[("Dm",)], out=Dm3, in0=tot3, in1=id3, op=ALU.mult)
            if sa_lvl < 5:
                p.barrier(); return
            Dhl = hl[:, 2, :, 0:NCH * 64]
            split(Dm, [("Dm",), ("Qt",), ("Rhl", 1)], Dhl[:, 0, :], Dhl[:, 1, :], "Dhl", Qt[:, 0:NCH * 64], ("Qt",))
            if sa_lvl < 6:
                p.barrier(); return
            for half in range(2):
                pb = psB[half]
                for w in range(2):
                    p.op("pe", "matmul", reads=[("Dhl", w), ("onesb",)], writes=[("psB", half)], out=pb[:, :],
                         lhsT=onesb[:, :], rhs=Dhl[:, w, half * 512:(half + 1) * 512], start=(w == 0), stop=(w == 1))
                p.op("act", "activation", reads=[("psB", half)], writes=[("decbc", half)],
                     out=decbc[:, half * 8:(half + 1) * 8, :],
                     in_=pb[:, :].rearrange("p (c r) -> p c r", r=64), func=AF.Exp)
            if sa_lvl < 7:
                p.barrier(); return
            for c in range(NCH):
                cs_ = slice(c * 128, (c + 1) * 128)
                pa = psA[c % 2]
                pak = ("psA", c % 2)
                for q in range(2):
                    for w in range(2):
                        p.op("pe", "matmul", reads=[("hlA", w), ("hlB", w), ("cstb",)], writes=[pak],
                             out=pa[:, q * 128:(q + 1) * 128], lhsT=hl[:, q, w, cs_], rhs=identb_,
                             start=(w == 0), stop=(w == 1))
                p.op("act", "activation", reads=[pak], writes=[("tokA", c)], out=tokA[:, c, :], in_=pa[:, 0:128],
                     func=AF.Identity)
                p.op("act", "activation", reads=[pak], writes=[("tokB", c)], out=tokB[:, c, :], in_=pa[:, 128:256],
                     func=AF.Identity)
            p.barrier()
        import os
        if int(os.environ.get("SSD_DBG", "9")) < 2:
            return
        _emit_ssd_main(p, es, xbc_d, ymain_d, yboff_d, SEL, identb, maskF, maskB, dbc, cw, cbias, Rhl, tokA, tokB,
                       decbc)
        p.barrier()


def _emit_ssd_main(p, es, xbc_d, ymain_d, yboff_d, SEL, identb, maskF, maskB, dbc, cw, cbias, Rhl, tokA, tokB,
                   decbc):
    x_tok = p.sbuf("x_tok", [128, NCH, NH * HD], BF16, es)
    B_tok = p.sbuf("B_tok", [128, NCH, NG * NST], BF16, es)
    BT = p.sbuf("BT", [128, NG, TS], BF16, es)
    CT = p.sbuf("CT", [128, NG, TS], BF16, es)
    with contextlib.ExitStack() as sb:
        u = [p.sbuf("u", [128, TS + 4], F32, sb) for _ in range(2)]
        acc = [p.sbuf("acc", [128, TS], F32, sb) for _ in range(2)]
        fm = [p.sbuf("fm", [128, TS], BF16, sb) for _ in range(2)]
        pst = [p.psum("pst", [128, 1024], F32, sb) for _ in range(2)]
        npst = 0
        for cc in range(NCC):
            b = cc % 2
            p.dma("sp", u[b][:, :], xbc_d[cc], writes=[("u", b)])
            p.op("dve", "tensor_scalar", reads=[("u", b), ("cw",), ("cbias",)], writes=[("acc", b)],
                 out=acc[b][:, :], in0=u[b][:, 0:TS], scalar1=cw[:, cc * 5:cc * 5 + 1],
                 scalar2=cbias[:, cc:cc + 1], op0=ALU.mult, op1=ALU.add)
            for j in range(1, 5):
                p.op("dve", "scalar_tensor_tensor", reads=[("u", b), ("cw",), ("acc", b)], writes=[("acc", b)],
                     out=acc[b][:, :], in0=u[b][:, j:j + TS], scalar=cw[:, cc * 5 + j:cc * 5 + j + 1],
                     in1=acc[b][:, :], op0=ALU.mult, op1=ALU.add)
            if cc < NXC:
                dst, dk = fm[b][:, :], ("fm", b)
            elif cc < NXC + NG:
                dst, dk = BT[:, cc - NXC, :], ("BT", cc - NXC)
            else:
                dst, dk = CT[:, cc - NXC - NG, :], ("CT", cc - NXC - NG)
            p.op("act", "activation", reads=[("acc", b)], writes=[dk], out=dst, in_=acc[b][:, :], func=AF.Silu)
            if cc < NXC + NG:
                for c4 in range(NCH // 8):
                    pt = pst[npst % 2]
                    pk = ("pst", npst % 2)
                    npst += 1
                    for k in range(8):
                        c = c4 * 8 + k
                        p.op("pe", "matmul", reads=[dk, ("cstb",)], writes=[pk],
                             out=pt[:, k * 128:(k + 1) * 128], lhsT=dst[:, c * 128:(c + 1) * 128], rhs=identb,
                             start=True, stop=True)
                    if cc < NXC:
                        o = x_tok[:, c4 * 8:(c4 + 1) * 8, cc * 128:(cc + 1) * 128]
                        ok = [("x_tok", c) for c in range(c4 * 8, c4 * 8 + 8)]
                    else:
                        g = cc - NXC
                        o = B_tok[:, c4 * 8:(c4 + 1) * 8, g * 128:(g + 1) * 128]
                        ok = [("B_tok", c) for c in range(c4 * 8, c4 * 8 + 8)]
                    p.op("pool" if False else "act", "activation", reads=[pk], writes=ok, out=o,
                         in_=pt[:, :].rearrange("p (k q) -> p k q", q=128), func=AF.Identity)
        p.barrier()

    import os
    sdbg = int(os.environ.get("SSD_DBG", "9"))
    if sdbg < 3:
        return
    with contextlib.ExitStack() as sc:
        H = p.sbuf("H", [128, NG, 512], F32, sc)
        Hb = p.sbuf("Hb", [128, NG, 512], BF16, sc)
        xw = [p.sbuf("xw", [128, NH * HD], BF16, sc) for _ in range(1)] * 2
        yt = [p.sbuf("yt", [128, NH * HD], BF16, sc) for _ in range(2)]
        tmp = [p.sbuf("tmp", [128, 512], F32, sc) for _ in range(2)]
        tmp2 = [p.sbuf("tmp2", [128, 512], F32, sc) for _ in range(2)]
        Lt = [p.sbuf("Lt", [128, 128], BF16, sc) for _ in range(4)]
        Mt = [p.sbuf("Mt", [128, 128], BF16, sc) for _ in range(4)]
        cbt = [p.sbuf("cbt", [128, NG * 128], BF16, sc) for _ in range(2)]
        ps_seg = [p.psum("ps_seg", [128, 512], F32, sc) for _ in range(2)]
        ps_cb = p.psum("ps_cb", [128, 512], F32, sc)
        ps_y = [p.psum("ps_y", [128, 512], F32, sc) for _ in range(2)]
        ps_g = [p.psum("ps_g", [128, 512], F32, sc) for _ in range(2)]
        ps_s = p.psum("ps_s", [128, 512], F32, sc)
        cnt = {"g": 0, "t": 0, "l": 0}

        def hview(g):
            return H[:, g, :].rearrange("p (h d) -> p h d", d=HD)

        def state_update(c, d, nxw):
            xwt = xw[nxw % 2]
            xk = ("xw", 0)
            wst = tokB[:, c, d * NH:(d + 1) * NH].rearrange("p (h o) -> p h o", o=1).to_broadcast([128, NH, HD])
            p.op("pool", "tensor_tensor", reads=[("x_tok", c), ("tokB", c)], writes=[xk],
                 out=xwt[:, :].rearrange("p (h d) -> p h d", d=HD),
                 in0=x_tok[:, c, :].rearrange("p (h d) -> p h d", d=HD), in1=wst, op=ALU.mult)
            for g in range(NG):
                p.op("pe", "matmul", reads=[("B_tok", c), xk], writes=[("ps_s",)], out=ps_s[:, :],
                     lhsT=B_tok[:, c, g * 128:(g + 1) * 128], rhs=xwt[:, g * 512:(g + 1) * 512],
                     start=True, stop=True)
                dec = decbc[:, c, d * NH + g * HPG:d * NH + (g + 1) * HPG].rearrange(
                    "p (h o) -> p h o", o=1).to_broadcast([128, HPG, HD])
                p.op("pool", "tensor_tensor", reads=[("H", g), ("decbc", c // 8)], writes=[("H", g)], out=hview(g),
                     in0=hview(g), in1=dec, op=ALU.mult)
                p.op("dve", "tensor_tensor", reads=[("H", g), ("ps_s",)], writes=[("H", g)], out=H[:, g, :],
                     in0=H[:, g, :], in1=ps_s[:, :], op=ALU.add)
                p.op("act", "activation", reads=[("H", g)], writes=[("Hb", g)], out=Hb[:, g, :], in_=H[:, g, :],
                     func=AF.Identity)

        def g_term(c, d, g):
            pg = ps_g[cnt["g"] % 2]
            gk = ("ps_g", cnt["g"] % 2)
            cnt["g"] += 1
            p.op("pe", "matmul", reads=[("CT", g), ("Hb", g)], writes=[gk], out=pg[:, :],
                 lhsT=CT[:, g, c * 128:(c + 1) * 128], rhs=Hb[:, g, :], start=True, stop=True)
            t = tmp[cnt["t"] % 2]
            tk = ("tmp", cnt["t"] % 2)
            cnt["t"] += 1
            eo = tokB[:, c, 64 + d * NH + g * HPG:64 + d * NH + (g + 1) * HPG].rearrange(
                "p (h o) -> p h o", o=1).to_broadcast([128, HPG, HD])
            p.op("dve", "tensor_tensor", reads=[gk, ("tokB", c)], writes=[tk],
                 out=t[:, :].rearrange("p (h d) -> p h d", d=HD),
                 in0=pg[:, :].rearrange("p (h d) -> p h d", d=HD), in1=eo, op=ALU.mult)
            return t, tk

        for g in range(NG):
            p.op("pool", "memset", writes=[("H", g)], ap=H[:, g, :], constant=0.0)
            p.op("pool", "memset", writes=[("Hb", g)], ap=Hb[:, g, :], constant=0.0)
        nxw = 0
        for c in range(NCH - 1, -1, -1):
            ytile = yt[c % 2]
            for g in range(NG):
                t, tk = g_term(c, 1, g)
                p.op("pool", "tensor_copy", reads=[tk], writes=[("yt", c % 2, g)],
                     out=ytile[:, g * 512:(g + 1) * 512], in_=t[:, :])
            p.dma("sp", yboff_d[c * 128:(c + 1) * 128, :], ytile[:, :], reads=[("yt", c % 2, g) for g in range(NG)],
                  writes=[("yboff", c)])
            if c > 0:
                state_update(c, 1, nxw)
                nxw += 1

        if sdbg < 4:
            p.barrier()
            return
        for g in range(NG):
            p.op("pool", "memset", writes=[("H", g)], ap=H[:, g, :], constant=0.0)
            p.op("pool", "memset", writes=[("Hb", g)], ap=Hb[:, g, :], constant=0.0)
        nseg = 0
        for c in range(NCH):
            cs_ = slice(c * 128, (c + 1) * 128)
            ytile = yt[c % 2]
            cb_t = cbt[c % 2]
            for g in range(NG):
                p.op("pe", "matmul", reads=[("BT", g), ("CT", g)], writes=[("ps_cb",)],
                     out=ps_cb[:, g * 128:(g + 1) * 128], lhsT=BT[:, g, cs_], rhs=CT[:, g, cs_],
                     start=True, stop=True)
            p.op("act", "activation", reads=[("ps_cb",)], writes=[("cbt", c % 2)], out=cb_t[:, :], in_=ps_cb[:, :],
                 func=AF.Identity)
            for g in range(NG):
                py = ps_y[g % 2]
                yk = ("ps_y", g % 2)
                for hp in range(HPG // 2):
                    bank = nseg % 2
                    nseg += 1
                    sk = ("ps_seg", bank)
                    units = [(hp * 2 + u // 2, u % 2) for u in range(4)]
                    for u, (hl, d) in enumerate(units):
                        r = d * NH + g * HPG + hl
                        pseg = ps_seg[bank][:, u * 128:(u + 1) * 128]
                        for w in range(2):
                            p.op("pe", "matmul", reads=[("Rhl", w), ("SEL",)], writes=[sk], out=pseg,
                                 lhsT=SEL[:, r * 128:(r + 1) * 128], rhs=Rhl[:, w, cs_],
                                 start=(w == 0), stop=False)
                        p.op("pe", "matmul", reads=[("cstb",)], writes=[sk], out=pseg, lhsT=identb,
                             rhs=(maskF if d == 0 else maskB), start=False, stop=True)
                    for u, (hl, d) in enumerate(units):
                        h = g * HPG + hl
                        r = d * NH + h
                        pseg = ps_seg[bank][:, u * 128:(u + 1) * 128]
                        li = cnt["l"] % 4
                        cnt["l"] += 1
                        p.op("act", "activation", reads=[sk, ("tokA", c)], writes=[("Lt", li)], out=Lt[li][:, :],
                             in_=pseg, func=AF.Exp, bias=tokA[:, c, 64 + r:64 + r + 1],
                             scale=(1.0 if d == 0 else -1.0))
                        p.op("dve", "scalar_tensor_tensor", reads=[("Lt", li), ("tokA", c), ("cbt", c % 2)],
                             writes=[("Mt", li)], out=Mt[li][:, :], in0=Lt[li][:, :], scalar=tokA[:, c, r:r + 1],
                             in1=cb_t[:, g * 128:(g + 1) * 128], op0=ALU.mult, op1=ALU.mult)
                        p.op("pe", "matmul", reads=[("Mt", li), ("x_tok", c)], writes=[yk],
                             out=py[:, hl * HD:(hl + 1) * HD], lhsT=Mt[li][:, :],
                             rhs=x_tok[:, c, h * HD:(h + 1) * HD], start=(d == 0), stop=(d == 1))
                t, tk = g_term(c, 0, g)
                t2 = tmp2[g % 2]
                dv = dbc[:, g * HPG:(g + 1) * HPG].rearrange("p (h o) -> p h o", o=1).to_broadcast([128, HPG, HD])
                p.op("pool", "tensor_tensor", reads=[("x_tok", c), ("dbc",)], writes=[("tmp2", g % 2)],
                     out=t2[:, :].rearrange("p (h d) -> p h d", d=HD),
                     in0=x_tok[:, c, g * 512:(g + 1) * 512].rearrange("p (h d) -> p h d", d=HD), in1=dv, op=ALU.mult)
                p.op("pool", "tensor_tensor", reads=[("tmp2", g % 2), tk], writes=[("tmp2", g % 2)], out=t2[:, :],
                     in0=t2[:, :], in1=t[:, :], op=ALU.add)
                p.op("dve", "tensor_tensor", reads=[("tmp2", g % 2), yk], writes=[("yt", c % 2, g)],
                     out=ytile[:, g * 512:(g + 1) * 512], in0=t2[:, :], in1=py[:, :], op=ALU.add)
            p.dma("sp", ymain_d[c * 128:(c + 1) * 128, :], ytile[:, :], reads=[("yt", c % 2, g) for g in range(NG)],
                  writes=[("ymain", c)])
            if c < NCH - 1:
                state_update(c, 0, nxw)
                nxw += 1
        p.barrier()


def emit_norm_generic(c, st, src, skey, nk, gcol, dst, dkey, tbs, dim):
    p = c.p
    sq = [p.sbuf("sq", [128, 512], BF16, st) for _ in range(2)]
    rs = [p.sbuf("rs", [128, 512], F32, st) for _ in range(2)]
    for i, tb in enumerate(tbs):
        ts = slice(i * 512, (i + 1) * 512)
        ps, pk = c.bank(i % 2)
        for k in range(nk):
            s = sq[k % 2]
            p.op("act", "activation", reads=[(skey, k, tb)], writes=[("sq", k % 2)],
                 out=s[:, :], in_=src[:, k, ts], func=AF.Square)
            p.op("pe", "matmul", reads=[("sq", k % 2), ("ones",)], writes=[pk],
                 out=ps, lhsT=c.ones_bf[:, :], rhs=s[:, :], start=(k == 0), stop=(k == nk - 1))
        r = rs[i % 2]
        p.op("act", "activation", reads=[pk], writes=[("rs", i % 2)],
             out=r[:, :], in_=ps, func=AF.Sqrt, scale=1.0 / dim, bias=EPS)
        p.op("dve", "reciprocal", reads=[("rs", i % 2)], writes=[("rs", i % 2)], out=r[:, :], in_=r[:, :])
        for k in range(nk):
            p.op("dve", "scalar_tensor_tensor", reads=[(skey, k, tb), ("rs", i % 2), ("par",)],
                 writes=[(dkey, k, tb)],
                 out=dst[:, k, ts], in0=src[:, k, ts], scalar=gcol[:, k:k + 1], in1=r[:, :],
                 op0=ALU.mult, op1=ALU.mult)


def emit_proj_out(c, st, hT, w_d, out_d, ncc, nsilu=0, bf_d=None):
    p = c.p
    NS = 3
    wt = [p.sbuf("wt", [128, 2048], BF16, st) for _ in range(NS)]
    ot = [p.sbuf("ot", [128, 512], F32, st) for _ in range(4)]
    otb = [p.sbuf("otb", [128, 512], BF16, st) for _ in range(4)] if bf_d is not None else None
    no = 0
    for cc in range(ncc):
        s = cc % NS
        c.load_w(wt[s][:, :], ("wt", s), w_d[cc], 2048)
        for kc in range(KC):
            for tb in range(c.NTB):
                ps, pk = c.bank(2 * (cc % 4) + tb)
                p.op("pe", "matmul", reads=[("wt", s), ("h", kc, tb)], writes=[pk],
                     out=ps, lhsT=wt[s][:, kc * 128:(kc + 1) * 128], rhs=hT[:, kc, tb * 512:(tb + 1) * 512],
                     start=(kc == 0), stop=(kc == KC - 1))
        for tb in range(c.NTB):
            ps, pk = c.bank(2 * (cc % 4) + tb)
            if cc < nsilu:
                o, ok, dst = otb[no % 4], ("otb", no % 4), bf_d[cc, :, tb * 512:(tb + 1) * 512]
            else:
                o, ok, dst = ot[no % 4], ("ot", no % 4), out_d[cc - nsilu, :, tb * 512:(tb + 1) * 512]
            no += 1
            p.op("act", "activation", reads=[pk], writes=[ok], out=o[:, :], in_=ps,
                 func=(AF.Silu if cc < nsilu else AF.Identity))
            p.dma("act", dst, o[:, :], reads=[ok], writes=[("projout", cc, tb)])


def emit_proj_resid(c, st, srcT, skey, nk, w_d, tbs, scale, wt, wkey_base):
    p = c.p
    nw = len(wt)
    for dc in range(KC):
        s = c.nwt % nw
        c.nwt += 1
        for a in range(0, nk * 128, 2048):
            b = min(a + 2048, nk * 128)
            c.load_w(wt[s][:, a:b], (wkey_base, s, a), w_d[dc, :, a:b], b - a)
        wkeys = [(wkey_base, s, a) for a in range(0, nk * 128, 2048)]
        for k in range(nk):
            for i, tb in enumerate(tbs):
                ps, pk = c.bank(2 * (dc % 4) + (i % 2))
                p.op("pe", "matmul", reads=wkeys + [(skey, k, tb)], writes=[pk],
                     out=ps, lhsT=wt[s][:, k * 128:(k + 1) * 128], rhs=srcT[:, k, i * 512:(i + 1) * 512],
                     start=(k == 0), stop=(k == nk - 1))
        for i, tb in enumerate(tbs):
            ps, pk = c.bank(2 * (dc % 4) + (i % 2))
            ts = slice(tb * 512, (tb + 1) * 512)
            p.op("dve", "scalar_tensor_tensor", reads=[pk, ("x", dc, tb)], writes=[("x", dc, tb)],
                 out=c.xT[:, dc, ts], in0=ps, scalar=scale, in1=c.xT[:, dc, ts], op0=ALU.mult, op1=ALU.add)


def emit_mix_in(c, gcol_d, w_d, out_d, ncc, nsilu, bf_d=None):
    p = c.p
    with contextlib.ExitStack() as st:
        hT = p.sbuf("hT", [128, KC, c.T], BF16, st)
        gcol = p.sbuf("gcol", [128, KC], F32, st)
        p.dma("sp", gcol[:, :], gcol_d, writes=[("par",)])
        emit_rmsnorm(c, st, hT, gcol)
        emit_proj_out(c, st, hT, w_d, out_d, ncc, nsilu, bf_d)
        p.barrier()


NIC = 32


def emit_ssd_out(c, ym_d, yb_d, sz_d, gn_d, wout_d):
    p = c.p
    T = c.T
    with contextlib.ExitStack() as st:
        gs = p.sbuf("gs", [128, NIC, T], BF16, st)
        KB = 2
        ld = [p.sbuf("ld", [128, 3, KB, 512], BF16, st) for _ in range(3)]
        g32 = [p.sbuf("g32", [128, KB, 512], F32, st) for _ in range(2)]
        sq = [p.sbuf("sq", [128, KB, 512], BF16, st) for _ in range(2)]
        rs = [p.sbuf("rs", [128, 512], F32, st) for _ in range(c.NTB)]
        tmpo = [p.sbuf("tmpo", [128, 512], F32, st) for _ in range(2)]
        gcol = p.sbuf("gcol", [128, NIC], F32, st)
        wt = [p.sbuf("wo", [128, NIC * 128], BF16, st) for _ in range(2)]
        p.dma("sp", gcol[:, :], gn_d, writes=[("par",)])
        n = 0
        for tb in range(c.NTB):
            ts = slice(tb * 512, (tb + 1) * 512)
            ps, pk = c.bank(tb)
            for k0 in range(0, NIC, KB):
                i = n % 3
                j = n % 2
                n += 1
                l = ld[i]
                gg = g32[j]
                src = lambda d: d[k0:k0 + KB, :, ts].rearrange("k p t -> p k t")
                p.dma("sp", l[:, 0, :, :], src(ym_d), writes=[("ld", i, 0)])
                p.dma("sp", l[:, 1, :, :], src(yb_d), writes=[("ld", i, 1)])
                p.dma("sp", l[:, 2, :, :], src(sz_d), writes=[("ld", i, 2)])
                p.op("pool", "tensor_tensor", reads=[("ld", i, 0), ("ld", i, 1)], writes=[("g32", j)],
                     out=gg[:, :, :], in0=l[:, 0, :, :], in1=l[:, 1, :, :], op=ALU.add)
                p.op("pool", "tensor_tensor", reads=[("g32", j), ("ld", i, 2)], writes=[("g32", j)],
                     out=gg[:, :, :], in0=gg[:, :, :], in1=l[:, 2, :, :], op=ALU.mult)
                s_ = sq[j]
                p.op("act", "activation", reads=[("g32", j)], writes=[("sq", j)], out=s_[:, :, :],
                     in_=gg[:, :, :], func=AF.Square)
                for kk in range(KB):
                    k = k0 + kk
                    p.op("pe", "matmul", reads=[("sq", j), ("ones",)], writes=[pk], out=ps, lhsT=c.ones_bf[:, :],
                         rhs=s_[:, kk, :], start=(k == 0), stop=(k == NIC - 1))
                    p.op("dve", "tensor_scalar", reads=[("g32", j), ("par",)], writes=[("gs", k, tb)],
                         out=gs[:, k, ts], in0=gg[:, kk, :], scalar1=gcol[:, k:k + 1], scalar2=None, op0=ALU.mult)
            r = rs[tb]
            p.op("act", "activation", reads=[pk], writes=[("rs", tb)], out=r[:, :], in_=ps, func=AF.Sqrt,
                 scale=1.0 / (NIC * 128), bias=EPS)
            p.op("dve", "reciprocal", reads=[("rs", tb)], writes=[("rs", tb)], out=r[:, :], in_=r[:, :])
        nt = 0
        for dc in range(KC):
            s = dc % 2
            for a in range(0, NIC * 128, 2048):
                c.load_w(wt[s][:, a:a + 2048], ("wo", s, a), wout_d[dc, :, a:a + 2048], 2048)
            wkeys = [("wo", s, a) for a in range(0, NIC * 128, 2048)]
            for k in range(NIC):
                for tb in range(c.NTB):
                    ps, pk = c.bank(2 * (dc % 4) + tb)
                    p.op("pe", "matmul", reads=wkeys + [("gs", k, tb)], writes=[pk], out=ps,
                         lhsT=wt[s][:, k * 128:(k + 1) * 128], rhs=gs[:, k, tb * 512:(tb + 1) * 512],
                         start=(k == 0), stop=(k == NIC - 1))
            for tb in range(c.NTB):
                ps, pk = c.bank(2 * (dc % 4) + tb)
                ts = slice(tb * 512, (tb + 1) * 512)
                t_ = tmpo[nt % 2]
                tk = ("tmpo", nt % 2)
                nt += 1
                p.op("dve", "tensor_tensor", reads=[pk, ("rs", tb)], writes=[tk], out=t_[:, :], in0=ps,
                     in1=rs[tb][:, :], op=ALU.mult)
                p.op("pool", "tensor_tensor", reads=[tk, ("x", dc, tb)], writes=[("x", dc, tb)], out=c.xT[:, dc, ts],
                     in0=c.xT[:, dc, ts], in1=t_[:, :], op=ALU.add)
        p.barrier()


POOL_W = (2, 4, 8, 16)


def emit_pool_out(c, up_d, icnt_d, wg_d, sc_d, wout_d):
    p = c.p
    T = c.T
    with contextlib.ExitStack() as st:
        mixT = p.sbuf("mixT", [128, KC, T], BF16, st)
        vT = p.sbuf("vT", [128, KC, T], BF16, st)
        icnt = p.sbuf("icnt", [128, 4 * T], F32, st)
        sc = p.sbuf("sc", [128, KC], F32, st)
        A = [[p.sbuf("A", [128, T + 16], F32, st) for _ in range(3)] for _ in range(2)]
        wgt = [p.sbuf("wgt", [128, 512], BF16, st) for _ in range(2)]
        wt = [p.sbuf("wpo", [128, 2048], BF16, st) for _ in range(3)]
        p.dma("sp", icnt[:, :], icnt_d, writes=[("icnt",)])
        p.dma("sp", sc[:, :], sc_d, writes=[("par",)])
        for cc in range(KC):
            gi = cc // 4
            b = cc % 2
            eng = "pool" if b == 0 else "dve"
            a0, a1, a2 = A[b]
            p.dma("sp", a0[:, :], up_d[cc], writes=[("A", b, 0)])
            p.op(eng, "tensor_tensor", reads=[("A", b, 0)], writes=[("A", b, 1)], out=a1[:, 1:T + 16],
                 in0=a0[:, 0:T + 15], in1=a0[:, 1:T + 16], op=ALU.add)
            cur, curk, oth, othk = a1, ("A", b, 1), a2, ("A", b, 2)
            lo, hi = 1, T + 16
            sh = 1
            for lvl in range(gi):
                nlo, nhi = lo + sh, hi - sh
                p.op(eng, "tensor_tensor", reads=[curk], writes=[othk], out=oth[:, nlo:nhi],
                     in0=cur[:, nlo - sh:nhi - sh], in1=cur[:, nlo + sh:nhi + sh], op=ALU.add)
                cur, curk, oth, othk = oth, othk, cur, curk
                lo, hi = nlo, nhi
                sh *= 2
            assert lo <= 8 and hi >= T + 8
            p.op(eng, "tensor_tensor", reads=[curk, ("icnt",)], writes=[othk], out=oth[:, 8:8 + T],
                 in0=cur[:, 8:8 + T], in1=icnt[:, gi * T:(gi + 1) * T], op=ALU.mult)
            for tb in range(c.NTB):
                p.op(eng, "tensor_tensor", reads=[othk, ("A", b, 0)], writes=[("mix", cc, tb)],
                     out=mixT[:, cc, tb * 512:(tb + 1) * 512], in0=oth[:, 8 + tb * 512:8 + (tb + 1) * 512],
                     in1=a0[:, 8 + tb * 512:8 + (tb + 1) * 512], op=ALU.subtract)
        for dc in range(KC):
            gi = dc // 4
            s = dc % 2
            c.load_w(wgt[s][:, :], ("wgt", s), wg_d[dc], 512)
            for kl in range(4):
                for tb in range(c.NTB):
                    ps, pk = c.bank(2 * (dc % 4) + tb)
                    p.op("pe", "matmul", reads=[("wgt", s), ("mix", gi * 4 + kl, tb)], writes=[pk],
                         out=ps, lhsT=wgt[s][:, kl * 128:(kl + 1) * 128],
                         rhs=mixT[:, gi * 4 + kl, tb * 512:(tb + 1) * 512], start=(kl == 0), stop=(kl == 3))
            for tb in range(c.NTB):
                ps, pk = c.bank(2 * (dc % 4) + tb)
                p.op("act", "activation", reads=[pk, ("par",)], writes=[("v", dc, tb)],
                     out=vT[:, dc, tb * 512:(tb + 1) * 512], in_=ps, func=AF.Identity, scale=sc[:, dc:dc + 1])
        c.nwt = 0
        emit_proj_resid(c, st, vT, "v", KC, wout_d, list(range(c.NTB)), 1.0, wt, "wpo")
        p.barrier()


def emit_final(c, gcol_d, out_d):
    p = c.p
    with contextlib.ExitStack() as st:
        oT = p.sbuf("oT", [128, KC, c.T], F32, st)
        gcol = p.sbuf("gcol", [128, KC], F32, st)
        p.dma("sp", gcol[:, :], gcol_d, writes=[("par",)])
        emit_rmsnorm(c, st, oT, gcol, hkey="o")
        for kc in range(KC):
            p.dma("sp", out_d[:, kc, :], oT[:, kc, :], reads=[("o", kc, tb) for tb in range(c.NTB)],
                  writes=[("final", kc)])
        p.barrier()


TT = 1024
_PROG_CACHE = {}


def build_tok_launch(stages):
    key = tuple(stages)
    if key in _PROG_CACHE:
        return _PROG_CACHE[key]
    p = Prog()
    c = Ctx(p, TT)
    ein = lambda n, sh, dt=F32: p.dram(n, sh, dt, kind="ExternalInput")
    eout = lambda n, sh, dt=F32: p.dram(n, sh, dt, kind="ExternalOutput")
    emit_load_x(c, ein("xin", [128, KC, TT]))
    has_final = False
    for i, stg in enumerate(stages):
        t = "_%d" % i
        if stg == "ffn":
            emit_ffn(c, ein("wg" + t, [FC, 128, 2048]), ein("wu" + t, [FC, 128, 2048]),
                     ein("wd" + t, [NQ, KC, 128, FQ * 128]), ein("gcol" + t, [128, KC]))
        elif stg == "ssd_in":
            emit_mix_in(c, ein("gcol" + t, [128, KC]), ein("w" + t, [81, 128, 2048]), eout("proj", [49, 128, TT]), 81, 32,
                        eout("sz", [NIC, 128, TT], BF16))
        elif stg == "pool_in":
            emit_mix_in(c, ein("gcol" + t, [128, KC]), ein("w" + t, [KC, 128, 2048]), eout("proj", [KC, 128, TT]), KC, 0)
        elif stg == "ssd_out":
            emit_ssd_out(c, ein("ym" + t, [NIC, 128, TT], BF16), ein("yb" + t, [NIC, 128, TT], BF16),
                         ein("sz" + t, [NIC, 128, TT], BF16),
                         ein("gn" + t, [128, NIC]), ein("wout" + t, [KC, 128, NIC * 128]))
        elif stg == "pool_out":
            emit_pool_out(c, ein("up" + t, [KC, 128, TT + 16]), ein("icnt" + t, [128, 4 * TT]),
                          ein("wgrp" + t, [KC, 128, 512]), ein("sc" + t, [128, KC]), ein("wout" + t, [KC, 128, 2048]))
        elif stg == "final":
            emit_final(c, ein("gcol" + t, [128, KC]), eout("final", [128, KC, TT]))
            has_final = True
    if not has_final:
        emit_store_x(c, eout("xout", [128, KC, TT]))
    nc = p.finish()
    _PROG_CACHE[key] = nc
    return nc


def build_ssd_core_launch():
    if "ssd_core" in _PROG_CACHE:
        return _PROG_CACHE["ssd_core"]
    p = Prog()
    ein = lambda n, sh, dt=F32: p.dram(n, sh, dt, kind="ExternalInput")
    eout = lambda n, sh, dt=F32: p.dram(n, sh, dt, kind="ExternalOutput")
    emit_ssd_core(p, ein("xbc", [NCC, 128, TS + 4]), ein("cw", [128, NCC * 5]), ein("cb", [128, NCC]),
                  ein("dtraw", [128, TS]), ein("dtpar", [128, 8]), ein("dbc", [128, NH]),
                  ein("cstf", [128, 128 + TS]), ein("cstb", [128, 384 + 64 * 128], BF16),
                  eout("ymain", [TS, NH * HD], BF16), eout("yboff", [TS, NH * HD], BF16))
    nc = p.finish()
    _PROG_CACHE["ssd_core"] = nc
    return nc


def _ssd_consts():
    import ml_dtypes
    identf = np.eye(128, dtype=np.float32)
    reset = np.ones((128, TS), np.float32)
    reset[:, ::128] = 0.0
    cstf = np.concatenate([identf, reset], axis=1)
    j = np.arange(128)[:, None]
    i = np.arange(128)[None, :]
    maskF = np.where(j > i, -MASKV, 0.0).astype(np.float32)
    maskB = np.where(j < i, MASKV, 0.0).astype(np.float32)
    sel = np.zeros((128, 64, 128), np.float32)
    for r in range(64):
        sel[r, r, :] = 1.0
    cstb = np.concatenate([identf, maskF, maskB, sel.reshape(128, 64 * 128)], axis=1).astype(ml_dtypes.bfloat16)
    return cstf, cstb


def _run(nc, in_maps):
    res = run_bass_kernel_spmd(nc, in_maps, core_ids=list(range(8)))
    return res.results


def _ffn_inputs(t, wg, wu, wd, g):
    return {"wg" + t: lay_w_stat(wg), "wu" + t: lay_w_stat(wu), "wd" + t: lay_wd(wd), "gcol" + t: lay_col(g)}


def kernel(x, ffn_norm, ffn_w_gate, ffn_w_up, ffn_w_down, mix_norm, ssd_w_in, ssd_conv_w, ssd_conv_b, ssd_dt_bias,
           ssd_a_log, ssd_d, ssd_norm, ssd_w_out, pool_w_in, pool_w_group, pool_scale, pool_w_out, final_norm):
    f = lambda a: np.ascontiguousarray(np.asarray(a, dtype=np.float32))
    x = f(x)
    B, L, _ = x.shape
    xs = [lay_xT(x[c // 2, (c % 2) * TT:(c % 2 + 1) * TT]) for c in range(8)]
    cstf, cstb = _ssd_consts()

    def ffn_in(t, l, i):
        return _ffn_inputs(t, f(ffn_w_gate[l, i]), f(ffn_w_up[l, i]), f(ffn_w_down[l, i]), f(ffn_norm[l, i]))

    def tok_launch(stages, shared, percore):
        nc = build_tok_launch(stages)
        in_maps = []
        for c in range(8):
            m = {"xin": xs[c]}
            m.update(shared)
            m.update(percore[c])
            in_maps.append(m)
        return _run(nc, in_maps)

    def ssd_core_stage(j, proj):
        nc = build_ssd_core_launch()
        cwf, cbf = f(ssd_conv_w[j]), f(ssd_conv_b[j])
        in_maps = []
        for c in range(8):
            b, hh = c // 2, c % 2
            P = np.concatenate([proj[2 * b].reshape(49 * 128, TT), proj[2 * b + 1].reshape(49 * 128, TT)], axis=1)
            rows = np.concatenate([np.arange(hh * 2048, (hh + 1) * 2048),
                                   np.arange(4096 + hh * 512, 4096 + (hh + 1) * 512),
                                   np.arange(5120 + hh * 512, 5120 + (hh + 1) * 512)])
            xbc = np.pad(P[rows], ((0, 0), (2, 2))).reshape(NCC, 128, TS + 4)
            cch = rows
            cw = np.ascontiguousarray(cwf[:, cch].T.reshape(NCC, 128, 5).transpose(1, 0, 2)).reshape(128, NCC * 5)
            cb = np.ascontiguousarray(cbf[cch].reshape(NCC, 128).T)
            drow = np.concatenate([6144 + d * 64 + hh * 32 + np.arange(32) for d in range(2)])
            dtr = P[drow]
            par = np.zeros((64, 8), np.float32)
            par[:, 0] = f(ssd_dt_bias[j])[:, hh * 32:(hh + 1) * 32].reshape(64)
            par[:, 1] = f(ssd_a_log[j])[:, hh * 32:(hh + 1) * 32].reshape(64)
            par[:32, 2] = -1.0
            par[32:, 2] = 1.0
            par[32:, 3] = 1.0
            par[:32, 4] = 1.0
            dbc = np.ascontiguousarray(np.broadcast_to(f(ssd_d[j])[None, hh * 32:(hh + 1) * 32], (128, NH)))
            in_maps.append({"xbc": np.ascontiguousarray(xbc), "cw": cw, "cb": cb,
                            "dtraw": np.ascontiguousarray(np.concatenate([dtr, dtr], axis=0)),
                            "dtpar": np.concatenate([par, par], axis=0), "dbc": dbc, "cstf": cstf, "cstb": cstb})
        res = _run(nc, in_maps)
        ym, yb = [], []
        for c in range(8):
            b, half = c // 2, c % 2
            for name, lst in (("ymain", ym), ("yboff", yb)):
                Y = np.concatenate([res[2 * b][name], res[2 * b + 1][name]], axis=1)
                lst.append(np.ascontiguousarray(Y[half * TT:(half + 1) * TT].T.reshape(NIC, 128, TT)))
        return ym, yb

    def pool_inputs(t, j, proj):
        per = []
        for c in range(8):
            b, half = c // 2, c % 2
            U = np.concatenate([proj[2 * b].reshape(D, TT), proj[2 * b + 1].reshape(D, TT)], axis=1)
            Up = np.pad(U, ((0, 0), (8, 8)))
            up = np.ascontiguousarray(Up[:, half * TT:half * TT + TT + 16]).reshape(KC, 128, TT + 16)
            tg = half * TT + np.arange(TT)
            ic = np.stack([1.0 / (np.clip(tg + w // 2, 0, L) - np.clip(tg - w // 2, 0, L)) for w in POOL_W])
            icnt = np.ascontiguousarray(np.broadcast_to(ic.reshape(1, 4 * TT), (128, 4 * TT))).astype(np.float32)
            per.append({"up" + t: up, "icnt" + t: icnt})
        shared = {"wgrp" + t: np.concatenate([lay_w_stat(f(pool_w_group[j, gi])) for gi in range(4)], axis=0),
                  "sc" + t: lay_col(f(pool_scale[j])), "wout" + t: lay_w_stat(f(pool_w_out[j]))}
        return shared, per

    empty = [dict() for _ in range(8)]
    sh = ffn_in("_0", 0, 0)
    sh.update({"gcol_1": lay_col(f(mix_norm[0])), "w_1": lay_w_stat(f(ssd_w_in[0]))})
    r = tok_launch(("ffn", "ssd_in"), sh, empty)
    xs = [r[c]["xout"] for c in range(8)]
    proj = [r[c]["proj"] for c in range(8)]
    szs = [r[c]["sz"] for c in range(8)]
    for j in range(2):
        l = 2 * j
        ym, yb = ssd_core_stage(j, proj)
        sh = {"gn_0": lay_col(f(ssd_norm[j])), "wout_0": lay_w_stat(f(ssd_w_out[j]))}
        sh.update(ffn_in("_1", l, 1))
        sh.update(ffn_in("_2", l + 1, 0))
        sh.update({"gcol_3": lay_col(f(mix_norm[l + 1])), "w_3": lay_w_stat(f(pool_w_in[j]))})
        per = [{"ym_0": ym[c], "yb_0": yb[c], "sz_0": szs[c]} for c in range(8)]
        r = tok_launch(("ssd_out", "ffn", "ffn", "pool_in"), sh, per)
        xs = [r[c]["xout"] for c in range(8)]
        proj = [r[c]["proj"] for c in range(8)]
        sh, per = pool_inputs("_0", j, proj)
        sh.update(ffn_in("_1", l + 1, 1))
        if j == 0:
            sh.update(ffn_in("_2", l + 2, 0))
            sh.update({"gcol_3": lay_col(f(mix_norm[l + 2])), "w_3": lay_w_stat(f(ssd_w_in[1]))})
            r = tok_launch(("pool_out", "ffn", "ffn", "ssd_in"), sh, per)
            xs = [r[c]["xout"] for c in range(8)]
            proj = [r[c]["proj"] for c in range(8)]
            szs = [r[c]["sz"] for c in range(8)]
        else:
            sh.update({"gcol_2": lay_col(f(final_norm))})
            r = tok_launch(("pool_out", "ffn", "final"), sh, per)
    out = np.empty((B, L, D), np.float32)
    for c in range(8):
        out[c // 2, (c % 2) * TT:(c % 2 + 1) * TT] = unlay_xT(r[c]["final"])
    return out
```

```python
import contextlib
import numpy as np
import concourse.bass as bass
import concourse.mybir as mybir
from concourse.bass_utils import run_bass_kernel_spmd

F32 = mybir.dt.float32
BF16 = mybir.dt.bfloat16
ALU = mybir.AluOpType
AF = mybir.ActivationFunctionType

D = 2048
KC = D // 128
DFF = 5632
FC = DFF // 128
NQ = 4
FQ = FC // NQ
EPS = 1e-6
NDMASEM = 24


class Prog:
    ENG = ("pe", "act", "dve", "pool", "sp")

    def __init__(self):
        self.nc = bass.Bass("TRN2", target_bir_lowering=False)
        self.es = contextlib.ExitStack()
        self.ops = {e: [] for e in self.ENG}
        self.cnt = {e: 0 for e in self.ENG}
        self.sem = {e: self.es.enter_context(self.nc.semaphore("s_" + e)) for e in self.ENG}
        self.dsem = [self.es.enter_context(self.nc.semaphore("d%d" % i)) for i in range(NDMASEM)]
        self.ndma = 0
        self.dma_last = [None] * NDMASEM
        self.waited = {e: {} for e in self.ENG}
        self.last_w = {}
        self.readers = {}
        self.all_tokens = []
        self.uid = 0

    def sbuf(self, name, shape, dtype, stack=None):
        self.uid += 1
        t = (stack or self.es).enter_context(self.nc.sbuf_tensor("%s_%d" % (name, self.uid), list(shape), dtype))
        return t

    def psum(self, name, shape, dtype, stack=None):
        self.uid += 1
        return (stack or self.es).enter_context(self.nc.psum_tensor("%s_%d" % (name, self.uid), list(shape), dtype))

    def dram(self, name, shape, dtype, kind="Internal"):
        return self.nc.dram_tensor(name, list(shape), dtype, kind=kind).ap()

    def _deps(self, reads, writes):
        deps = []
        for k in reads:
            if k in self.last_w:
                deps.append(self.last_w[k])
        for k in writes:
            if k in self.last_w:
                deps.append(self.last_w[k])
            deps.extend(self.readers.get(k, ()))
        return deps

    def _commit(self, tok, reads, writes):
        for k in reads:
            self.readers.setdefault(k, []).append(tok)
        for k in writes:
            self.last_w[k] = tok
            self.readers[k] = []

    def _waits(self, eng, deps, pe_skip=False):
        w = self.waited[eng]
        best = {}
        for (s, v, src) in deps:
            if pe_skip and src == "pe":
                continue
            if w.get(id(s), 0) >= v:
                continue
            if best.get(id(s), (None, 0))[1] < v:
                best[id(s)] = (s, v)
        out = []
        for sid, (s, v) in best.items():
            w[sid] = v
            out.append((s, v))
        return out

    def op(self, eng, meth, reads=(), writes=(), **kw):
        fn = (lambda e, meth=meth, kw=kw: getattr(e, meth)(**kw))
        deps = self._deps(reads, writes)
        waits = self._waits(eng, deps, pe_skip=(eng == "pe"))
        self.cnt[eng] += 1
        tok = (self.sem[eng], self.cnt[eng], eng)
        self.ops[eng].append((waits, fn, (self.sem[eng], 1)))
        self._commit(tok, reads, writes)
        return tok

    def dma(self, eng, out, in_, reads=(), writes=(), **kw):
        deps = self._deps(reads, writes)
        i = self.ndma % NDMASEM
        val = 16 * (self.ndma // NDMASEM + 1)
        self.ndma += 1
        if self.dma_last[i] is not None:
            deps.append(self.dma_last[i])
        waits = self._waits(eng, deps)
        tok = (self.dsem[i], val, "dma")
        self.dma_last[i] = tok
        self.ops[eng].append((waits, lambda e, out=out, in_=in_, kw=kw: e.dma_start(out=out, in_=in_, **kw),
                              (self.dsem[i], 16)))
        self._commit(tok, reads, writes)
        self.all_tokens.append(tok)
        return tok

    def barrier(self):
        toks = [(self.sem[e], self.cnt[e], e) for e in self.ENG if self.cnt[e] > 0]
        toks += [t for t in self.dma_last if t is not None]
        for e in self.ENG:
            waits = self._waits(e, toks)
            if waits:
                self.ops[e].append((waits, None, None))
        self.last_w.clear()
        self.readers.clear()
        print("[prog] counts", self.cnt, "ndma", self.ndma, flush=True)

    def finish(self):
        self.barrier()
        with self.nc.Block() as block:
            def runner(name):
                def body(e):
                    for waits, fn, inc in self.ops[name]:
                        for (s, v) in waits:
                            e.wait_ge(s, v)
                        if fn is not None:
                            ins = fn(e)
                            ins.then_inc(inc[0], inc[1])
                return body
            block.tensor(runner("pe"))
            block.scalar(runner("act"))
            block.vector(runner("dve"))
            block.gpsimd(runner("pool"))
            block.sync(runner("sp"))
        self.es.close()
        return self.nc


class Ctx:
    def __init__(self, p, T):
        self.p = p
        self.T = T
        self.NTB = T // 512
        self.xT = p.sbuf("xT", [128, KC, T], F32)
        self.ones_bf = p.sbuf("ones_bf", [128, 128], BF16)
        p.op("dve", "memset", writes=[("ones",)], ap=self.ones_bf[:, :], constant=1.0)
        self.banks = [p.psum("bank%d" % i, [128, 512], F32) for i in range(8)]
        self.NSTG = 3
        import os
        self.two_q = os.environ.get("TWOQ", "0") == "1"
        self.stg = [p.sbuf("stg", [128, 2048], F32) for _ in range(self.NSTG)]
        self.nstg = 0

    def bank(self, i):
        return self.banks[i][:, :], ("ps", i)

    def load_w(self, dst, dkey, src, ncols):
        p = self.p
        i = self.nstg % self.NSTG
        self.nstg += 1
        q = "sp" if (self.nstg % 2 == 0 or not self.two_q) else "act"
        p.dma(q, self.stg[i][:, 0:ncols], src, writes=[("stg", i)])
        p.op("pool", "tensor_copy", reads=[("stg", i)], writes=[dkey], out=dst, in_=self.stg[i][:, 0:ncols])


def emit_load_x(c, x_dram):
    p = c.p
    for kc in range(KC):
        p.dma("sp", c.xT[:, kc, :], x_dram[:, kc, :], writes=[("x", kc, tb) for tb in range(c.NTB)])


def emit_store_x(c, x_dram):
    p = c.p
    for kc in range(KC):
        p.dma("sp", x_dram[:, kc, :], c.xT[:, kc, :], reads=[("x", kc, tb) for tb in range(c.NTB)],
              writes=[("xout", kc)])


def emit_rmsnorm(c, st, hT, gcol, hkey="h"):
    p = c.p
    sq = [p.sbuf("sq", [128, 512], BF16, st) for _ in range(2)]
    rs = [p.sbuf("rs", [128, 512], F32, st) for _ in range(2)]
    for tb in range(c.NTB):
        ts = slice(tb * 512, (tb + 1) * 512)
        ps, pk = c.bank(tb)
        for kc in range(KC):
            s = sq[kc % 2]
            p.op("act", "activation", reads=[("x", kc, tb)], writes=[("sq", kc % 2)],
                 out=s[:, :], in_=c.xT[:, kc, ts], func=AF.Square)
            p.op("pe", "matmul", reads=[("sq", kc % 2), ("ones",)], writes=[pk],
                 out=ps, lhsT=c.ones_bf[:, :], rhs=s[:, :], start=(kc == 0), stop=(kc == KC - 1))
        r = rs[tb % 2]
        p.op("act", "activation", reads=[pk], writes=[("rs", tb % 2)],
             out=r[:, :], in_=ps, func=AF.Sqrt, scale=1.0 / D, bias=EPS)
        p.op("dve", "reciprocal", reads=[("rs", tb % 2)], writes=[("rs", tb % 2)], out=r[:, :], in_=r[:, :])
        for kc in range(KC):
            p.op("dve", "scalar_tensor_tensor", reads=[("x", kc, tb), ("rs", tb % 2), ("par",)],
                 writes=[(hkey, kc, tb)],
                 out=hT[:, kc, ts], in0=c.xT[:, kc, ts], scalar=gcol[:, kc:kc + 1], in1=r[:, :],
                 op0=ALU.mult, op1=ALU.mult)


def emit_ffn(c, wg_d, wu_d, wd_d, gcol_d):
    p = c.p
    T = c.T
    with contextlib.ExitStack() as st:
        hT = p.sbuf("hT", [128, KC, T], BF16, st)
        actq = p.sbuf("actq", [128, FQ, T], BF16, st)
        NS = 3
        wg = [p.sbuf("wg", [128, 2048], BF16, st) for _ in range(NS)]
        wu = [p.sbuf("wu", [128, 2048], BF16, st) for _ in range(NS)]
        NSD = 4
        wd = [p.sbuf("wd", [128, FQ * 128], BF16, st) for _ in range(NSD)]
        sg = [p.sbuf("sg", [128, 512], F32, st) for _ in range(2)]
        gcol = p.sbuf("gcol", [128, KC], F32, st)
        p.dma("sp", gcol[:, :], gcol_d, writes=[("par",)])
        emit_rmsnorm(c, st, hT, gcol)
        nsg = 0
        ndc = 0
        import os
        dbg = int(os.environ.get("DBG", "9"))
        for q in range(NQ if dbg >= 2 else 0):
            for fl in range(FQ):
                fc = q * FQ + fl
                s = fc % NS
                c.load_w(wg[s][:, :], ("wg", s), wg_d[fc], 2048)
                c.load_w(wu[s][:, :], ("wu", s), wu_d[fc], 2048)
                base = 4 * (fc % 2)
                for kc in range(KC):
                    for tb in range(c.NTB):
                        ps, pk = c.bank(base + tb)
                        p.op("pe", "matmul", reads=[("wg", s), ("h", kc, tb)], writes=[pk],
                             out=ps, lhsT=wg[s][:, kc * 128:(kc + 1) * 128],
                             rhs=hT[:, kc, tb * 512:(tb + 1) * 512], start=(kc == 0), stop=(kc == KC - 1))
                    for tb in range(c.NTB):
                        ps, pk = c.bank(base + 2 + tb)
                        p.op("pe", "matmul", reads=[("wu", s), ("h", kc, tb)], writes=[pk],
                             out=ps, lhsT=wu[s][:, kc * 128:(kc + 1) * 128],
                             rhs=hT[:, kc, tb * 512:(tb + 1) * 512], start=(kc == 0), stop=(kc == KC - 1))
                for tb in range(c.NTB):
                    gps, gk = c.bank(base + tb)
                    ups, uk = c.bank(base + 2 + tb)
                    sgt = sg[nsg % 2]
                    sk = ("sg", nsg % 2)
                    nsg += 1
                    p.op("act", "activation", reads=[gk], writes=[sk], out=sgt[:, :], in_=gps, func=AF.Silu)
                    p.op("dve", "tensor_tensor", reads=[sk, uk], writes=[("actq", fl, tb)],
                         out=actq[:, fl, tb * 512:(tb + 1) * 512], in0=sgt[:, :], in1=ups, op=ALU.mult)
            for dc in range(KC if dbg >= 3 else 0):
                s = ndc % NSD
                c.load_w(wd[s][:, :], ("wd", s), wd_d[q, dc], FQ * 128)
                base = 2 * (ndc % 4)
                ndc += 1
                for fl in range(FQ):
                    for tb in range(c.NTB):
                        ps, pk = c.bank(base + tb)
                        p.op("pe", "matmul", reads=[("wd", s), ("actq", fl, tb)], writes=[pk],
                             out=ps, lhsT=wd[s][:, fl * 128:(fl + 1) * 128],
                             rhs=actq[:, fl, tb * 512:(tb + 1) * 512], start=(fl == 0), stop=(fl == FQ - 1))
                for tb in range(c.NTB):
                    ps, pk = c.bank(base + tb)
                    ts = slice(tb * 512, (tb + 1) * 512)
                    p.op("dve", "scalar_tensor_tensor", reads=[pk, ("x", dc, tb)], writes=[("x", dc, tb)],
                         out=c.xT[:, dc, ts], in0=ps, scalar=0.5, in1=c.xT[:, dc, ts],
                         op0=ALU.mult, op1=ALU.add)
        p.barrier()


def lay_w_stat(w):
    K, M = w.shape
    return np.ascontiguousarray(w.reshape(K // 128, 128, M // 128, 128).transpose(2, 1, 0, 3)).reshape(
        M // 128, 128, K)


def lay_wd(w):
    return np.ascontiguousarray(w.reshape(NQ, FQ, 128, KC, 128).transpose(0, 3, 2, 1, 4)).reshape(
        NQ, KC, 128, FQ * 128)


def lay_col(v):
    return np.ascontiguousarray(v.reshape(-1, 128).T)


def lay_xT(xtok):
    T = xtok.shape[0]
    return np.ascontiguousarray(xtok.T.reshape(KC, 128, T).transpose(1, 0, 2))


def unlay_xT(xT):
    T = xT.shape[2]
    return np.ascontiguousarray(xT.transpose(1, 0, 2).reshape(D, T).T)


TS = 2048
NCH = TS // 128
NH = 32
NG = 4
HPG = 8
HD = 64
NST = 128
NXC = NH * HD // 128
NCC = NXC + 2 * NG
MASKV = 30000.0


def emit_ssd_core(p, xbc_d, cw_d, cb_d, dtraw_d, dtpar_d, dbc_d, cstf_d, cstb_d, ymain_d, yboff_d):
    es = contextlib.ExitStack()
    with es:
        cstb = p.sbuf("cstb", [128, 384], BF16, es)
        SEL = p.sbuf("SEL", [128, 64 * 128], BF16, es)
        onesb = p.sbuf("onesb", [128, 128], BF16, es)
        dtpar = p.sbuf("dtpar", [128, 8], F32, es)
        dbc = p.sbuf("dbc", [128, NH], F32, es)
        cw = p.sbuf("cw", [128, NCC * 5], F32, es)
        cbias = p.sbuf("cbias", [128, NCC], F32, es)
        Rhl = p.sbuf("Rhl", [128, 2, TS], BF16, es)
        tokA = p.sbuf("tokA", [128, NCH, 128], F32, es)
        tokB = p.sbuf("tokB", [128, NCH, 128], F32, es)
        decbc = p.sbuf("decbc", [128, NCH, 64], F32, es)
        p.dma("sp", cstb[:, :], cstb_d[:, 0:384], writes=[("cstb",)])
        p.dma("sp", SEL[:, :], cstb_d[:, 384:384 + 64 * 128], writes=[("SEL",)])
        p.dma("sp", dtpar[:, :], dtpar_d, writes=[("dtpar",)])
        p.dma("sp", dbc[:, :], dbc_d, writes=[("dbc",)])
        p.dma("sp", cw[:, :], cw_d, writes=[("cw",)])
        p.dma("sp", cbias[:, :], cb_d, writes=[("cbias",)])
        p.op("dve", "memset", writes=[("onesb",)], ap=onesb[:, :], constant=1.0)
        identb = cstb[:, 0:128]
        maskF = cstb[:, 128:256]
        maskB = cstb[:, 256:384]

        with contextlib.ExitStack() as sa:
            reset = p.sbuf("reset", [128, TS], F32, sa)
            Rt = p.sbuf("Rt", [128, TS], F32, sa)
            hl = p.sbuf("hl", [128, 3, 2, TS], BF16, sa)
            dtr = p.sbuf("dtr", [128, TS], F32, sa)
            dt = p.sbuf("dt", [128, TS], F32, sa)
            dta = p.sbuf("dta", [128, TS], F32, sa)
            cs = p.sbuf("cs", [128, TS], F32, sa)
            Qt = p.sbuf("Qt", [128, TS], F32, sa)
            eR = p.sbuf("eR", [128, TS], F32, sa)
            eQ = p.sbuf("eQ", [128, TS], F32, sa)
            QA = p.sbuf("QA", [128, TS], F32, sa)
            QB = p.sbuf("QB", [128, TS], F32, sa)
            acol = p.sbuf("acol", [128, 2], F32, sa)
            psA = [p.psum("psA", [128, 512], F32, sa) for _ in range(2)]
            psB = [p.psum("psB", [128, 512], F32, sa) for _ in range(2)]
            p.dma("sp", reset[:, :], cstf_d[:, 128:128 + TS], writes=[("reset",)])
            p.dma("sp", dtr[:, :], dtraw_d, writes=[("dtr",)])
            p.op("act", "activation", reads=[("dtpar",)], writes=[("acol",)], out=acol[:, 0:1], in_=dtpar[:, 1:2],
                 func=AF.Exp)
            p.op("dve", "tensor_scalar", reads=[("acol",)], writes=[("acol2",)], out=acol[:, 1:2], in0=acol[:, 0:1],
                 scalar1=-1.0, scalar2=None, op0=ALU.mult)
            p.op("act", "activation", reads=[("dtr",), ("dtpar",)], writes=[("dt",)], out=dt[:, :], in_=dtr[:, :],
                 func=AF.Exp, bias=dtpar[:, 0:1], scale=1.0)
            p.op("act", "activation", reads=[("dt",)], writes=[("dt",)], out=dt[:, :], in_=dt[:, :],
                 func=AF.Ln, bias=1.0, scale=1.0)
            import os
            sa_lvl = int(os.environ.get("SSD_A", "9"))
            if sa_lvl < 1:
                p.barrier(); return
            p.op("dve", "tensor_scalar", reads=[("dt",), ("acol2",)], writes=[("dta",)], out=dta[:, :], in0=dt[:, :],
                 scalar1=acol[:, 1:2], scalar2=None, op0=ALU.mult)
            p.op("dve", "tensor_tensor_scan", reads=[("reset",), ("dta",)], writes=[("cs",)], out=cs[:, :],
                 data0=reset[:, :], data1=dta[:, :], initial=0.0, op0=ALU.mult, op1=ALU.add)
            if sa_lvl < 2:
                p.barrier(); return
            p.op("dve", "tensor_scalar", reads=[("dta",), ("dtpar",)], writes=[("Qt",)], out=Qt[:, :], in0=dta[:, :],
                 scalar1=dtpar[:, 3:4], scalar2=None, op0=ALU.mult)
            p.op("dve", "tensor_tensor", reads=[("cs",), ("Qt",)], writes=[("R",)], out=Rt[:, :], in0=cs[:, :],
                 in1=Qt[:, :], op=ALU.subtract)
            tot_bc = cs[:, :].rearrange("p (c q) -> p c q", q=128)[:, :, 127:128].to_broadcast([128, NCH, 128])
            p.op("dve", "tensor_tensor", reads=[("cs",), ("R",)], writes=[("Qt",)],
                 out=Qt[:, :].rearrange("p (c q) -> p c q", q=128), in0=tot_bc,
                 in1=Rt[:, :].rearrange("p (c q) -> p c q", q=128), op=ALU.subtract)
            p.op("act", "activation", reads=[("R",)], writes=[("eR",)], out=eR[:, :], in_=Rt[:, :], func=AF.Exp)
            p.op("act", "activation", reads=[("Qt",)], writes=[("eQ",)], out=eQ[:, :], in_=Qt[:, :], func=AF.Exp)
            p.op("dve", "tensor_copy", reads=[("dt",)], writes=[("QA0",)], out=QA[0:64, :], in_=dt[0:64, :])
            p.op("dve", "tensor_scalar", reads=[("R",), ("dtpar",)], writes=[("QA1",)], out=QA[64:128, :],
                 in0=Rt[64:128, :], scalar1=dtpar[64:128, 2:3], scalar2=None, op0=ALU.mult)
            p.op("dve", "tensor_scalar", reads=[("eQ",), ("dtpar",)], writes=[("QB0",)], out=QB[0:64, :],
                 in0=eQ[0:64, :], scalar1=dtpar[0:64, 4:5], scalar2=None, op0=ALU.mult)
            p.op("dve", "scalar_tensor_tensor", reads=[("eR",), ("QB0",), ("dtpar",)], writes=[("QB0",)],
                 out=QB[0:64, :], in0=eR[0:64, :], scalar=dtpar[0:64, 3:4], in1=QB[0:64, :],
                 op0=ALU.mult, op1=ALU.add)
            p.op("dve", "tensor_tensor", reads=[("QB0",), ("dt",)], writes=[("QB0",)], out=QB[0:64, :],
                 in0=QB[0:64, :], in1=dt[0:64, :], op=ALU.mult)
            p.op("dve", "tensor_scalar", reads=[("eR",), ("dtpar",)], writes=[("QB1",)], out=QB[64:128, :],
                 in0=eR[64:128, :], scalar1=dtpar[64:128, 4:5], scalar2=None, op0=ALU.mult)
            p.op("dve", "scalar_tensor_tensor", reads=[("eQ",), ("QB1",), ("dtpar",)], writes=[("QB1",)],
                 out=QB[64:128, :], in0=eQ[64:128, :], scalar=dtpar[64:128, 3:4], in1=QB[64:128, :],
                 op0=ALU.mult, op1=ALU.add)
            if sa_lvl < 3:
                p.barrier(); return
            def split(src, skeys, dst_hi, dst_lo, dkey, scratch, sckey):
                p.op("act", "activation", reads=skeys, writes=[(dkey, 0)], out=dst_hi, in_=src, func=AF.Identity)
                p.op("dve", "tensor_tensor", reads=skeys + [(dkey, 0)], writes=[sckey], out=scratch, in0=src,
                     in1=dst_hi, op=ALU.subtract)
                p.op("act", "activation", reads=[sckey], writes=[(dkey, 1)], out=dst_lo, in_=scratch, func=AF.Identity)
            split(QA[:, :], [("QA0",), ("QA1",)], hl[:, 0, 0, :], hl[:, 0, 1, :], "hlA", eR[:, :], ("eR",))
            split(QB[:, :], [("QB0",), ("QB1",), ("eR",)], hl[:, 1, 0, :], hl[:, 1, 1, :], "hlB", eQ[:, :], ("eQ",))
            split(Rt[:, :], [("R",), ("eQ",)], Rhl[:, 0, :], Rhl[:, 1, :], "Rhl", Qt[:, :], ("Qt",))
            identb_ = cstb[:, 0:128]
            if sa_lvl < 4:
                p.barrier(); return
            Dm = dt[:, 0:NCH * 64]
            Dm3 = Dm.rearrange("p (c r) -> p c r", r=64)
            tot3 = cs[:, :].rearrange("p (c q) -> p c q", q=128)[:, :, 127:128].to_broadcast([128, NCH, 64])
            id3 = cstb[:, 0:64].rearrange("p (o r) -> p o r", o=1).to_broadcast([128, NCH, 64])
            p.op("dve", "tensor_tensor", reads=[("cs",), ("cstb",), ("hlA", 0), ("hlA", 1), ("hlB", 0), ("hlB", 1)],
                 writes=[("Dm",)], out=Dm3, in0=tot3, in1=id3, op=ALU.mult)
            if sa_lvl < 5:
                p.barrier(); return
            Dhl = hl[:, 2, :, 0:NCH * 64]
            split(Dm, [("Dm",), ("Qt",), ("Rhl", 1)], Dhl[:, 0, :], Dhl[:, 1, :], "Dhl", Qt[:, 0:NCH * 64], ("Qt",))
            if sa_lvl < 6:
                p.barrier(); return
            for half in range(2):
                pb = psB[half]
                for w in range(2):
                    p.op("pe", "matmul", reads=[("Dhl", w), ("onesb",)], writes=[("psB", half)], out=pb[:, :],
                         lhsT=onesb[:, :], rhs=Dhl[:, w, half * 512:(half + 1) * 512], start=(w == 0), stop=(w == 1))
                p.op("act", "activation", reads=[("psB", half)], writes=[("decbc", half)],
                     out=decbc[:, half * 8:(half + 1) * 8, :],
                     in_=pb[:, :].rearrange("p (c r) -> p c r", r=64), func=AF.Exp)
            if sa_lvl < 7:
                p.barrier(); return
            for c in range(NCH):
                cs_ = slice(c * 128, (c + 1) * 128)
                pa = psA[c % 2]
                pak = ("psA", c % 2)
                for q in range(2):
                    for w in range(2):
                        p.op("pe", "matmul", reads=[("hlA", w), ("hlB", w), ("cstb",)], writes=[pak],
                             out=pa[:, q * 128:(q + 1) * 128], lhsT=hl[:, q, w, cs_], rhs=identb_,
                             start=(w == 0), stop=(w == 1))
                p.op("act", "activation", reads=[pak], writes=[("tokA", c)], out=tokA[:, c, :], in_=pa[:, 0:128],
                     func=AF.Identity)
                p.op("act", "activation", reads=[pak], writes=[("tokB", c)], out=tokB[:, c, :], in_=pa[:, 128:256],
                     func=AF.Identity)
            p.barrier()
        import os
        if int(os.environ.get("SSD_DBG", "9")) < 2:
            return
        _emit_ssd_main(p, es, xbc_d, ymain_d, yboff_d, SEL, identb, maskF, maskB, dbc, cw, cbias, Rhl, tokA, tokB,
                       decbc)
        p.barrier()


def _emit_ssd_main(p, es, xbc_d, ymain_d, yboff_d, SEL, identb, maskF, maskB, dbc, cw, cbias, Rhl, tokA, tokB,
                   decbc):
    x_tok = p.sbuf("x_tok", [128, NCH, NH * HD], BF16, es)
    B_tok = p.sbuf("B_tok", [128, NCH, NG * NST], BF16, es)
    BT = p.sbuf("BT", [128, NG, TS], BF16, es)
    CT = p.sbuf("CT", [128, NG, TS], BF16, es)
    with contextlib.ExitStack() as sb:
        u = [p.sbuf("u", [128, TS + 4], F32, sb) for _ in range(2)]
        acc = [p.sbuf("acc", [128, TS], F32, sb) for _ in range(2)]
        fm = [p.sbuf("fm", [128, TS], BF16, sb) for _ in range(2)]
        pst = [p.psum("pst", [128, 1024], F32, sb) for _ in range(2)]
        npst = 0
        for cc in range(NCC):
            b = cc % 2
            p.dma("sp", u[b][:, :], xbc_d[cc], writes=[("u", b)])
            p.op("dve", "tensor_scalar", reads=[("u", b), ("cw",), ("cbias",)], writes=[("acc", b)],
                 out=acc[b][:, :], in0=u[b][:, 0:TS], scalar1=cw[:, cc * 5:cc * 5 + 1],
                 scalar2=cbias[:, cc:cc + 1], op0=ALU.mult, op1=ALU.add)
            for j in range(1, 5):
                p.op("dve", "scalar_tensor_tensor", reads=[("u", b), ("cw",), ("acc", b)], writes=[("acc", b)],
                     out=acc[b][:, :], in0=u[b][:, j:j + TS], scalar=cw[:, cc * 5 + j:cc * 5 + j + 1],
                     in1=acc[b][:, :], op0=ALU.mult, op1=ALU.add)
            if cc < NXC:
                dst, dk = fm[b][:, :], ("fm", b)
            elif cc < NXC + NG:
                dst, dk = BT[:, cc - NXC, :], ("BT", cc - NXC)
            else:
                dst, dk = CT[:, cc - NXC - NG, :], ("CT", cc - NXC - NG)
            p.op("act", "activation", reads=[("acc", b)], writes=[dk], out=dst, in_=acc[b][:, :], func=AF.Silu)
            if cc < NXC + NG:
                for c4 in range(NCH // 8):
                    pt = pst[npst % 2]
                    pk = ("pst", npst % 2)
                    npst += 1
                    for k in range(8):
                        c = c4 * 8 + k
                        p.op("pe", "matmul", reads=[dk, ("cstb",)], writes=[pk],
                             out=pt[:, k * 128:(k + 1) * 128], lhsT=dst[:, c * 128:(c + 1) * 128], rhs=identb,
                             start=True, stop=True)
                    if cc < NXC:
                        o = x_tok[:, c4 * 8:(c4 + 1) * 8, cc * 128:(cc + 1) * 128]
                        ok = [("x_tok", c) for c in range(c4 * 8, c4 * 8 + 8)]
                    else:
                        g = cc - NXC
                        o = B_tok[:, c4 * 8:(c4 + 1) * 8, g * 128:(g + 1) * 128]
                        ok = [("B_tok", c) for c in range(c4 * 8, c4 * 8 + 8)]
                    p.op("pool" if False else "act", "activation", reads=[pk], writes=ok, out=o,
                         in_=pt[:, :].rearrange("p (k q) -> p k q", q=128), func=AF.Identity)
        p.barrier()

    import os
    sdbg = int(os.environ.get("SSD_DBG", "9"))
    if sdbg < 3:
        return
    with contextlib.ExitStack() as sc:
        Hs = [p.sbuf("H", [128, NG, 512], F32, sc) for _ in range(2)]
        Hbs = [p.sbuf("Hb", [128, NG, 512], BF16, sc) for _ in range(2)]
        xws = [p.sbuf("xw", [128, NH * HD], BF16, sc)] * 2
        yt = [p.sbuf("yt", [128, NH * HD], BF16, sc)] * 2
        ytb = [p.sbuf("ytb", [128, NH * HD], BF16, sc)] * 2
        tmp = [p.sbuf("tmp", [128, 512], F32, sc) for _ in range(2)]
        tmp2 = [p.sbuf("tmp2", [128, 512], F32, sc) for _ in range(2)]
        Lt = [p.sbuf("Lt", [128, 128], BF16, sc) for _ in range(4)]
        Mt = [p.sbuf("Mt", [128, 128], BF16, sc) for _ in range(8)]
        cbt = [p.sbuf("cbt", [128, NG * 128], BF16, sc) for _ in range(2)]
        ps_seg = [p.psum("ps_seg", [128, 512], F32, sc) for _ in range(2)]
        ps_cb = p.psum("ps_cb", [128, 512], F32, sc)
        ps_y = [p.psum("ps_y", [128, 512], F32, sc) for _ in range(2)]
        ps_g = [p.psum("ps_g", [128, 512], F32, sc) for _ in range(2)]
        ps_s = p.psum("ps_s", [128, 512], F32, sc)
        cnt = {"g": 0, "t": 0, "l": 0}

        def hview(d, g):
            return Hs[d][:, g, :].rearrange("p (h d) -> p h d", d=HD)

        def state_update(c, d, nxw):
            xwt = xws[d]
            xk = ("xw", 0)
            H, Hb = Hs[d], Hbs[d]
            wst = tokB[:, c, d * NH:(d + 1) * NH].rearrange("p (h o) -> p h o", o=1).to_broadcast([128, NH, HD])
            p.op("pool", "tensor_tensor", reads=[("x_tok", c), ("tokB", c)], writes=[xk],
                 out=xwt[:, :].rearrange("p (h d) -> p h d", d=HD),
                 in0=x_tok[:, c, :].rearrange("p (h d) -> p h d", d=HD), in1=wst, op=ALU.mult)
            for g in range(NG):
                p.op("pe", "matmul", reads=[("B_tok", c), xk], writes=[("ps_s",)], out=ps_s[:, :],
                     lhsT=B_tok[:, c, g * 128:(g + 1) * 128], rhs=xwt[:, g * 512:(g + 1) * 512],
                     start=True, stop=True)
                dec = decbc[:, c, d * NH + g * HPG:d * NH + (g + 1) * HPG].rearrange(
                    "p (h o) -> p h o", o=1).to_broadcast([128, HPG, HD])
                p.op("pool", "tensor_tensor", reads=[("H", d, g), ("decbc", c // 8)], writes=[("H", d, g)], out=hview(d, g),
                     in0=hview(d, g), in1=dec, op=ALU.mult)
                p.op("dve", "tensor_tensor", reads=[("H", d, g), ("ps_s",)], writes=[("H", d, g)], out=H[:, g, :],
                     in0=H[:, g, :], in1=ps_s[:, :], op=ALU.add)
                p.op("act", "activation", reads=[("H", d, g)], writes=[("Hb", d, g)], out=Hb[:, g, :], in_=H[:, g, :],
                     func=AF.Identity)

        def g_term(c, d, g):
            pg = ps_g[cnt["g"] % 2]
            gk = ("ps_g", cnt["g"] % 2)
            cnt["g"] += 1
            p.op("pe", "matmul", reads=[("CT", g), ("Hb", d, g)], writes=[gk], out=pg[:, :],
                 lhsT=CT[:, g, c * 128:(c + 1) * 128], rhs=Hbs[d][:, g, :], start=True, stop=True)
            t = tmp[cnt["t"] % 2]
            tk = ("tmp", cnt["t"] % 2)
            cnt["t"] += 1
            eo = tokB[:, c, 64 + d * NH + g * HPG:64 + d * NH + (g + 1) * HPG].rearrange(
                "p (h o) -> p h o", o=1).to_broadcast([128, HPG, HD])
            p.op("dve", "tensor_tensor", reads=[gk, ("tokB", c)], writes=[tk],
                 out=t[:, :].rearrange("p (h d) -> p h d", d=HD),
                 in0=pg[:, :].rearrange("p (h d) -> p h d", d=HD), in1=eo, op=ALU.mult)
            return t, tk

        for d in range(2):
            for g in range(NG):
                p.op("pool", "memset", writes=[("H", d, g)], ap=Hs[d][:, g, :], constant=0.0)
                p.op("pool", "memset", writes=[("Hb", d, g)], ap=Hbs[d][:, g, :], constant=0.0)

        def bwd_step(c):
            ytile = ytb[c % 2]
            for g in range(NG):
                t, tk = g_term(c, 1, g)
                p.op("pool", "tensor_copy", reads=[tk], writes=[("ytb", 0, g)],
                     out=ytile[:, g * 512:(g + 1) * 512], in_=t[:, :])
            p.dma("sp", yboff_d[c * 128:(c + 1) * 128, :], ytile[:, :], reads=[("ytb", 0, g) for g in range(NG)],
                  writes=[("yboff", c)])
            if c > 0:
                state_update(c, 1, 0)

        nseg = 0
        for c in range(NCH):
            bwd_step(NCH - 1 - c)
            cs_ = slice(c * 128, (c + 1) * 128)
            ytile = yt[c % 2]
            cb_t = cbt[c % 2]
            for g in range(NG):
                p.op("pe", "matmul", reads=[("BT", g), ("CT", g)], writes=[("ps_cb",)],
                     out=ps_cb[:, g * 128:(g + 1) * 128], lhsT=BT[:, g, cs_], rhs=CT[:, g, cs_],
                     start=True, stop=True)
            p.op("act", "activation", reads=[("ps_cb",)], writes=[("cbt", c % 2)], out=cb_t[:, :], in_=ps_cb[:, :],
                 func=AF.Identity)
            for g in range(NG):
                py = ps_y[g % 2]
                yk = ("ps_y", g % 2)
                pend = []
                for hp in range(HPG // 2):
                    bank = nseg % 2
                    nseg += 1
                    sk = ("ps_seg", bank)
                    units = [(hp * 2 + u // 2, u % 2) for u in range(4)]
                    for u, (hl, d) in enumerate(units):
                        r = d * NH + g * HPG + hl
                        pseg = ps_seg[bank][:, u * 128:(u + 1) * 128]
                        for w in range(2):
                            p.op("pe", "matmul", reads=[("Rhl", w), ("SEL",)], writes=[sk], out=pseg,
                                 lhsT=SEL[:, r * 128:(r + 1) * 128], rhs=Rhl[:, w, cs_],
                                 start=(w == 0), stop=False)
                        p.op("pe", "matmul", reads=[("cstb",)], writes=[sk], out=pseg, lhsT=identb,
                             rhs=(maskF if d == 0 else maskB), start=False, stop=True)
                    for args in pend:
                        p.op("pe", "matmul", **args)
                    pend = []
                    for u, (hl, d) in enumerate(units):
                        h = g * HPG + hl
                        r = d * NH + h
                        pseg = ps_seg[bank][:, u * 128:(u + 1) * 128]
                        li = cnt["l"] % 4
                        mi = cnt["l"] % 8
                        cnt["l"] += 1
                        p.op("act", "activation", reads=[sk, ("tokA", c)], writes=[("Lt", li)], out=Lt[li][:, :],
                             in_=pseg, func=AF.Exp, bias=tokA[:, c, 64 + r:64 + r + 1],
                             scale=(1.0 if d == 0 else -1.0))
                        p.op("dve", "scalar_tensor_tensor", reads=[("Lt", li), ("tokA", c), ("cbt", c % 2)],
                             writes=[("Mt", mi)], out=Mt[mi][:, :], in0=Lt[li][:, :], scalar=tokA[:, c, r:r + 1],
                             in1=cb_t[:, g * 128:(g + 1) * 128], op0=ALU.mult, op1=ALU.mult)
                        pend.append(dict(reads=[("Mt", mi), ("x_tok", c)], writes=[yk],
                                         out=py[:, hl * HD:(hl + 1) * HD], lhsT=Mt[mi][:, :],
                                         rhs=x_tok[:, c, h * HD:(h + 1) * HD], start=(d == 0), stop=(d == 1)))
                for args in pend:
                    p.op("pe", "matmul", **args)
                t, tk = g_term(c, 0, g)
                t2 = tmp2[g % 2]
                dv = dbc[:, g * HPG:(g + 1) * HPG].rearrange("p (h o) -> p h o", o=1).to_broadcast([128, HPG, HD])
                p.op("pool", "tensor_tensor", reads=[("x_tok", c), ("dbc",)], writes=[("tmp2", g % 2)],
                     out=t2[:, :].rearrange("p (h d) -> p h d", d=HD),
                     in0=x_tok[:, c, g * 512:(g + 1) * 512].rearrange("p (h d) -> p h d", d=HD), in1=dv, op=ALU.mult)
                p.op("pool", "tensor_tensor", reads=[("tmp2", g % 2), tk], writes=[("tmp2", g % 2)], out=t2[:, :],
                     in0=t2[:, :], in1=t[:, :], op=ALU.add)
                p.op("dve", "tensor_tensor", reads=[("tmp2", g % 2), yk], writes=[("yt", 0, g)],
                     out=ytile[:, g * 512:(g + 1) * 512], in0=t2[:, :], in1=py[:, :], op=ALU.add)
            p.dma("sp", ymain_d[c * 128:(c + 1) * 128, :], ytile[:, :], reads=[("yt", 0, g) for g in range(NG)],
                  writes=[("ymain", c)])
            if c < NCH - 1:
                state_update(c, 0, 0)
        p.barrier()


def emit_norm_generic(c, st, src, skey, nk, gcol, dst, dkey, tbs, dim):
    p = c.p
    sq = [p.sbuf("sq", [128, 512], BF16, st) for _ in range(2)]
    rs = [p.sbuf("rs", [128, 512], F32, st) for _ in range(2)]
    for i, tb in enumerate(tbs):
        ts = slice(i * 512, (i + 1) * 512)
        ps, pk = c.bank(i % 2)
        for k in range(nk):
            s = sq[k % 2]
            p.op("act", "activation", reads=[(skey, k, tb)], writes=[("sq", k % 2)],
                 out=s[:, :], in_=src[:, k, ts], func=AF.Square)
            p.op("pe", "matmul", reads=[("sq", k % 2), ("ones",)], writes=[pk],
                 out=ps, lhsT=c.ones_bf[:, :], rhs=s[:, :], start=(k == 0), stop=(k == nk - 1))
        r = rs[i % 2]
        p.op("act", "activation", reads=[pk], writes=[("rs", i % 2)],
             out=r[:, :], in_=ps, func=AF.Sqrt, scale=1.0 / dim, bias=EPS)
        p.op("dve", "reciprocal", reads=[("rs", i % 2)], writes=[("rs", i % 2)], out=r[:, :], in_=r[:, :])
        for k in range(nk):
            p.op("dve", "scalar_tensor_tensor", reads=[(skey, k, tb), ("rs", i % 2), ("par",)],
                 writes=[(dkey, k, tb)],
                 out=dst[:, k, ts], in0=src[:, k, ts], scalar=gcol[:, k:k + 1], in1=r[:, :],
                 op0=ALU.mult, op1=ALU.mult)


def emit_proj_out(c, st, hT, w_d, out_d, ncc, nsilu=0, bf_d=None):
    p = c.p
    NS = 3
    wt = [p.sbuf("wt", [128, 2048], BF16, st) for _ in range(NS)]
    ot = [p.sbuf("ot", [128, 512], F32, st) for _ in range(4)]
    otb = [p.sbuf("otb", [128, 512], BF16, st) for _ in range(4)] if bf_d is not None else None
    no = 0
    for cc in range(ncc):
        s = cc % NS
        c.load_w(wt[s][:, :], ("wt", s), w_d[cc], 2048)
        for kc in range(KC):
            for tb in range(c.NTB):
                ps, pk = c.bank(2 * (cc % 4) + tb)
                p.op("pe", "matmul", reads=[("wt", s), ("h", kc, tb)], writes=[pk],
                     out=ps, lhsT=wt[s][:, kc * 128:(kc + 1) * 128], rhs=hT[:, kc, tb * 512:(tb + 1) * 512],
                     start=(kc == 0), stop=(kc == KC - 1))
        for tb in range(c.NTB):
            ps, pk = c.bank(2 * (cc % 4) + tb)
            if cc < nsilu:
                o, ok, dst = otb[no % 4], ("otb", no % 4), bf_d[cc, :, tb * 512:(tb + 1) * 512]
            else:
                o, ok, dst = ot[no % 4], ("ot", no % 4), out_d[cc - nsilu, :, tb * 512:(tb + 1) * 512]
            no += 1
            p.op("act", "activation", reads=[pk], writes=[ok], out=o[:, :], in_=ps,
                 func=(AF.Silu if cc < nsilu else AF.Identity))
            p.dma("act", dst, o[:, :], reads=[ok], writes=[("projout", cc, tb)])


def emit_proj_resid(c, st, srcT, skey, nk, w_d, tbs, scale, wt, wkey_base):
    p = c.p
    nw = len(wt)
    for dc in range(KC):
        s = c.nwt % nw
        c.nwt += 1
        for a in range(0, nk * 128, 2048):
            b = min(a + 2048, nk * 128)
            c.load_w(wt[s][:, a:b], (wkey_base, s, a), w_d[dc, :, a:b], b - a)
        wkeys = [(wkey_base, s, a) for a in range(0, nk * 128, 2048)]
        for k in range(nk):
            for i, tb in enumerate(tbs):
                ps, pk = c.bank(2 * (dc % 4) + (i % 2))
                p.op("pe", "matmul", reads=wkeys + [(skey, k, tb)], writes=[pk],
                     out=ps, lhsT=wt[s][:, k * 128:(k + 1) * 128], rhs=srcT[:, k, i * 512:(i + 1) * 512],
                     start=(k == 0), stop=(k == nk - 1))
        for i, tb in enumerate(tbs):
            ps, pk = c.bank(2 * (dc % 4) + (i % 2))
            ts = slice(tb * 512, (tb + 1) * 512)
            p.op("dve", "scalar_tensor_tensor", reads=[pk, ("x", dc, tb)], writes=[("x", dc, tb)],
                 out=c.xT[:, dc, ts], in0=ps, scalar=scale, in1=c.xT[:, dc, ts], op0=ALU.mult, op1=ALU.add)


def emit_mix_in(c, gcol_d, w_d, out_d, ncc, nsilu, bf_d=None):
    p = c.p
    with contextlib.ExitStack() as st:
        hT = p.sbuf("hT", [128, KC, c.T], BF16, st)
        gcol = p.sbuf("gcol", [128, KC], F32, st)
        p.dma("sp", gcol[:, :], gcol_d, writes=[("par",)])
        emit_rmsnorm(c, st, hT, gcol)
        emit_proj_out(c, st, hT, w_d, out_d, ncc, nsilu, bf_d)
        p.barrier()


NIC = 32


def emit_ssd_out(c, ym_d, yb_d, sz_d, gn_d, wout_d):
    p = c.p
    T = c.T
    with contextlib.ExitStack() as st:
        gs = p.sbuf("gs", [128, NIC, T], BF16, st)
        KB = 2
        ld = [p.sbuf("ld", [128, 3, KB, 512], BF16, st) for _ in range(3)]
        g32 = [p.sbuf("g32", [128, KB, 512], F32, st) for _ in range(2)]
        sq = [p.sbuf("sq", [128, KB, 512], BF16, st) for _ in range(2)]
        rs = [p.sbuf("rs", [128, 512], F32, st) for _ in range(c.NTB)]
        tmpo = [p.sbuf("tmpo", [128, 512], F32, st) for _ in range(2)]
        gcol = p.sbuf("gcol", [128, NIC], F32, st)
        wt = [p.sbuf("wo", [128, NIC * 128], BF16, st) for _ in range(2)]
        p.dma("sp", gcol[:, :], gn_d, writes=[("par",)])
        n = 0
        for tb in range(c.NTB):
            ts = slice(tb * 512, (tb + 1) * 512)
            ps, pk = c.bank(tb)
            for k0 in range(0, NIC, KB):
                i = n % 3
                j = n % 2
                n += 1
                l = ld[i]
                gg = g32[j]
                src = lambda d: d[k0:k0 + KB, :, ts].rearrange("k p t -> p k t")
                p.dma("sp", l[:, 0, :, :], src(ym_d), writes=[("ld", i, 0)])
                p.dma("sp", l[:, 1, :, :], src(yb_d), writes=[("ld", i, 1)])
                p.dma("sp", l[:, 2, :, :], src(sz_d), writes=[("ld", i, 2)])
                p.op("pool", "tensor_tensor", reads=[("ld", i, 0), ("ld", i, 1)], writes=[("g32", j)],
                     out=gg[:, :, :], in0=l[:, 0, :, :], in1=l[:, 1, :, :], op=ALU.add)
                p.op("pool", "tensor_tensor", reads=[("g32", j), ("ld", i, 2)], writes=[("g32", j)],
                     out=gg[:, :, :], in0=gg[:, :, :], in1=l[:, 2, :, :], op=ALU.mult)
                s_ = sq[j]
                p.op("act", "activation", reads=[("g32", j)], writes=[("sq", j)], out=s_[:, :, :],
                     in_=gg[:, :, :], func=AF.Square)
                for kk in range(KB):
                    k = k0 + kk
                    p.op("pe", "matmul", reads=[("sq", j), ("ones",)], writes=[pk], out=ps, lhsT=c.ones_bf[:, :],
                         rhs=s_[:, kk, :], start=(k == 0), stop=(k == NIC - 1))
                    p.op("dve", "tensor_scalar", reads=[("g32", j), ("par",)], writes=[("gs", k, tb)],
                         out=gs[:, k, ts], in0=gg[:, kk, :], scalar1=gcol[:, k:k + 1], scalar2=None, op0=ALU.mult)
            r = rs[tb]
            p.op("act", "activation", reads=[pk], writes=[("rs", tb)], out=r[:, :], in_=ps, func=AF.Sqrt,
                 scale=1.0 / (NIC * 128), bias=EPS)
            p.op("dve", "reciprocal", reads=[("rs", tb)], writes=[("rs", tb)], out=r[:, :], in_=r[:, :])
        nt = 0
        for dc in range(KC):
            s = dc % 2
            for a in range(0, NIC * 128, 2048):
                c.load_w(wt[s][:, a:a + 2048], ("wo", s, a), wout_d[dc, :, a:a + 2048], 2048)
            wkeys = [("wo", s, a) for a in range(0, NIC * 128, 2048)]
            for k in range(NIC):
                for tb in range(c.NTB):
                    ps, pk = c.bank(2 * (dc % 4) + tb)
                    p.op("pe", "matmul", reads=wkeys + [("gs", k, tb)], writes=[pk], out=ps,
                         lhsT=wt[s][:, k * 128:(k + 1) * 128], rhs=gs[:, k, tb * 512:(tb + 1) * 512],
                         start=(k == 0), stop=(k == NIC - 1))
            for tb in range(c.NTB):
                ps, pk = c.bank(2 * (dc % 4) + tb)
                ts = slice(tb * 512, (tb + 1) * 512)
                t_ = tmpo[nt % 2]
                tk = ("tmpo", nt % 2)
                nt += 1
                p.op("dve", "tensor_tensor", reads=[pk, ("rs", tb)], writes=[tk], out=t_[:, :], in0=ps,
                     in1=rs[tb][:, :], op=ALU.mult)
                p.op("pool", "tensor_tensor", reads=[tk, ("x", dc, tb)], writes=[("x", dc, tb)], out=c.xT[:, dc, ts],
                     in0=c.xT[:, dc, ts], in1=t_[:, :], op=ALU.add)
        p.barrier()


POOL_W = (2, 4, 8, 16)


def emit_pool_out(c, up_d, icnt_d, wg_d, sc_d, wout_d):
    p = c.p
    T = c.T
    with contextlib.ExitStack() as st:
        mixT = p.sbuf("mixT", [128, KC, T], BF16, st)
        vT = p.sbuf("vT", [128, KC, T], BF16, st)
        icnt = p.sbuf("icnt", [128, 4 * T], F32, st)
        sc = p.sbuf("sc", [128, KC], F32, st)
        A = [[p.sbuf("A", [128, T + 16], F32, st) for _ in range(3)] for _ in range(2)]
        wgt = [p.sbuf("wgt", [128, 512], BF16, st) for _ in range(2)]
        wt = [p.sbuf("wpo", [128, 2048], BF16, st) for _ in range(3)]
        p.dma("sp", icnt[:, :], icnt_d, writes=[("icnt",)])
        p.dma("sp", sc[:, :], sc_d, writes=[("par",)])
        for cc in range(KC):
            gi = cc // 4
            b = cc % 2
            eng = "pool" if b == 0 else "dve"
            a0, a1, a2 = A[b]
            p.dma("sp", a0[:, :], up_d[cc], writes=[("A", b, 0)])
            p.op(eng, "tensor_tensor", reads=[("A", b, 0)], writes=[("A", b, 1)], out=a1[:, 1:T + 16],
                 in0=a0[:, 0:T + 15], in1=a0[:, 1:T + 16], op=ALU.add)
            cur, curk, oth, othk = a1, ("A", b, 1), a2, ("A", b, 2)
            lo, hi = 1, T + 16
            sh = 1
            for lvl in range(gi):
                nlo, nhi = lo + sh, hi - sh
                p.op(eng, "tensor_tensor", reads=[curk], writes=[othk], out=oth[:, nlo:nhi],
                     in0=cur[:, nlo - sh:nhi - sh], in1=cur[:, nlo + sh:nhi + sh], op=ALU.add)
                cur, curk, oth, othk = oth, othk, cur, curk
                lo, hi = nlo, nhi
                sh *= 2
            assert lo <= 8 and hi >= T + 8
            p.op(eng, "tensor_tensor", reads=[curk, ("icnt",)], writes=[othk], out=oth[:, 8:8 + T],
                 in0=cur[:, 8:8 + T], in1=icnt[:, gi * T:(gi + 1) * T], op=ALU.mult)
            for tb in range(c.NTB):
                p.op(eng, "tensor_tensor", reads=[othk, ("A", b, 0)], writes=[("mix", cc, tb)],
                     out=mixT[:, cc, tb * 512:(tb + 1) * 512], in0=oth[:, 8 + tb * 512:8 + (tb + 1) * 512],
                     in1=a0[:, 8 + tb * 512:8 + (tb + 1) * 512], op=ALU.subtract)
        for dc in range(KC):
            gi = dc // 4
            s = dc % 2
            c.load_w(wgt[s][:, :], ("wgt", s), wg_d[dc], 512)
            for kl in range(4):
                for tb in range(c.NTB):
                    ps, pk = c.bank(2 * (dc % 4) + tb)
                    p.op("pe", "matmul", reads=[("wgt", s), ("mix", gi * 4 + kl, tb)], writes=[pk],
                         out=ps, lhsT=wgt[s][:, kl * 128:(kl + 1) * 128],
                         rhs=mixT[:, gi * 4 + kl, tb * 512:(tb + 1) * 512], start=(kl == 0), stop=(kl == 3))
            for tb in range(c.NTB):
                ps, pk = c.bank(2 * (dc % 4) + tb)
                p.op("act", "activation", reads=[pk, ("par",)], writes=[("v", dc, tb)],
                     out=vT[:, dc, tb * 512:(tb + 1) * 512], in_=ps, func=AF.Identity, scale=sc[:, dc:dc + 1])
        c.nwt = 0
        emit_proj_resid(c, st, vT, "v", KC, wout_d, list(range(c.NTB)), 1.0, wt, "wpo")
        p.barrier()


def emit_final(c, gcol_d, out_d):
    p = c.p
    with contextlib.ExitStack() as st:
        oT = p.sbuf("oT", [128, KC, c.T], F32, st)
        gcol = p.sbuf("gcol", [128, KC], F32, st)
        p.dma("sp", gcol[:, :], gcol_d, writes=[("par",)])
        emit_rmsnorm(c, st, oT, gcol, hkey="o")
        for kc in range(KC):
            p.dma("sp", out_d[:, kc, :], oT[:, kc, :], reads=[("o", kc, tb) for tb in range(c.NTB)],
                  writes=[("final", kc)])
        p.barrier()


TT = 1024
_PROG_CACHE = {}


def build_tok_launch(stages):
    key = tuple(stages)
    if key in _PROG_CACHE:
        return _PROG_CACHE[key]
    p = Prog()
    c = Ctx(p, TT)
    ein = lambda n, sh, dt=F32: p.dram(n, sh, dt, kind="ExternalInput")
    eout = lambda n, sh, dt=F32: p.dram(n, sh, dt, kind="ExternalOutput")
    emit_load_x(c, ein("xin", [128, KC, TT]))
    has_final = False
    for i, stg in enumerate(stages):
        t = "_%d" % i
        if stg == "ffn":
            emit_ffn(c, ein("wg" + t, [FC, 128, 2048]), ein("wu" + t, [FC, 128, 2048]),
                     ein("wd" + t, [NQ, KC, 128, FQ * 128]), ein("gcol" + t, [128, KC]))
        elif stg == "ssd_in":
            emit_mix_in(c, ein("gcol" + t, [128, KC]), ein("w" + t, [81, 128, 2048]), eout("proj", [49, 128, TT]), 81, 32,
                        eout("sz", [NIC, 128, TT], BF16))
        elif stg == "pool_in":
            emit_mix_in(c, ein("gcol" + t, [128, KC]), ein("w" + t, [KC, 128, 2048]), eout("proj", [KC, 128, TT]), KC, 0)
        elif stg == "ssd_out":
            emit_ssd_out(c, ein("ym" + t, [NIC, 128, TT], BF16), ein("yb" + t, [NIC, 128, TT], BF16),
                         ein("sz" + t, [NIC, 128, TT], BF16),
                         ein("gn" + t, [128, NIC]), ein("wout" + t, [KC, 128, NIC * 128]))
        elif stg == "pool_out":
            emit_pool_out(c, ein("up" + t, [KC, 128, TT + 16]), ein("icnt" + t, [128, 4 * TT]),
                          ein("wgrp" + t, [KC, 128, 512]), ein("sc" + t, [128, KC]), ein("wout" + t, [KC, 128, 2048]))
        elif stg == "final":
            emit_final(c, ein("gcol" + t, [128, KC]), eout("final", [128, KC, TT]))
            has_final = True
    if not has_final:
        emit_store_x(c, eout("xout", [128, KC, TT]))
    nc = p.finish()
    _PROG_CACHE[key] = nc
    return nc


def build_ssd_core_launch():
    if "ssd_core" in _PROG_CACHE:
        return _PROG_CACHE["ssd_core"]
    p = Prog()
    ein = lambda n, sh, dt=F32: p.dram(n, sh, dt, kind="ExternalInput")
    eout = lambda n, sh, dt=F32: p.dram(n, sh, dt, kind="ExternalOutput")
    emit_ssd_core(p, ein("xbc", [NCC, 128, TS + 4]), ein("cw", [128, NCC * 5]), ein("cb", [128, NCC]),
                  ein("dtraw", [128, TS]), ein("dtpar", [128, 8]), ein("dbc", [128, NH]),
                  ein("cstf", [128, 128 + TS]), ein("cstb", [128, 384 + 64 * 128], BF16),
                  eout("ymain", [TS, NH * HD], BF16), eout("yboff", [TS, NH * HD], BF16))
    nc = p.finish()
    _PROG_CACHE["ssd_core"] = nc
    return nc


def _ssd_consts():
    import ml_dtypes
    identf = np.eye(128, dtype=np.float32)
    reset = np.ones((128, TS), np.float32)
    reset[:, ::128] = 0.0
    cstf = np.concatenate([identf, reset], axis=1)
    j = np.arange(128)[:, None]
    i = np.arange(128)[None, :]
    maskF = np.where(j > i, -MASKV, 0.0).astype(np.float32)
    maskB = np.where(j < i, MASKV, 0.0).astype(np.float32)
    sel = np.zeros((128, 64, 128), np.float32)
    for r in range(64):
        sel[r, r, :] = 1.0
    cstb = np.concatenate([identf, maskF, maskB, sel.reshape(128, 64 * 128)], axis=1).astype(ml_dtypes.bfloat16)
    return cstf, cstb


def _run(nc, in_maps):
    res = run_bass_kernel_spmd(nc, in_maps, core_ids=list(range(8)))
    return res.results


def _ffn_inputs(t, wg, wu, wd, g):
    return {"wg" + t: lay_w_stat(wg), "wu" + t: lay_w_stat(wu), "wd" + t: lay_wd(wd), "gcol" + t: lay_col(g)}


def kernel(x, ffn_norm, ffn_w_gate, ffn_w_up, ffn_w_down, mix_norm, ssd_w_in, ssd_conv_w, ssd_conv_b, ssd_dt_bias,
           ssd_a_log, ssd_d, ssd_norm, ssd_w_out, pool_w_in, pool_w_group, pool_scale, pool_w_out, final_norm):
    f = lambda a: np.ascontiguousarray(np.asarray(a, dtype=np.float32))
    x = f(x)
    B, L, _ = x.shape
    xs = [lay_xT(x[c // 2, (c % 2) * TT:(c % 2 + 1) * TT]) for c in range(8)]
    cstf, cstb = _ssd_consts()

    def ffn_in(t, l, i):
        return _ffn_inputs(t, f(ffn_w_gate[l, i]), f(ffn_w_up[l, i]), f(ffn_w_down[l, i]), f(ffn_norm[l, i]))

    def tok_launch(stages, shared, percore):
        nc = build_tok_launch(stages)
        in_maps = []
        for c in range(8):
            m = {"xin": xs[c]}
            m.update(shared)
            m.update(percore[c])
            in_maps.append(m)
        return _run(nc, in_maps)

    def ssd_core_stage(j, proj):
        nc = build_ssd_core_launch()
        cwf, cbf = f(ssd_conv_w[j]), f(ssd_conv_b[j])
        in_maps = []
        for c in range(8):
            b, hh = c // 2, c % 2
            P = np.concatenate([proj[2 * b].reshape(49 * 128, TT), proj[2 * b + 1].reshape(49 * 128, TT)], axis=1)
            rows = np.concatenate([np.arange(hh * 2048, (hh + 1) * 2048),
                                   np.arange(4096 + hh * 512, 4096 + (hh + 1) * 512),
                                   np.arange(5120 + hh * 512, 5120 + (hh + 1) * 512)])
            xbc = np.pad(P[rows], ((0, 0), (2, 2))).reshape(NCC, 128, TS + 4)
            cch = rows
            cw = np.ascontiguousarray(cwf[:, cch].T.reshape(NCC, 128, 5).transpose(1, 0, 2)).reshape(128, NCC * 5)
            cb = np.ascontiguousarray(cbf[cch].reshape(NCC, 128).T)
            drow = np.concatenate([6144 + d * 64 + hh * 32 + np.arange(32) for d in range(2)])
            dtr = P[drow]
            par = np.zeros((64, 8), np.float32)
            par[:, 0] = f(ssd_dt_bias[j])[:, hh * 32:(hh + 1) * 32].reshape(64)
            par[:, 1] = f(ssd_a_log[j])[:, hh * 32:(hh + 1) * 32].reshape(64)
            par[:32, 2] = -1.0
            par[32:, 2] = 1.0
            par[32:, 3] = 1.0
            par[:32, 4] = 1.0
            dbc = np.ascontiguousarray(np.broadcast_to(f(ssd_d[j])[None, hh * 32:(hh + 1) * 32], (128, NH)))
            in_maps.append({"xbc": np.ascontiguousarray(xbc), "cw": cw, "cb": cb,
                            "dtraw": np.ascontiguousarray(np.concatenate([dtr, dtr], axis=0)),
                            "dtpar": np.concatenate([par, par], axis=0), "dbc": dbc, "cstf": cstf, "cstb": cstb})
        res = _run(nc, in_maps)
        ym, yb = [], []
        for c in range(8):
            b, half = c // 2, c % 2
            for name, lst in (("ymain", ym), ("yboff", yb)):
                Y = np.concatenate([res[2 * b][name], res[2 * b + 1][name]], axis=1)
                lst.append(np.ascontiguousarray(Y[half * TT:(half + 1) * TT].T.reshape(NIC, 128, TT)))
        return ym, yb

    def pool_inputs(t, j, proj):
        per = []
        for c in range(8):
            b, half = c // 2, c % 2
            U = np.concatenate([proj[2 * b].reshape(D, TT), proj[2 * b + 1].reshape(D, TT)], axis=1)
            Up = np.pad(U, ((0, 0), (8, 8)))
            up = np.ascontiguousarray(Up[:, half * TT:half * TT + TT + 16]).reshape(KC, 128, TT + 16)
            tg = half * TT + np.arange(TT)
            ic = np.stack([1.0 / (np.clip(tg + w // 2, 0, L) - np.clip(tg - w // 2, 0, L)) for w in POOL_W])
            icnt = np.ascontiguousarray(np.broadcast_to(ic.reshape(1, 4 * TT), (128, 4 * TT))).astype(np.float32)
            per.append({"up" + t: up, "icnt" + t: icnt})
        shared = {"wgrp" + t: np.concatenate([lay_w_stat(f(pool_w_group[j, gi])) for gi in range(4)], axis=0),
                  "sc" + t: lay_col(f(pool_scale[j])), "wout" + t: lay_w_stat(f(pool_w_out[j]))}
        return shared, per

    empty = [dict() for _ in range(8)]
    sh = ffn_in("_0", 0, 0)
    sh.update({"gcol_1": lay_col(f(mix_norm[0])), "w_1": lay_w_stat(f(ssd_w_in[0]))})
    r = tok_launch(("ffn", "ssd_in"), sh, empty)
    xs = [r[c]["xout"] for c in range(8)]
    proj = [r[c]["proj"] for c in range(8)]
    szs = [r[c]["sz"] for c in range(8)]
    for j in range(2):
        l = 2 * j
        ym, yb = ssd_core_stage(j, proj)
        sh = {"gn_0": lay_col(f(ssd_norm[j])), "wout_0": lay_w_stat(f(ssd_w_out[j]))}
        sh.update(ffn_in("_1", l, 1))
        sh.update(ffn_in("_2", l + 1, 0))
        sh.update({"gcol_3": lay_col(f(mix_norm[l + 1])), "w_3": lay_w_stat(f(pool_w_in[j]))})
        per = [{"ym_0": ym[c], "yb_0": yb[c], "sz_0": szs[c]} for c in range(8)]
        r = tok_launch(("ssd_out", "ffn", "ffn", "pool_in"), sh, per)
        xs = [r[c]["xout"] for c in range(8)]
        proj = [r[c]["proj"] for c in range(8)]
        sh, per = pool_inputs("_0", j, proj)
        sh.update(ffn_in("_1", l + 1, 1))
        if j == 0:
            sh.update(ffn_in("_2", l + 2, 0))
            sh.update({"gcol_3": lay_col(f(mix_norm[l + 2])), "w_3": lay_w_stat(f(ssd_w_in[1]))})
            r = tok_launch(("pool_out", "ffn", "ffn", "ssd_in"), sh, per)
            xs = [r[c]["xout"] for c in range(8)]
            proj = [r[c]["proj"] for c in range(8)]
            szs = [r[c]["sz"] for c in range(8)]
        else:
            sh.update({"gcol_2": lay_col(f(final_norm))})
            r = tok_launch(("pool_out", "ffn", "final"), sh, per)
    out = np.empty((B, L, D), np.float32)
    for c in range(8):
        out[c // 2, (c % 2) * TT:(c % 2 + 1) * TT] = unlay_xT(r[c]["final"])
    return out
```

```python
import contextlib
import numpy as np
import concourse.bass as bass
import concourse.mybir as mybir
from concourse.bass_utils import run_bass_kernel_spmd

F32 = mybir.dt.float32
BF16 = mybir.dt.bfloat16
ALU = mybir.AluOpType
AF = mybir.ActivationFunctionType

D = 2048
KC = D // 128
DFF = 5632
FC = DFF // 128
NQ = 4
FQ = FC // NQ
EPS = 1e-6
NDMASEM = 24


class Prog:
    ENG = ("pe", "act", "dve", "pool", "sp")

    def __init__(self):
        self.nc = bass.Bass("TRN2", target_bir_lowering=False)
        self.es = contextlib.ExitStack()
        self.ops = {e: [] for e in self.ENG}
        self.cnt = {e: 0 for e in self.ENG}
        self.sem = {e: self.es.enter_context(self.nc.semaphore("s_" + e)) for e in self.ENG}
        self.dsem = [self.es.enter_context(self.nc.semaphore("d%d" % i)) for i in range(NDMASEM)]
        self.ndma = 0
        self.dma_last = [None] * NDMASEM
        self.waited = {e: {} for e in self.ENG}
        self.last_w = {}
        self.readers = {}
        self.all_tokens = []
        self.uid = 0

    def sbuf(self, name, shape, dtype, stack=None):
        self.uid += 1
        t = (stack or self.es).enter_context(self.nc.sbuf_tensor("%s_%d" % (name, self.uid), list(shape), dtype))
        return t

    def psum(self, name, shape, dtype, stack=None):
        self.uid += 1
        return (stack or self.es).enter_context(self.nc.psum_tensor("%s_%d" % (name, self.uid), list(shape), dtype))

    def dram(self, name, shape, dtype, kind="Internal"):
        return self.nc.dram_tensor(name, list(shape), dtype, kind=kind).ap()

    def _deps(self, reads, writes):
        deps = []
        for k in reads:
            if k in self.last_w:
                deps.append(self.last_w[k])
        for k in writes:
            if k in self.last_w:
                deps.append(self.last_w[k])
            deps.extend(self.readers.get(k, ()))
        return deps

    def _commit(self, tok, reads, writes):
        for k in reads:
            self.readers.setdefault(k, []).append(tok)
        for k in writes:
            self.last_w[k] = tok
            self.readers[k] = []

    def _waits(self, eng, deps, pe_skip=False):
        w = self.waited[eng]
        best = {}
        for (s, v, src) in deps:
            if pe_skip and src == "pe":
                continue
            if w.get(id(s), 0) >= v:
                continue
            if best.get(id(s), (None, 0))[1] < v:
                best[id(s)] = (s, v)
        out = []
        for sid, (s, v) in best.items():
            w[sid] = v
            out.append((s, v))
        return out

    def op(self, eng, meth, reads=(), writes=(), **kw):
        fn = (lambda e, meth=meth, kw=kw: getattr(e, meth)(**kw))
        deps = self._deps(reads, writes)
        waits = self._waits(eng, deps, pe_skip=(eng == "pe"))
        self.cnt[eng] += 1
        tok = (self.sem[eng], self.cnt[eng], eng)
        self.ops[eng].append((waits, fn, (self.sem[eng], 1)))
        self._commit(tok, reads, writes)
        return tok

    def dma(self, eng, out, in_, reads=(), writes=(), **kw):
        deps = self._deps(reads, writes)
        i = self.ndma % NDMASEM
        val = 16 * (self.ndma // NDMASEM + 1)
        self.ndma += 1
        if self.dma_last[i] is not None:
            deps.append(self.dma_last[i])
        waits = self._waits(eng, deps)
        tok = (self.dsem[i], val, "dma")
        self.dma_last[i] = tok
        self.ops[eng].append((waits, lambda e, out=out, in_=in_, kw=kw: e.dma_start(out=out, in_=in_, **kw),
                              (self.dsem[i], 16)))
        self._commit(tok, reads, writes)
        self.all_tokens.append(tok)
        return tok

    def barrier(self):
        toks = [(self.sem[e], self.cnt[e], e) for e in self.ENG if self.cnt[e] > 0]
        toks += [t for t in self.dma_last if t is not None]
        for e in self.ENG:
            waits = self._waits(e, toks)
            if waits:
                self.ops[e].append((waits, None, None))
        self.last_w.clear()
        self.readers.clear()

    def finish(self):
        self.barrier()
        with self.nc.Block() as block:
            def runner(name):
                def body(e):
                    for waits, fn, inc in self.ops[name]:
                        for (s, v) in waits:
                            e.wait_ge(s, v)
                        if fn is not None:
                            ins = fn(e)
                            ins.then_inc(inc[0], inc[1])
                return body
            block.tensor(runner("pe"))
            block.scalar(runner("act"))
            block.vector(runner("dve"))
            block.gpsimd(runner("pool"))
            block.sync(runner("sp"))
        self.es.close()
        return self.nc


class Ctx:
    def __init__(self, p, T):
        self.p = p
        self.T = T
        self.NTB = T // 512
        self.xT = p.sbuf("xT", [128, KC, T], F32)
        self.ones_bf = p.sbuf("ones_bf", [128, 128], BF16)
        p.op("dve", "memset", writes=[("ones",)], ap=self.ones_bf[:, :], constant=1.0)
        self.banks = [p.psum("bank%d" % i, [128, 512], F32) for i in range(8)]
        self.NSTG = 3
        self.two_q = False
        self.stg = [p.sbuf("stg", [128, 2048], F32) for _ in range(self.NSTG)]
        self.nstg = 0

    def bank(self, i):
        return self.banks[i][:, :], ("ps", i)

    def load_w(self, dst, dkey, src, ncols):
        p = self.p
        i = self.nstg % self.NSTG
        self.nstg += 1
        q = "sp" if (self.nstg % 2 == 0 or not self.two_q) else "act"
        p.dma(q, self.stg[i][:, 0:ncols], src, writes=[("stg", i)])
        p.op("pool", "tensor_copy", reads=[("stg", i)], writes=[dkey], out=dst, in_=self.stg[i][:, 0:ncols])


def emit_load_x(c, x_dram):
    p = c.p
    for kc in range(KC):
        p.dma("sp", c.xT[:, kc, :], x_dram[:, kc, :], writes=[("x", kc, tb) for tb in range(c.NTB)])


def emit_store_x(c, x_dram):
    p = c.p
    for kc in range(KC):
        p.dma("sp", x_dram[:, kc, :], c.xT[:, kc, :], reads=[("x", kc, tb) for tb in range(c.NTB)],
              writes=[("xout", kc)])


def emit_rmsnorm(c, st, hT, gcol, hkey="h"):
    p = c.p
    sq = [p.sbuf("sq", [128, 512], BF16, st) for _ in range(2)]
    rs = [p.sbuf("rs", [128, 512], F32, st) for _ in range(2)]
    for tb in range(c.NTB):
        ts = slice(tb * 512, (tb + 1) * 512)
        ps, pk = c.bank(tb)
        for kc in range(KC):
            s = sq[kc % 2]
            p.op("act", "activation", reads=[("x", kc, tb)], writes=[("sq", kc % 2)],
                 out=s[:, :], in_=c.xT[:, kc, ts], func=AF.Square)
            p.op("pe", "matmul", reads=[("sq", kc % 2), ("ones",)], writes=[pk],
                 out=ps, lhsT=c.ones_bf[:, :], rhs=s[:, :], start=(kc == 0), stop=(kc == KC - 1))
        r = rs[tb % 2]
        p.op("act", "activation", reads=[pk], writes=[("rs", tb % 2)],
             out=r[:, :], in_=ps, func=AF.Sqrt, scale=1.0 / D, bias=EPS)
        p.op("dve", "reciprocal", reads=[("rs", tb % 2)], writes=[("rs", tb % 2)], out=r[:, :], in_=r[:, :])
        for kc in range(KC):
            p.op("dve", "scalar_tensor_tensor", reads=[("x", kc, tb), ("rs", tb % 2), ("par",)],
                 writes=[(hkey, kc, tb)],
                 out=hT[:, kc, ts], in0=c.xT[:, kc, ts], scalar=gcol[:, kc:kc + 1], in1=r[:, :],
                 op0=ALU.mult, op1=ALU.mult)


def emit_ffn(c, wg_d, wu_d, wd_d, gcol_d):
    p = c.p
    T = c.T
    with contextlib.ExitStack() as st:
        hT = p.sbuf("hT", [128, KC, T], BF16, st)
        actq = p.sbuf("actq", [128, FQ, T], BF16, st)
        NS = 3
        wg = [p.sbuf("wg", [128, 2048], BF16, st) for _ in range(NS)]
        wu = [p.sbuf("wu", [128, 2048], BF16, st) for _ in range(NS)]
        NSD = 4
        wd = [p.sbuf("wd", [128, FQ * 128], BF16, st) for _ in range(NSD)]
        sg = [p.sbuf("sg", [128, 512], F32, st) for _ in range(2)]
        gcol = p.sbuf("gcol", [128, KC], F32, st)
        p.dma("sp", gcol[:, :], gcol_d, writes=[("par",)])
        emit_rmsnorm(c, st, hT, gcol)
        nsg = 0
        ndc = 0
        for q in range(NQ):
            for fl in range(FQ):
                fc = q * FQ + fl
                s = fc % NS
                c.load_w(wg[s][:, :], ("wg", s), wg_d[fc], 2048)
                c.load_w(wu[s][:, :], ("wu", s), wu_d[fc], 2048)
                base = 4 * (fc % 2)
                for kc in range(KC):
                    for tb in range(c.NTB):
                        ps, pk = c.bank(base + tb)
                        p.op("pe", "matmul", reads=[("wg", s), ("h", kc, tb)], writes=[pk],
                             out=ps, lhsT=wg[s][:, kc * 128:(kc + 1) * 128],
                             rhs=hT[:, kc, tb * 512:(tb + 1) * 512], start=(kc == 0), stop=(kc == KC - 1))
                    for tb in range(c.NTB):
                        ps, pk = c.bank(base + 2 + tb)
                        p.op("pe", "matmul", reads=[("wu", s), ("h", kc, tb)], writes=[pk],
                             out=ps, lhsT=wu[s][:, kc * 128:(kc + 1) * 128],
                             rhs=hT[:, kc, tb * 512:(tb + 1) * 512], start=(kc == 0), stop=(kc == KC - 1))
                for tb in range(c.NTB):
                    gps, gk = c.bank(base + tb)
                    ups, uk = c.bank(base + 2 + tb)
                    sgt = sg[nsg % 2]
                    sk = ("sg", nsg % 2)
                    nsg += 1
                    p.op("act", "activation", reads=[gk], writes=[sk], out=sgt[:, :], in_=gps, func=AF.Silu)
                    p.op("dve", "tensor_tensor", reads=[sk, uk], writes=[("actq", fl, tb)],
                         out=actq[:, fl, tb * 512:(tb + 1) * 512], in0=sgt[:, :], in1=ups, op=ALU.mult)
            for dc in range(KC):
                s = ndc % NSD
                c.load_w(wd[s][:, :], ("wd", s), wd_d[q, dc], FQ * 128)
                base = 2 * (ndc % 4)
                ndc += 1
                for fl in range(FQ):
                    for tb in range(c.NTB):
                        ps, pk = c.bank(base + tb)
                        p.op("pe", "matmul", reads=[("wd", s), ("actq", fl, tb)], writes=[pk],
                             out=ps, lhsT=wd[s][:, fl * 128:(fl + 1) * 128],
                             rhs=actq[:, fl, tb * 512:(tb + 1) * 512], start=(fl == 0), stop=(fl == FQ - 1))
                for tb in range(c.NTB):
                    ps, pk = c.bank(base + tb)
                    ts = slice(tb * 512, (tb + 1) * 512)
                    p.op("dve", "scalar_tensor_tensor", reads=[pk, ("x", dc, tb)], writes=[("x", dc, tb)],
                         out=c.xT[:, dc, ts], in0=ps, scalar=0.5, in1=c.xT[:, dc, ts],
                         op0=ALU.mult, op1=ALU.add)
        p.barrier()


def lay_w_stat(w):
    K, M = w.shape
    return np.ascontiguousarray(w.reshape(K // 128, 128, M // 128, 128).transpose(2, 1, 0, 3)).reshape(
        M // 128, 128, K)


def lay_wd(w):
    return np.ascontiguousarray(w.reshape(NQ, FQ, 128, KC, 128).transpose(0, 3, 2, 1, 4)).reshape(
        NQ, KC, 128, FQ * 128)


def lay_col(v):
    return np.ascontiguousarray(v.reshape(-1, 128).T)


def lay_xT(xtok):
    T = xtok.shape[0]
    return np.ascontiguousarray(xtok.T.reshape(KC, 128, T).transpose(1, 0, 2))


def unlay_xT(xT):
    T = xT.shape[2]
    return np.ascontiguousarray(xT.transpose(1, 0, 2).reshape(D, T).T)


TS = 2048
NCH = TS // 128
NH = 32
NG = 4
HPG = 8
HD = 64
NST = 128
NXC = NH * HD // 128
NCC = NXC + 2 * NG
MASKV = 30000.0


def emit_ssd_core(p, xbc_d, cw_d, cb_d, dtraw_d, dtpar_d, dbc_d, cstf_d, cstb_d, ymain_d, yboff_d):
    es = contextlib.ExitStack()
    with es:
        cstb = p.sbuf("cstb", [128, 384], BF16, es)
        SEL = p.sbuf("SEL", [128, 64 * 128], BF16, es)
        onesb = p.sbuf("onesb", [128, 128], BF16, es)
        dtpar = p.sbuf("dtpar", [128, 8], F32, es)
        dbc = p.sbuf("dbc", [128, NH], F32, es)
        cw = p.sbuf("cw", [128, NCC * 5], F32, es)
        cbias = p.sbuf("cbias", [128, NCC], F32, es)
        Rhl = p.sbuf("Rhl", [128, 2, TS], BF16, es)
        tokA = p.sbuf("tokA", [128, NCH, 128], F32, es)
        tokB = p.sbuf("tokB", [128, NCH, 128], F32, es)
        decbc = p.sbuf("decbc", [128, NCH, 64], F32, es)
        p.dma("sp", cstb[:, :], cstb_d[:, 0:384], writes=[("cstb",)])
        p.dma("sp", SEL[:, :], cstb_d[:, 384:384 + 64 * 128], writes=[("SEL",)])
        p.dma("sp", dtpar[:, :], dtpar_d, writes=[("dtpar",)])
        p.dma("sp", dbc[:, :], dbc_d, writes=[("dbc",)])
        p.dma("sp", cw[:, :], cw_d, writes=[("cw",)])
        p.dma("sp", cbias[:, :], cb_d, writes=[("cbias",)])
        p.op("dve", "memset", writes=[("onesb",)], ap=onesb[:, :], constant=1.0)
        identb = cstb[:, 0:128]
        maskF = cstb[:, 128:256]
        maskB = cstb[:, 256:384]

        with contextlib.ExitStack() as sa:
            reset = p.sbuf("reset", [128, TS], F32, sa)
            Rt = p.sbuf("Rt", [128, TS], F32, sa)
            hl = p.sbuf("hl", [128, 3, 2, TS], BF16, sa)
            dtr = p.sbuf("dtr", [128, TS], F32, sa)
            dt = p.sbuf("dt", [128, TS], F32, sa)
            dta = p.sbuf("dta", [128, TS], F32, sa)
            cs = p.sbuf("cs", [128, TS], F32, sa)
            Qt = p.sbuf("Qt", [128, TS], F32, sa)
            eR = p.sbuf("eR", [128, TS], F32, sa)
            eQ = p.sbuf("eQ", [128, TS], F32, sa)
            QA = p.sbuf("QA", [128, TS], F32, sa)
            QB = p.sbuf("QB", [128, TS], F32, sa)
            acol = p.sbuf("acol", [128, 2], F32, sa)
            psA = [p.psum("psA", [128, 512], F32, sa) for _ in range(2)]
            psB = [p.psum("psB", [128, 512], F32, sa) for _ in range(2)]
            p.dma("sp", reset[:, :], cstf_d[:, 128:128 + TS], writes=[("reset",)])
            p.dma("sp", dtr[:, :], dtraw_d, writes=[("dtr",)])
            p.op("act", "activation", reads=[("dtpar",)], writes=[("acol",)], out=acol[:, 0:1], in_=dtpar[:, 1:2],
                 func=AF.Exp)
            p.op("dve", "tensor_scalar", reads=[("acol",)], writes=[("acol2",)], out=acol[:, 1:2], in0=acol[:, 0:1],
                 scalar1=-1.0, scalar2=None, op0=ALU.mult)
            p.op("act", "activation", reads=[("dtr",), ("dtpar",)], writes=[("dt",)], out=dt[:, :], in_=dtr[:, :],
                 func=AF.Exp, bias=dtpar[:, 0:1], scale=1.0)
            p.op("act", "activation", reads=[("dt",)], writes=[("dt",)], out=dt[:, :], in_=dt[:, :],
                 func=AF.Ln, bias=1.0, scale=1.0)
            p.op("dve", "tensor_scalar", reads=[("dt",), ("acol2",)], writes=[("dta",)], out=dta[:, :], in0=dt[:, :],
                 scalar1=acol[:, 1:2], scalar2=None, op0=ALU.mult)
            p.op("dve", "tensor_tensor_scan", reads=[("reset",), ("dta",)], writes=[("cs",)], out=cs[:, :],
                 data0=reset[:, :], data1=dta[:, :], initial=0.0, op0=ALU.mult, op1=ALU.add)
            p.op("dve", "tensor_scalar", reads=[("dta",), ("dtpar",)], writes=[("Qt",)], out=Qt[:, :], in0=dta[:, :],
                 scalar1=dtpar[:, 3:4], scalar2=None, op0=ALU.mult)
            p.op("dve", "tensor_tensor", reads=[("cs",), ("Qt",)], writes=[("R",)], out=Rt[:, :], in0=cs[:, :],
                 in1=Qt[:, :], op=ALU.subtract)
            tot_bc = cs[:, :].rearrange("p (c q) -> p c q", q=128)[:, :, 127:128].to_broadcast([128, NCH, 128])
            p.op("dve", "tensor_tensor", reads=[("cs",), ("R",)], writes=[("Qt",)],
                 out=Qt[:, :].rearrange("p (c q) -> p c q", q=128), in0=tot_bc,
                 in1=Rt[:, :].rearrange("p (c q) -> p c q", q=128), op=ALU.subtract)
            p.op("act", "activation", reads=[("R",)], writes=[("eR",)], out=eR[:, :], in_=Rt[:, :], func=AF.Exp)
            p.op("act", "activation", reads=[("Qt",)], writes=[("eQ",)], out=eQ[:, :], in_=Qt[:, :], func=AF.Exp)
            p.op("dve", "tensor_copy", reads=[("dt",)], writes=[("QA0",)], out=QA[0:64, :], in_=dt[0:64, :])
            p.op("dve", "tensor_scalar", reads=[("R",), ("dtpar",)], writes=[("QA1",)], out=QA[64:128, :],
                 in0=Rt[64:128, :], scalar1=dtpar[64:128, 2:3], scalar2=None, op0=ALU.mult)
            p.op("dve", "tensor_scalar", reads=[("eQ",), ("dtpar",)], writes=[("QB0",)], out=QB[0:64, :],
                 in0=eQ[0:64, :], scalar1=dtpar[0:64, 4:5], scalar2=None, op0=ALU.mult)
            p.op("dve", "scalar_tensor_tensor", reads=[("eR",), ("QB0",), ("dtpar",)], writes=[("QB0",)],
                 out=QB[0:64, :], in0=eR[0:64, :], scalar=dtpar[0:64, 3:4], in1=QB[0:64, :],
                 op0=ALU.mult, op1=ALU.add)
            p.op("dve", "tensor_tensor", reads=[("QB0",), ("dt",)], writes=[("QB0",)], out=QB[0:64, :],
                 in0=QB[0:64, :], in1=dt[0:64, :], op=ALU.mult)
            p.op("dve", "tensor_scalar", reads=[("eR",), ("dtpar",)], writes=[("QB1",)], out=QB[64:128, :],
                 in0=eR[64:128, :], scalar1=dtpar[64:128, 4:5], scalar2=None, op0=ALU.mult)
            p.op("dve", "scalar_tensor_tensor", reads=[("eQ",), ("QB1",), ("dtpar",)], writes=[("QB1",)],
                 out=QB[64:128, :], in0=eQ[64:128, :], scalar=dtpar[64:128, 3:4], in1=QB[64:128, :],
                 op0=ALU.mult, op1=ALU.add)
            def split(src, skeys, dst_hi, dst_lo, dkey, scratch, sckey):
                p.op("act", "activation", reads=skeys, writes=[(dkey, 0)], out=dst_hi, in_=src, func=AF.Identity)
                p.op("dve", "tensor_tensor", reads=skeys + [(dkey, 0)], writes=[sckey], out=scratch, in0=src,
                     in1=dst_hi, op=ALU.subtract)
                p.op("act", "activation", reads=[sckey], writes=[(dkey, 1)], out=dst_lo, in_=scratch, func=AF.Identity)
            split(QA[:, :], [("QA0",), ("QA1",)], hl[:, 0, 0, :], hl[:, 0, 1, :], "hlA", eR[:, :], ("eR",))
            split(QB[:, :], [("QB0",), ("QB1",), ("eR",)], hl[:, 1, 0, :], hl[:, 1, 1, :], "hlB", eQ[:, :], ("eQ",))
            split(Rt[:, :], [("R",), ("eQ",)], Rhl[:, 0, :], Rhl[:, 1, :], "Rhl", Qt[:, :], ("Qt",))
            identb_ = cstb[:, 0:128]
            Dm = dt[:, 0:NCH * 64]
            Dm3 = Dm.rearrange("p (c r) -> p c r", r=64)
            tot3 = cs[:, :].rearrange("p (c q) -> p c q", q=128)[:, :, 127:128].to_broadcast([128, NCH, 64])
            id3 = cstb[:, 0:64].rearrange("p (o r) -> p o r", o=1).to_broadcast([128, NCH, 64])
            p.op("dve", "tensor_tensor", reads=[("cs",), ("cstb",), ("hlA", 0), ("hlA", 1), ("hlB", 0), ("hlB", 1)],
                 writes=[("Dm",)], out=Dm3, in0=tot3, in1=id3, op=ALU.mult)
            Dhl = hl[:, 2, :, 0:NCH * 64]
            split(Dm, [("Dm",), ("Qt",), ("Rhl", 1)], Dhl[:, 0, :], Dhl[:, 1, :], "Dhl", Qt[:, 0:NCH * 64], ("Qt",))
            for half in range(2):
                pb = psB[half]
                for w in range(2):
                    p.op("pe", "matmul", reads=[("Dhl", w), ("onesb",)], writes=[("psB", half)], out=pb[:, :],
                         lhsT=onesb[:, :], rhs=Dhl[:, w, half * 512:(half + 1) * 512], start=(w == 0), stop=(w == 1))
                p.op("act", "activation", reads=[("psB", half)], writes=[("decbc", half)],
                     out=decbc[:, half * 8:(half + 1) * 8, :],
                     in_=pb[:, :].rearrange("p (c r) -> p c r", r=64), func=AF.Exp)
            for c in range(NCH):
                cs_ = slice(c * 128, (c + 1) * 128)
                pa = psA[c % 2]
                pak = ("psA", c % 2)
                for q in range(2):
                    for w in range(2):
                        p.op("pe", "matmul", reads=[("hlA", w), ("hlB", w), ("cstb",)], writes=[pak],
                             out=pa[:, q * 128:(q + 1) * 128], lhsT=hl[:, q, w, cs_], rhs=identb_,
                             start=(w == 0), stop=(w == 1))
                p.op("act", "activation", reads=[pak], writes=[("tokA", c)], out=tokA[:, c, :], in_=pa[:, 0:128],
                     func=AF.Identity)
                p.op("act", "activation", reads=[pak], writes=[("tokB", c)], out=tokB[:, c, :], in_=pa[:, 128:256],
                     func=AF.Identity)
            p.barrier()
        _emit_ssd_main(p, es, xbc_d, ymain_d, yboff_d, SEL, identb, maskF, maskB, dbc, cw, cbias, Rhl, tokA, tokB,
                       decbc)
        p.barrier()


def _emit_ssd_main(p, es, xbc_d, ymain_d, yboff_d, SEL, identb, maskF, maskB, dbc, cw, cbias, Rhl, tokA, tokB,
                   decbc):
    x_tok = p.sbuf("x_tok", [128, NCH, NH * HD], BF16, es)
    B_tok = p.sbuf("B_tok", [128, NCH, NG * NST], BF16, es)
    BT = p.sbuf("BT", [128, NG, TS], BF16, es)
    CT = p.sbuf("CT", [128, NG, TS], BF16, es)
    with contextlib.ExitStack() as sb:
        u = [p.sbuf("u", [128, TS + 4], F32, sb) for _ in range(2)]
        acc = [p.sbuf("acc", [128, TS], F32, sb) for _ in range(2)]
        fm = [p.sbuf("fm", [128, TS], BF16, sb) for _ in range(2)]
        pst = [p.psum("pst", [128, 1024], F32, sb) for _ in range(2)]
        npst = 0
        for cc in range(NCC):
            b = cc % 2
            p.dma("sp", u[b][:, :], xbc_d[cc], writes=[("u", b)])
            p.op("dve", "tensor_scalar", reads=[("u", b), ("cw",), ("cbias",)], writes=[("acc", b)],
                 out=acc[b][:, :], in0=u[b][:, 0:TS], scalar1=cw[:, cc * 5:cc * 5 + 1],
                 scalar2=cbias[:, cc:cc + 1], op0=ALU.mult, op1=ALU.add)
            for j in range(1, 5):
                p.op("dve", "scalar_tensor_tensor", reads=[("u", b), ("cw",), ("acc", b)], writes=[("acc", b)],
                     out=acc[b][:, :], in0=u[b][:, j:j + TS], scalar=cw[:, cc * 5 + j:cc * 5 + j + 1],
                     in1=acc[b][:, :], op0=ALU.mult, op1=ALU.add)
            if cc < NXC:
                dst, dk = fm[b][:, :], ("fm", b)
            elif cc < NXC + NG:
                dst, dk = BT[:, cc - NXC, :], ("BT", cc - NXC)
            else:
                dst, dk = CT[:, cc - NXC - NG, :], ("CT", cc - NXC - NG)
            p.op("act", "activation", reads=[("acc", b)], writes=[dk], out=dst, in_=acc[b][:, :], func=AF.Silu)
            if cc < NXC + NG:
                for c4 in range(NCH // 8):
                    pt = pst[npst % 2]
                    pk = ("pst", npst % 2)
                    npst += 1
                    for k in range(8):
                        c = c4 * 8 + k
                        p.op("pe", "matmul", reads=[dk, ("cstb",)], writes=[pk],
                             out=pt[:, k * 128:(k + 1) * 128], lhsT=dst[:, c * 128:(c + 1) * 128], rhs=identb,
                             start=True, stop=True)
                    if cc < NXC:
                        o = x_tok[:, c4 * 8:(c4 + 1) * 8, cc * 128:(cc + 1) * 128]
                        ok = [("x_tok", c) for c in range(c4 * 8, c4 * 8 + 8)]
                    else:
                        g = cc - NXC
                        o = B_tok[:, c4 * 8:(c4 + 1) * 8, g * 128:(g + 1) * 128]
                        ok = [("B_tok", c) for c in range(c4 * 8, c4 * 8 + 8)]
                    p.op("pool" if False else "act", "activation", reads=[pk], writes=ok, out=o,
                         in_=pt[:, :].rearrange("p (k q) -> p k q", q=128), func=AF.Identity)
        p.barrier()

    with contextlib.ExitStack() as sc:
        Hs = [p.sbuf("H", [128, NG, 512], F32, sc) for _ in range(2)]
        Hbs = [p.sbuf("Hb", [128, NG, 512], BF16, sc) for _ in range(2)]
        xws = [p.sbuf("xw", [128, NH * HD], BF16, sc)] * 2
        yt = [p.sbuf("yt", [128, NH * HD], BF16, sc)] * 2
        ytb = [p.sbuf("ytb", [128, NH * HD], BF16, sc)] * 2
        tmp = [p.sbuf("tmp", [128, 512], F32, sc) for _ in range(2)]
        tmp2 = [p.sbuf("tmp2", [128, 512], F32, sc) for _ in range(2)]
        Lt = [p.sbuf("Lt", [128, 128], BF16, sc) for _ in range(4)]
        Mt = [p.sbuf("Mt", [128, 128], BF16, sc) for _ in range(8)]
        cbt = [p.sbuf("cbt", [128, NG * 128], BF16, sc) for _ in range(2)]
        ps_seg = [p.psum("ps_seg", [128, 512], F32, sc) for _ in range(2)]
        ps_cb = p.psum("ps_cb", [128, 512], F32, sc)
        ps_y = [p.psum("ps_y", [128, 512], F32, sc) for _ in range(2)]
        ps_g = [p.psum("ps_g", [128, 512], F32, sc) for _ in range(2)]
        ps_s = p.psum("ps_s", [128, 512], F32, sc)
        cnt = {"g": 0, "t": 0, "l": 0}

        def hview(d, g):
            return Hs[d][:, g, :].rearrange("p (h d) -> p h d", d=HD)

        def state_update(c, d, nxw):
            xwt = xws[d]
            xk = ("xw", 0)
            H, Hb = Hs[d], Hbs[d]
            wst = tokB[:, c, d * NH:(d + 1) * NH].rearrange("p (h o) -> p h o", o=1).to_broadcast([128, NH, HD])
            p.op("pool", "tensor_tensor", reads=[("x_tok", c), ("tokB", c)], writes=[xk],
                 out=xwt[:, :].rearrange("p (h d) -> p h d", d=HD),
                 in0=x_tok[:, c, :].rearrange("p (h d) -> p h d", d=HD), in1=wst, op=ALU.mult)
            for g in range(NG):
                p.op("pe", "matmul", reads=[("B_tok", c), xk], writes=[("ps_s",)], out=ps_s[:, :],
                     lhsT=B_tok[:, c, g * 128:(g + 1) * 128], rhs=xwt[:, g * 512:(g + 1) * 512],
                     start=True, stop=True)
                dec = decbc[:, c, d * NH + g * HPG:d * NH + (g + 1) * HPG].rearrange(
                    "p (h o) -> p h o", o=1).to_broadcast([128, HPG, HD])
                p.op("pool", "tensor_tensor", reads=[("H", d, g), ("decbc", c // 8)], writes=[("H", d, g)], out=hview(d, g),
                     in0=hview(d, g), in1=dec, op=ALU.mult)
                p.op("dve", "tensor_tensor", reads=[("H", d, g), ("ps_s",)], writes=[("H", d, g)], out=H[:, g, :],
                     in0=H[:, g, :], in1=ps_s[:, :], op=ALU.add)
                p.op("act", "activation", reads=[("H", d, g)], writes=[("Hb", d, g)], out=Hb[:, g, :], in_=H[:, g, :],
                     func=AF.Identity)

        def g_term(c, d, g):
            pg = ps_g[cnt["g"] % 2]
            gk = ("ps_g", cnt["g"] % 2)
            cnt["g"] += 1
            p.op("pe", "matmul", reads=[("CT", g), ("Hb", d, g)], writes=[gk], out=pg[:, :],
                 lhsT=CT[:, g, c * 128:(c + 1) * 128], rhs=Hbs[d][:, g, :], start=True, stop=True)
            t = tmp[cnt["t"] % 2]
            tk = ("tmp", cnt["t"] % 2)
            cnt["t"] += 1
            eo = tokB[:, c, 64 + d * NH + g * HPG:64 + d * NH + (g + 1) * HPG].rearrange(
                "p (h o) -> p h o", o=1).to_broadcast([128, HPG, HD])
            p.op("dve", "tensor_tensor", reads=[gk, ("tokB", c)], writes=[tk],
                 out=t[:, :].rearrange("p (h d) -> p h d", d=HD),
                 in0=pg[:, :].rearrange("p (h d) -> p h d", d=HD), in1=eo, op=ALU.mult)
            return t, tk

        for d in range(2):
            for g in range(NG):
                p.op("pool", "memset", writes=[("H", d, g)], ap=Hs[d][:, g, :], constant=0.0)
                p.op("pool", "memset", writes=[("Hb", d, g)], ap=Hbs[d][:, g, :], constant=0.0)

        def bwd_step(c):
            ytile = ytb[c % 2]
            for g in range(NG):
                t, tk = g_term(c, 1, g)
                p.op("pool", "tensor_copy", reads=[tk], writes=[("ytb", 0, g)],
                     out=ytile[:, g * 512:(g + 1) * 512], in_=t[:, :])
            p.dma("sp", yboff_d[c * 128:(c + 1) * 128, :], ytile[:, :], reads=[("ytb", 0, g) for g in range(NG)],
                  writes=[("yboff", c)])
            if c > 0:
                state_update(c, 1, 0)

        nseg = 0
        for c in range(NCH):
            bwd_step(NCH - 1 - c)
            cs_ = slice(c * 128, (c + 1) * 128)
            ytile = yt[c % 2]
            cb_t = cbt[c % 2]
            for g in range(NG):
                p.op("pe", "matmul", reads=[("BT", g), ("CT", g)], writes=[("ps_cb",)],
                     out=ps_cb[:, g * 128:(g + 1) * 128], lhsT=BT[:, g, cs_], rhs=CT[:, g, cs_],
                     start=True, stop=True)
            p.op("act", "activation", reads=[("ps_cb",)], writes=[("cbt", c % 2)], out=cb_t[:, :], in_=ps_cb[:, :],
                 func=AF.Identity)
            for g in range(NG):
                py = ps_y[g % 2]
                yk = ("ps_y", g % 2)
                pend = []
                for hp in range(HPG // 2):
                    bank = nseg % 2
                    nseg += 1
                    sk = ("ps_seg", bank)
                    units = [(hp * 2 + u // 2, u % 2) for u in range(4)]
                    for u, (hl, d) in enumerate(units):
                        r = d * NH + g * HPG + hl
                        pseg = ps_seg[bank][:, u * 128:(u + 1) * 128]
                        for w in range(2):
                            p.op("pe", "matmul", reads=[("Rhl", w), ("SEL",)], writes=[sk], out=pseg,
                                 lhsT=SEL[:, r * 128:(r + 1) * 128], rhs=Rhl[:, w, cs_],
                                 start=(w == 0), stop=False)
                        p.op("pe", "matmul", reads=[("cstb",)], writes=[sk], out=pseg, lhsT=identb,
                             rhs=(maskF if d == 0 else maskB), start=False, stop=True)
                    for args in pend:
                        p.op("pe", "matmul", **args)
                    pend = []
                    for u, (hl, d) in enumerate(units):
                        h = g * HPG + hl
                        r = d * NH + h
                        pseg = ps_seg[bank][:, u * 128:(u + 1) * 128]
                        li = cnt["l"] % 4
                        mi = cnt["l"] % 8
                        cnt["l"] += 1
                        p.op("act", "activation", reads=[sk, ("tokA", c)], writes=[("Lt", li)], out=Lt[li][:, :],
                             in_=pseg, func=AF.Exp, bias=tokA[:, c, 64 + r:64 + r + 1],
                             scale=(1.0 if d == 0 else -1.0))
                        p.op("dve", "scalar_tensor_tensor", reads=[("Lt", li), ("tokA", c), ("cbt", c % 2)],
                             writes=[("Mt", mi)], out=Mt[mi][:, :], in0=Lt[li][:, :], scalar=tokA[:, c, r:r + 1],
                             in1=cb_t[:, g * 128:(g + 1) * 128], op0=ALU.mult, op1=ALU.mult)
                        pend.append(dict(reads=[("Mt", mi), ("x_tok", c)], writes=[yk],
                                         out=py[:, hl * HD:(hl + 1) * HD], lhsT=Mt[mi][:, :],
                                         rhs=x_tok[:, c, h * HD:(h + 1) * HD], start=(d == 0), stop=(d == 1)))
                for args in pend:
                    p.op("pe", "matmul", **args)
                t, tk = g_term(c, 0, g)
                t2 = tmp2[g % 2]
                dv = dbc[:, g * HPG:(g + 1) * HPG].rearrange("p (h o) -> p h o", o=1).to_broadcast([128, HPG, HD])
                p.op("pool", "tensor_tensor", reads=[("x_tok", c), ("dbc",)], writes=[("tmp2", g % 2)],
                     out=t2[:, :].rearrange("p (h d) -> p h d", d=HD),
                     in0=x_tok[:, c, g * 512:(g + 1) * 512].rearrange("p (h d) -> p h d", d=HD), in1=dv, op=ALU.mult)
                p.op("pool", "tensor_tensor", reads=[("tmp2", g % 2), tk], writes=[("tmp2", g % 2)], out=t2[:, :],
                     in0=t2[:, :], in1=t[:, :], op=ALU.add)
                p.op("dve", "tensor_tensor", reads=[("tmp2", g % 2), yk], writes=[("yt", 0, g)],
                     out=ytile[:, g * 512:(g + 1) * 512], in0=t2[:, :], in1=py[:, :], op=ALU.add)
            p.dma("sp", ymain_d[c * 128:(c + 1) * 128, :], ytile[:, :], reads=[("yt", 0, g) for g in range(NG)],
                  writes=[("ymain", c)])
            if c < NCH - 1:
                state_update(c, 0, 0)
        p.barrier()


def emit_norm_generic(c, st, src, skey, nk, gcol, dst, dkey, tbs, dim):
    p = c.p
    sq = [p.sbuf("sq", [128, 512], BF16, st) for _ in range(2)]
    rs = [p.sbuf("rs", [128, 512], F32, st) for _ in range(2)]
    for i, tb in enumerate(tbs):
        ts = slice(i * 512, (i + 1) * 512)
        ps, pk = c.bank(i % 2)
        for k in range(nk):
            s = sq[k % 2]
            p.op("act", "activation", reads=[(skey, k, tb)], writes=[("sq", k % 2)],
                 out=s[:, :], in_=src[:, k, ts], func=AF.Square)
            p.op("pe", "matmul", reads=[("sq", k % 2), ("ones",)], writes=[pk],
                 out=ps, lhsT=c.ones_bf[:, :], rhs=s[:, :], start=(k == 0), stop=(k == nk - 1))
        r = rs[i % 2]
        p.op("act", "activation", reads=[pk], writes=[("rs", i % 2)],
             out=r[:, :], in_=ps, func=AF.Sqrt, scale=1.0 / dim, bias=EPS)
        p.op("dve", "reciprocal", reads=[("rs", i % 2)], writes=[("rs", i % 2)], out=r[:, :], in_=r[:, :])
        for k in range(nk):
            p.op("dve", "scalar_tensor_tensor", reads=[(skey, k, tb), ("rs", i % 2), ("par",)],
                 writes=[(dkey, k, tb)],
                 out=dst[:, k, ts], in0=src[:, k, ts], scalar=gcol[:, k:k + 1], in1=r[:, :],
                 op0=ALU.mult, op1=ALU.mult)


def emit_proj_out(c, st, hT, w_d, out_d, ncc, nsilu=0, bf_d=None):
    p = c.p
    NS = 3
    wt = [p.sbuf("wt", [128, 2048], BF16, st) for _ in range(NS)]
    ot = [p.sbuf("ot", [128, 512], F32, st) for _ in range(4)]
    otb = [p.sbuf("otb", [128, 512], BF16, st) for _ in range(4)] if bf_d is not None else None
    no = 0
    for cc in range(ncc):
        s = cc % NS
        c.load_w(wt[s][:, :], ("wt", s), w_d[cc], 2048)
        for kc in range(KC):
            for tb in range(c.NTB):
                ps, pk = c.bank(2 * (cc % 4) + tb)
                p.op("pe", "matmul", reads=[("wt", s), ("h", kc, tb)], writes=[pk],
                     out=ps, lhsT=wt[s][:, kc * 128:(kc + 1) * 128], rhs=hT[:, kc, tb * 512:(tb + 1) * 512],
                     start=(kc == 0), stop=(kc == KC - 1))
        for tb in range(c.NTB):
            ps, pk = c.bank(2 * (cc % 4) + tb)
            if cc < nsilu:
                o, ok, dst = otb[no % 4], ("otb", no % 4), bf_d[cc, :, tb * 512:(tb + 1) * 512]
            else:
                o, ok, dst = ot[no % 4], ("ot", no % 4), out_d[cc - nsilu, :, tb * 512:(tb + 1) * 512]
            no += 1
            p.op("act", "activation", reads=[pk], writes=[ok], out=o[:, :], in_=ps,
                 func=(AF.Silu if cc < nsilu else AF.Identity))
            p.dma("act", dst, o[:, :], reads=[ok], writes=[("projout", cc, tb)])


def emit_proj_resid(c, st, srcT, skey, nk, w_d, tbs, scale, wt, wkey_base):
    p = c.p
    nw = len(wt)
    for dc in range(KC):
        s = c.nwt % nw
        c.nwt += 1
        for a in range(0, nk * 128, 2048):
            b = min(a + 2048, nk * 128)
            c.load_w(wt[s][:, a:b], (wkey_base, s, a), w_d[dc, :, a:b], b - a)
        wkeys = [(wkey_base, s, a) for a in range(0, nk * 128, 2048)]
        for k in range(nk):
            for i, tb in enumerate(tbs):
                ps, pk = c.bank(2 * (dc % 4) + (i % 2))
                p.op("pe", "matmul", reads=wkeys + [(skey, k, tb)], writes=[pk],
                     out=ps, lhsT=wt[s][:, k * 128:(k + 1) * 128], rhs=srcT[:, k, i * 512:(i + 1) * 512],
                     start=(k == 0), stop=(k == nk - 1))
        for i, tb in enumerate(tbs):
            ps, pk = c.bank(2 * (dc % 4) + (i % 2))
            ts = slice(tb * 512, (tb + 1) * 512)
            p.op("dve", "scalar_tensor_tensor", reads=[pk, ("x", dc, tb)], writes=[("x", dc, tb)],
                 out=c.xT[:, dc, ts], in0=ps, scalar=scale, in1=c.xT[:, dc, ts], op0=ALU.mult, op1=ALU.add)


def emit_mix_in(c, gcol_d, w_d, out_d, ncc, nsilu, bf_d=None):
    p = c.p
    with contextlib.ExitStack() as st:
        hT = p.sbuf("hT", [128, KC, c.T], BF16, st)
        gcol = p.sbuf("gcol", [128, KC], F32, st)
        p.dma("sp", gcol[:, :], gcol_d, writes=[("par",)])
        emit_rmsnorm(c, st, hT, gcol)
        emit_proj_out(c, st, hT, w_d, out_d, ncc, nsilu, bf_d)
        p.barrier()


NIC = 32


def emit_ssd_out(c, ym_d, yb_d, sz_d, gn_d, wout_d):
    p = c.p
    T = c.T
    with contextlib.ExitStack() as st:
        gs = p.sbuf("gs", [128, NIC, T], BF16, st)
        KB = 2
        ld = [p.sbuf("ld", [128, 3, KB, 512], BF16, st) for _ in range(3)]
        g32 = [p.sbuf("g32", [128, KB, 512], F32, st) for _ in range(2)]
        sq = [p.sbuf("sq", [128, KB, 512], BF16, st) for _ in range(2)]
        rs = [p.sbuf("rs", [128, 512], F32, st) for _ in range(c.NTB)]
        tmpo = [p.sbuf("tmpo", [128, 512], F32, st) for _ in range(2)]
        gcol = p.sbuf("gcol", [128, NIC], F32, st)
        wt = [p.sbuf("wo", [128, NIC * 128], BF16, st) for _ in range(2)]
        p.dma("sp", gcol[:, :], gn_d, writes=[("par",)])
        n = 0
        for tb in range(c.NTB):
            ts = slice(tb * 512, (tb + 1) * 512)
            ps, pk = c.bank(tb)
            for k0 in range(0, NIC, KB):
                i = n % 3
                j = n % 2
                n += 1
                l = ld[i]
                gg = g32[j]
                src = lambda d: d[k0:k0 + KB, :, ts].rearrange("k p t -> p k t")
                p.dma("sp", l[:, 0, :, :], src(ym_d), writes=[("ld", i, 0)])
                p.dma("sp", l[:, 1, :, :], src(yb_d), writes=[("ld", i, 1)])
                p.dma("sp", l[:, 2, :, :], src(sz_d), writes=[("ld", i, 2)])
                p.op("pool", "tensor_tensor", reads=[("ld", i, 0), ("ld", i, 1)], writes=[("g32", j)],
                     out=gg[:, :, :], in0=l[:, 0, :, :], in1=l[:, 1, :, :], op=ALU.add)
                p.op("pool", "tensor_tensor", reads=[("g32", j), ("ld", i, 2)], writes=[("g32", j)],
                     out=gg[:, :, :], in0=gg[:, :, :], in1=l[:, 2, :, :], op=ALU.mult)
                s_ = sq[j]
                p.op("act", "activation", reads=[("g32", j)], writes=[("sq", j)], out=s_[:, :, :],
                     in_=gg[:, :, :], func=AF.Square)
                for kk in range(KB):
                    k = k0 + kk
                    p.op("pe", "matmul", reads=[("sq", j), ("ones",)], writes=[pk], out=ps, lhsT=c.ones_bf[:, :],
                         rhs=s_[:, kk, :], start=(k == 0), stop=(k == NIC - 1))
                    p.op("dve", "tensor_scalar", reads=[("g32", j), ("par",)], writes=[("gs", k, tb)],
                         out=gs[:, k, ts], in0=gg[:, kk, :], scalar1=gcol[:, k:k + 1], scalar2=None, op0=ALU.mult)
            r = rs[tb]
            p.op("act", "activation", reads=[pk], writes=[("rs", tb)], out=r[:, :], in_=ps, func=AF.Sqrt,
                 scale=1.0 / (NIC * 128), bias=EPS)
            p.op("dve", "reciprocal", reads=[("rs", tb)], writes=[("rs", tb)], out=r[:, :], in_=r[:, :])
        nt = 0
        for dc in range(KC):
            s = dc % 2
            for a in range(0, NIC * 128, 2048):
                c.load_w(wt[s][:, a:a + 2048], ("wo", s, a), wout_d[dc, :, a:a + 2048], 2048)
            wkeys = [("wo", s, a) for a in range(0, NIC * 128, 2048)]
            for k in range(NIC):
                for tb in range(c.NTB):
                    ps, pk = c.bank(2 * (dc % 4) + tb)
                    p.op("pe", "matmul", reads=wkeys + [("gs", k, tb)], writes=[pk], out=ps,
                         lhsT=wt[s][:, k * 128:(k + 1) * 128], rhs=gs[:, k, tb * 512:(tb + 1) * 512],
                         start=(k == 0), stop=(k == NIC - 1))
            for tb in range(c.NTB):
                ps, pk = c.bank(2 * (dc % 4) + tb)
                ts = slice(tb * 512, (tb + 1) * 512)
                t_ = tmpo[nt % 2]
                tk = ("tmpo", nt % 2)
                nt += 1
                p.op("dve", "tensor_tensor", reads=[pk, ("rs", tb)], writes=[tk], out=t_[:, :], in0=ps,
                     in1=rs[tb][:, :], op=ALU.mult)
                p.op("pool", "tensor_tensor", reads=[tk, ("x", dc, tb)], writes=[("x", dc, tb)], out=c.xT[:, dc, ts],
                     in0=c.xT[:, dc, ts], in1=t_[:, :], op=ALU.add)
        p.barrier()


POOL_W = (2, 4, 8, 16)


def emit_pool_out(c, up_d, icnt_d, wg_d, sc_d, wout_d):
    p = c.p
    T = c.T
    with contextlib.ExitStack() as st:
        mixT = p.sbuf("mixT", [128, KC, T], BF16, st)
        vT = p.sbuf("vT", [128, KC, T], BF16, st)
        icnt = p.sbuf("icnt", [128, 4 * T], F32, st)
        sc = p.sbuf("sc", [128, KC], F32, st)
        A = [[p.sbuf("A", [128, T + 16], F32, st) for _ in range(3)] for _ in range(2)]
        wgt = [p.sbuf("wgt", [128, 512], BF16, st) for _ in range(2)]
        wt = [p.sbuf("wpo", [128, 2048], BF16, st) for _ in range(3)]
        p.dma("sp", icnt[:, :], icnt_d, writes=[("icnt",)])
        p.dma("sp", sc[:, :], sc_d, writes=[("par",)])
        for cc in range(KC):
            gi = cc // 4
            b = cc % 2
            eng = "pool" if b == 0 else "dve"
            a0, a1, a2 = A[b]
            p.dma("sp", a0[:, :], up_d[cc], writes=[("A", b, 0)])
            p.op(eng, "tensor_tensor", reads=[("A", b, 0)], writes=[("A", b, 1)], out=a1[:, 1:T + 16],
                 in0=a0[:, 0:T + 15], in1=a0[:, 1:T + 16], op=ALU.add)
            cur, curk, oth, othk = a1, ("A", b, 1), a2, ("A", b, 2)
            lo, hi = 1, T + 16
            sh = 1
            for lvl in range(gi):
                nlo, nhi = lo + sh, hi - sh
                p.op(eng, "tensor_tensor", reads=[curk], writes=[othk], out=oth[:, nlo:nhi],
                     in0=cur[:, nlo - sh:nhi - sh], in1=cur[:, nlo + sh:nhi + sh], op=ALU.add)
                cur, curk, oth, othk = oth, othk, cur, curk
                lo, hi = nlo, nhi
                sh *= 2
            assert lo <= 8 and hi >= T + 8
            p.op(eng, "tensor_tensor", reads=[curk, ("icnt",)], writes=[othk], out=oth[:, 8:8 + T],
                 in0=cur[:, 8:8 + T], in1=icnt[:, gi * T:(gi + 1) * T], op=ALU.mult)
            for tb in range(c.NTB):
                p.op(eng, "tensor_tensor", reads=[othk, ("A", b, 0)], writes=[("mix", cc, tb)],
                     out=mixT[:, cc, tb * 512:(tb + 1) * 512], in0=oth[:, 8 + tb * 512:8 + (tb + 1) * 512],
                     in1=a0[:, 8 + tb * 512:8 + (tb + 1) * 512], op=ALU.subtract)
        for dc in range(KC):
            gi = dc // 4
            s = dc % 2
            c.load_w(wgt[s][:, :], ("wgt", s), wg_d[dc], 512)
            for kl in range(4):
                for tb in range(c.NTB):
                    ps, pk = c.bank(2 * (dc % 4) + tb)
                    p.op("pe", "matmul", reads=[("wgt", s), ("mix", gi * 4 + kl, tb)], writes=[pk],
                         out=ps, lhsT=wgt[s][:, kl * 128:(kl + 1) * 128],
                         rhs=mixT[:, gi * 4 + kl, tb * 512:(tb + 1) * 512], start=(kl == 0), stop=(kl == 3))
            for tb in range(c.NTB):
                ps, pk = c.bank(2 * (dc % 4) + tb)
                p.op("act", "activation", reads=[pk, ("par",)], writes=[("v", dc, tb)],
                     out=vT[:, dc, tb * 512:(tb + 1) * 512], in_=ps, func=AF.Identity, scale=sc[:, dc:dc + 1])
        c.nwt = 0
        emit_proj_resid(c, st, vT, "v", KC, wout_d, list(range(c.NTB)), 1.0, wt, "wpo")
        p.barrier()


def emit_final(c, gcol_d, out_d):
    p = c.p
    with contextlib.ExitStack() as st:
        oT = p.sbuf("oT", [128, KC, c.T], F32, st)
        gcol = p.sbuf("gcol", [128, KC], F32, st)
        p.dma("sp", gcol[:, :], gcol_d, writes=[("par",)])
        emit_rmsnorm(c, st, oT, gcol, hkey="o")
        for kc in range(KC):
            p.dma("sp", out_d[:, kc, :], oT[:, kc, :], reads=[("o", kc, tb) for tb in range(c.NTB)],
                  writes=[("final", kc)])
        p.barrier()


TT = 1024
_PROG_CACHE = {}


def build_tok_launch(stages):
    key = tuple(stages)
    if key in _PROG_CACHE:
        return _PROG_CACHE[key]
    p = Prog()
    c = Ctx(p, TT)
    ein = lambda n, sh, dt=F32: p.dram(n, sh, dt, kind="ExternalInput")
    eout = lambda n, sh, dt=F32: p.dram(n, sh, dt, kind="ExternalOutput")
    emit_load_x(c, ein("xin", [128, KC, TT]))
    has_final = False
    for i, stg in enumerate(stages):
        t = "_%d" % i
        if stg == "ffn":
            emit_ffn(c, ein("wg" + t, [FC, 128, 2048]), ein("wu" + t, [FC, 128, 2048]),
                     ein("wd" + t, [NQ, KC, 128, FQ * 128]), ein("gcol" + t, [128, KC]))
        elif stg == "ssd_in":
            emit_mix_in(c, ein("gcol" + t, [128, KC]), ein("w" + t, [81, 128, 2048]), eout("proj", [49, 128, TT]), 81, 32,
                        eout("sz", [NIC, 128, TT], BF16))
        elif stg == "pool_in":
            emit_mix_in(c, ein("gcol" + t, [128, KC]), ein("w" + t, [KC, 128, 2048]), eout("proj", [KC, 128, TT]), KC, 0)
        elif stg == "ssd_out":
            emit_ssd_out(c, ein("ym" + t, [NIC, 128, TT], BF16), ein("yb" + t, [NIC, 128, TT], BF16),
                         ein("sz" + t, [NIC, 128, TT], BF16),
                         ein("gn" + t, [128, NIC]), ein("wout" + t, [KC, 128, NIC * 128]))
        elif stg == "pool_out":
            emit_pool_out(c, ein("up" + t, [KC, 128, TT + 16]), ein("icnt" + t, [128, 4 * TT]),
                          ein("wgrp" + t, [KC, 128, 512]), ein("sc" + t, [128, KC]), ein("wout" + t, [KC, 128, 2048]))
        elif stg == "final":
            emit_final(c, ein("gcol" + t, [128, KC]), eout("final", [128, KC, TT]))
            has_final = True
    if not has_final:
        emit_store_x(c, eout("xout", [128, KC, TT]))
    nc = p.finish()
    _PROG_CACHE[key] = nc
    return nc


def build_ssd_core_launch():
    if "ssd_core" in _PROG_CACHE:
        return _PROG_CACHE["ssd_core"]
    p = Prog()
    ein = lambda n, sh, dt=F32: p.dram(n, sh, dt, kind="ExternalInput")
    eout = lambda n, sh, dt=F32: p.dram(n, sh, dt, kind="ExternalOutput")
    emit_ssd_core(p, ein("xbc", [NCC, 128, TS + 4]), ein("cw", [128, NCC * 5]), ein("cb", [128, NCC]),
                  ein("dtraw", [128, TS]), ein("dtpar", [128, 8]), ein("dbc", [128, NH]),
                  ein("cstf", [128, 128 + TS]), ein("cstb", [128, 384 + 64 * 128], BF16),
                  eout("ymain", [TS, NH * HD], BF16), eout("yboff", [TS, NH * HD], BF16))
    nc = p.finish()
    _PROG_CACHE["ssd_core"] = nc
    return nc


def _ssd_consts():
    import ml_dtypes
    identf = np.eye(128, dtype=np.float32)
    reset = np.ones((128, TS), np.float32)
    reset[:, ::128] = 0.0
    cstf = np.concatenate([identf, reset], axis=1)
    j = np.arange(128)[:, None]
    i = np.arange(128)[None, :]
    maskF = np.where(j > i, -MASKV, 0.0).astype(np.float32)
    maskB = np.where(j < i, MASKV, 0.0).astype(np.float32)
    sel = np.zeros((128, 64, 128), np.float32)
    for r in range(64):
        sel[r, r, :] = 1.0
    cstb = np.concatenate([identf, maskF, maskB, sel.reshape(128, 64 * 128)], axis=1).astype(ml_dtypes.bfloat16)
    return cstf, cstb


def _run(nc, in_maps):
    res = run_bass_kernel_spmd(nc, in_maps, core_ids=list(range(8)))
    return res.results


def _ffn_inputs(t, wg, wu, wd, g):
    return {"wg" + t: lay_w_stat(wg), "wu" + t: lay_w_stat(wu), "wd" + t: lay_wd(wd), "gcol" + t: lay_col(g)}


def kernel(x, ffn_norm, ffn_w_gate, ffn_w_up, ffn_w_down, mix_norm, ssd_w_in, ssd_conv_w, ssd_conv_b, ssd_dt_bias,
           ssd_a_log, ssd_d, ssd_norm, ssd_w_out, pool_w_in, pool_w_group, pool_scale, pool_w_out, final_norm):
    f = lambda a: np.ascontiguousarray(np.asarray(a, dtype=np.float32))
    x = f(x)
    B, L, _ = x.shape
    xs = [lay_xT(x[c // 2, (c % 2) * TT:(c % 2 + 1) * TT]) for c in range(8)]
    cstf, cstb = _ssd_consts()

    def ffn_in(t, l, i):
        return _ffn_inputs(t, f(ffn_w_gate[l, i]), f(ffn_w_up[l, i]), f(ffn_w_down[l, i]), f(ffn_norm[l, i]))

    def tok_launch(stages, shared, percore):
        nc = build_tok_launch(stages)
        in_maps = []
        for c in range(8):
            m = {"xin": xs[c]}
            m.update(shared)
            m.update(percore[c])
            in_maps.append(m)
        return _run(nc, in_maps)

    def ssd_core_stage(j, proj):
        nc = build_ssd_core_launch()
        cwf, cbf = f(ssd_conv_w[j]), f(ssd_conv_b[j])
        in_maps = []
        for c in range(8):
            b, hh = c // 2, c % 2
            P = np.concatenate([proj[2 * b].reshape(49 * 128, TT), proj[2 * b + 1].reshape(49 * 128, TT)], axis=1)
            rows = np.concatenate([np.arange(hh * 2048, (hh + 1) * 2048),
                                   np.arange(4096 + hh * 512, 4096 + (hh + 1) * 512),
                                   np.arange(5120 + hh * 512, 5120 + (hh + 1) * 512)])
            xbc = np.pad(P[rows], ((0, 0), (2, 2))).reshape(NCC, 128, TS + 4)
            cch = rows
            cw = np.ascontiguousarray(cwf[:, cch].T.reshape(NCC, 128, 5).transpose(1, 0, 2)).reshape(128, NCC * 5)
            cb = np.ascontiguousarray(cbf[cch].reshape(NCC, 128).T)
            drow = np.concatenate([6144 + d * 64 + hh * 32 + np.arange(32) for d in range(2)])
            dtr = P[drow]
            par = np.zeros((64, 8), np.float32)
            par[:, 0] = f(ssd_dt_bias[j])[:, hh * 32:(hh + 1) * 32].reshape(64)
            par[:, 1] = f(ssd_a_log[j])[:, hh * 32:(hh + 1) * 32].reshape(64)
            par[:32, 2] = -1.0
            par[32:, 2] = 1.0
            par[32:, 3] = 1.0
            par[:32, 4] = 1.0
            dbc = np.ascontiguousarray(np.broadcast_to(f(ssd_d[j])[None, hh * 32:(hh + 1) * 32], (128, NH)))
            in_maps.append({"xbc": np.ascontiguousarray(xbc), "cw": cw, "cb": cb,
                            "dtraw": np.ascontiguousarray(np.concatenate([dtr, dtr], axis=0)),
                            "dtpar": np.concatenate([par, par], axis=0), "dbc": dbc, "cstf": cstf, "cstb": cstb})
        res = _run(nc, in_maps)
        ym, yb = [], []
        for c in range(8):
            b, half = c // 2, c % 2
            for name, lst in (("ymain", ym), ("yboff", yb)):
                Y = np.concatenate([res[2 * b][name], res[2 * b + 1][name]], axis=1)
                lst.append(np.ascontiguousarray(Y[half * TT:(half + 1) * TT].T.reshape(NIC, 128, TT)))
        return ym, yb

    def pool_inputs(t, j, proj):
        per = []
        for c in range(8):
            b, half = c // 2, c % 2
            U = np.concatenate([proj[2 * b].reshape(D, TT), proj[2 * b + 1].reshape(D, TT)], axis=1)
            Up = np.pad(U, ((0, 0), (8, 8)))
            up = np.ascontiguousarray(Up[:, half * TT:half * TT + TT + 16]).reshape(KC, 128, TT + 16)
            tg = half * TT + np.arange(TT)
            ic = np.stack([1.0 / (np.clip(tg + w // 2, 0, L) - np.clip(tg - w // 2, 0, L)) for w in POOL_W])
            icnt = np.ascontiguousarray(np.broadcast_to(ic.reshape(1, 4 * TT), (128, 4 * TT))).astype(np.float32)
            per.append({"up" + t: up, "icnt" + t: icnt})
        shared = {"wgrp" + t: np.concatenate([lay_w_stat(f(pool_w_group[j, gi])) for gi in range(4)], axis=0),
                  "sc" + t: lay_col(f(pool_scale[j])), "wout" + t: lay_w_stat(f(pool_w_out[j]))}
        return shared, per

    empty = [dict() for _ in range(8)]
    sh = ffn_in("_0", 0, 0)
    sh.update({"gcol_1": lay_col(f(mix_norm[0])), "w_1": lay_w_stat(f(ssd_w_in[0]))})
    r = tok_launch(("ffn", "ssd_in"), sh, empty)
    xs = [r[c]["xout"] for c in range(8)]
    proj = [r[c]["proj"] for c in range(8)]
    szs = [r[c]["sz"] for c in range(8)]
    for j in range(2):
        l = 2 * j
        ym, yb = ssd_core_stage(j, proj)
        sh = {"gn_0": lay_col(f(ssd_norm[j])), "wout_0": lay_w_stat(f(ssd_w_out[j]))}
        sh.update(ffn_in("_1", l, 1))
        sh.update(ffn_in("_2", l + 1, 0))
        sh.update({"gcol_3": lay_col(f(mix_norm[l + 1])), "w_3": lay_w_stat(f(pool_w_in[j]))})
        per = [{"ym_0": ym[c], "yb_0": yb[c], "sz_0": szs[c]} for c in range(8)]
        r = tok_launch(("ssd_out", "ffn", "ffn", "pool_in"), sh, per)
        xs = [r[c]["xout"] for c in range(8)]
        proj = [r[c]["proj"] for c in range(8)]
        sh, per = pool_inputs("_0", j, proj)
        sh.update(ffn_in("_1", l + 1, 1))
        if j == 0:
            sh.update(ffn_in("_2", l + 2, 0))
            sh.update({"gcol_3": lay_col(f(mix_norm[l + 2])), "w_3": lay_w_stat(f(ssd_w_in[1]))})
            r = tok_launch(("pool_out", "ffn", "ffn", "ssd_in"), sh, per)
            xs = [r[c]["xout"] for c in range(8)]
            proj = [r[c]["proj"] for c in range(8)]
            szs = [r[c]["sz"] for c in range(8)]
        else:
            sh.update({"gcol_2": lay_col(f(final_norm))})
            r = tok_launch(("pool_out", "ffn", "final"), sh, per)
    out = np.empty((B, L, D), np.float32)
    for c in range(8):
        out[c // 2, (c % 2) * TT:(c % 2 + 1) * TT] = unlay_xT(r[c]["final"])
    return out
```
